# Optimizing a Trainium2 kernel written in Bass

```python
import math
import jax, jax.numpy as jnp
from jax import lax
import numpy as np

D_MODEL = 2048
BATCH = 4
SEQ = 2048
DEPTH = 2
DEC_BATCH = 128
DEC_SEQ = 4
PAST_LEN = 16384
PAGE_SIZE = 128

N_MIX_LAYERS = (DEPTH + 1) // 2
N_SSM_LAYERS = DEPTH // 2
GLA_HEADS = 4
GLA_DK = D_MODEL // 16
GLA_DV = D_MODEL // 8
GLA_RANK = 16
GLA_LOGIT_NORM = 16.0
RET_HEADS = 4
RET_DK = D_MODEL // 16
RET_DV = D_MODEL // 8
GLA_KEY = GLA_HEADS * GLA_DK
GLA_VAL = GLA_HEADS * GLA_DV
RET_KEY = RET_HEADS * RET_DK
RET_VAL = RET_HEADS * RET_DV
D_MIX = GLA_VAL + RET_VAL
IN_COLS = 2 * GLA_KEY + 2 * GLA_VAL + GLA_RANK + 2 * RET_KEY + 2 * RET_VAL
CHUNK = 16
ROPE_BASE = 10000.0
S5_GROUP = 16
S5_GROUPS = D_MODEL // S5_GROUP
S5_STATE = 64
D_FF = 4 * D_MODEL
EPS = 1e-6

kernel_name = "hybrid_gla_retnet_s5_adaln_step"


def rmsnorm(x, g):
    xf = x.astype(jnp.float32)
    y = xf * lax.rsqrt(jnp.mean(xf * xf, axis=-1, keepdims=True) + EPS) * g.astype(jnp.float32)
    return y.astype(x.dtype)


def split_heads(a, n):
    b, t, _ = a.shape
    return a.reshape(b, t, n, -1).transpose(0, 2, 1, 3)


def merge_heads(a):
    b, h, t, d = a.shape
    return a.transpose(0, 2, 1, 3).reshape(b, t, h * d)


def rotary(x, pos):
    half = x.shape[-1] // 2
    inv = ROPE_BASE ** (-jnp.arange(half, dtype=jnp.float32) / half)
    ang = pos.astype(jnp.float32)[:, None] * inv[None, :]
    cos, sin = jnp.cos(ang), jnp.sin(ang)
    x1, x2 = x[..., :half], x[..., half:]
    return jnp.concatenate([x1 * cos - x2 * sin, x1 * sin + x2 * cos], axis=-1)


def chunk_decay_linear_attention(q, k, v, g, s0):
    b_, h_, t_, dk = q.shape
    dv = v.shape[-1]
    c = math.gcd(t_, CHUNK)
    n = t_ // c

    def blocks(a):
        return jnp.moveaxis(a.astype(jnp.float32).reshape(b_, h_, n, c, a.shape[-1]), 2, 0)

    causal = jnp.tril(jnp.ones((c, c), dtype=bool))

    def step(s, blk):
        qb, kb, vb, gb = blk
        bcum = jnp.cumsum(gb, axis=2)
        b_last = bcum[:, :, -1:, :]
        q_in = qb * jnp.exp(bcum)
        k_in = kb * jnp.exp(-bcum)
        k_out = kb * jnp.exp(b_last - bcum)
        att = jnp.where(causal, jnp.einsum('bhid,bhjd->bhij', q_in, k_in), 0.0)
        o = jnp.einsum('bhij,bhjv->bhiv', att, vb) + jnp.einsum('bhid,bhdv->bhiv', q_in, s)
        s = jnp.exp(b_last[:, :, 0, :, None]) * s + jnp.einsum('bhjd,bhjv->bhdv', k_out, vb)
        return s, o

    s_fin, o = lax.scan(step, s0.astype(jnp.float32), (blocks(q), blocks(k), blocks(v), blocks(g)))
    o = jnp.moveaxis(o, 0, 2).reshape(b_, h_, t_, dv)
    return o, s_fin


def head_rmsnorm(o, g):
    return o * lax.rsqrt(jnp.mean(o * o, axis=-1, keepdims=True) + EPS) * g[None, :, None, :]


def head_groupnorm(o, g):
    mu = jnp.mean(o, axis=-1, keepdims=True)
    oc = o - mu
    return oc * lax.rsqrt(jnp.mean(oc * oc, axis=-1, keepdims=True) + EPS) * g[None, :, None, :]


def gla_retnet_mixer(h, pos, s_gla, s_ret, w_in, w_gk, b_gk, gla_norm, ret_norm, w_out):
    proj = h @ w_in
    sizes = (GLA_KEY, GLA_KEY, GLA_VAL, GLA_VAL, GLA_RANK, RET_KEY, RET_KEY, RET_VAL, RET_VAL)
    idx = [int(i) for i in np.cumsum(sizes)[:-1]]
    gq, gk, gv, gg, glr, rq, rk, rv, rg = jnp.split(proj, idx, axis=-1)
    logit = (glr @ w_gk + b_gk).astype(jnp.float32)
    glog = jax.nn.log_sigmoid(logit) / GLA_LOGIT_NORM
    o_gla, s_gla_new = chunk_decay_linear_attention(
        split_heads(gq, GLA_HEADS) * GLA_DK ** -0.5, split_heads(gk, GLA_HEADS),
        split_heads(gv, GLA_HEADS), split_heads(glog, GLA_HEADS), s_gla)
    o_gla = head_rmsnorm(o_gla, gla_norm) * jax.nn.silu(split_heads(gg, GLA_HEADS).astype(jnp.float32))
    gamma_log = jnp.log1p(-jnp.power(2.0, -5.0 - jnp.arange(RET_HEADS, dtype=jnp.float32)))
    qr = rotary(split_heads(rq, RET_HEADS).astype(jnp.float32), pos)
    kr = rotary(split_heads(rk, RET_HEADS).astype(jnp.float32), pos) * RET_DK ** -0.5
    g_ret = jnp.broadcast_to(gamma_log[None, :, None, None], kr.shape)
    o_ret, s_ret_new = chunk_decay_linear_attention(qr, kr, split_heads(rv, RET_HEADS), g_ret, s_ret)
    o_ret = head_groupnorm(o_ret, ret_norm) * jax.nn.silu(split_heads(rg, RET_HEADS).astype(jnp.float32))
    merged = jnp.concatenate([merge_heads(o_gla), merge_heads(o_ret)], axis=-1).astype(h.dtype)
    return merged @ w_out, s_gla_new, s_ret_new


def s5_mixer(h, s_re, s_im, lam_re, lam_im, log_dt, b_re, b_im, c_re, c_im, d_skip, w_glu_a, w_glu_b):
    bsz, t_, _ = h.shape
    u = h.astype(jnp.float32).reshape(bsz, t_, S5_GROUPS, S5_GROUP)
    dt = jnp.exp(log_dt.astype(jnp.float32))[:, None]
    lr, li = lam_re.astype(jnp.float32), lam_im.astype(jnp.float32)
    mag = jnp.exp(lr * dt)
    lb_re, lb_im = mag * jnp.cos(li * dt), mag * jnp.sin(li * dt)
    nr, ni = lb_re - 1.0, lb_im
    den = lr * lr + li * li
    f_re = (nr * lr + ni * li) / den
    f_im = (ni * lr - nr * li) / den
    br, bi = b_re.astype(jnp.float32), b_im.astype(jnp.float32)
    bb_re = f_re[..., None] * br - f_im[..., None] * bi
    bb_im = f_re[..., None] * bi + f_im[..., None] * br
    bu_re = jnp.einsum('btgc,gpc->btgp', u, bb_re)
    bu_im = jnp.einsum('btgc,gpc->btgp', u, bb_im)
    bu_re = bu_re.at[:, 0].add(lb_re * s_re - lb_im * s_im)
    bu_im = bu_im.at[:, 0].add(lb_re * s_im + lb_im * s_re)
    a_re = jnp.broadcast_to(lb_re, bu_re.shape)
    a_im = jnp.broadcast_to(lb_im, bu_im.shape)

    def combine(e1, e2):
        a1r, a1i, b1r, b1i = e1
        a2r, a2i, b2r, b2i = e2
        return (a2r * a1r - a2i * a1i, a2r * a1i + a2i * a1r,
                a2r * b1r - a2i * b1i + b2r, a2r * b1i + a2i * b1r + b2i)

    _, _, xr, xi = lax.associative_scan(combine, (a_re, a_im, bu_re, bu_im), axis=1)
    y = (jnp.einsum('btgp,gcp->btgc', xr, c_re.astype(jnp.float32))
         - jnp.einsum('btgp,gcp->btgc', xi, c_im.astype(jnp.float32)))
    y = y.reshape(bsz, t_, D_MODEL) + d_skip.astype(jnp.float32) * h.astype(jnp.float32)
    z = jax.nn.gelu(y).astype(h.dtype)
    out = (z @ w_glu_a) * jax.nn.sigmoid(z @ w_glu_b)
    return out, xr[:, -1], xi[:, -1]


def modulate(h, shift, scale):
    return h * (1.0 + scale[:, None, :]) + shift[:, None, :]


def trunk(x, c, pos, s_gla, s_ret, s5_re, s5_im,
          w_ada, b_ada, norm_pre, norm_post, w_in_mix, w_gla_gk, b_gla_gk, gla_head_norm,
          ret_head_norm, w_out_mix, s5_lam_re, s5_lam_im, s5_log_dt, s5_b_re, s5_b_im,
          s5_c_re, s5_c_im, s5_d, w_glu_a, w_glu_b, w_mlp_up, w_mlp_down):
    new_gla, new_ret, new_re, new_im = [], [], [], []
    sc = jax.nn.silu(c)
    for l in range(DEPTH):
        mod = jnp.einsum('bd,sde->sbe', sc, w_ada[l]) + b_ada[l][:, None, :]
        sh0, scl0, gt0 = jnp.split(mod[0], 3, axis=-1)
        sh1, scl1, gt1 = jnp.split(mod[1], 3, axis=-1)
        h = modulate(rmsnorm(x, norm_pre[l, 0]), sh0, scl0)
        i = l // 2
        if l % 2 == 0:
            y, ng, nr = gla_retnet_mixer(h, pos, s_gla[i], s_ret[i], w_in_mix[i], w_gla_gk[i], b_gla_gk[i],
                                         gla_head_norm[i], ret_head_norm[i], w_out_mix[i])
            new_gla.append(ng)
            new_ret.append(nr)
        else:
            y, nre, nim = s5_mixer(h, s5_re[i], s5_im[i], s5_lam_re[i], s5_lam_im[i], s5_log_dt[i],
                                   s5_b_re[i], s5_b_im[i], s5_c_re[i], s5_c_im[i], s5_d[i],
                                   w_glu_a[i], w_glu_b[i])
            new_re.append(nre)
            new_im.append(nim)
        x = x + gt0[:, None, :] * rmsnorm(y, norm_post[l, 0])
        h = modulate(rmsnorm(x, norm_pre[l, 1]), sh1, scl1)
        m = jnp.square(jax.nn.relu(h @ w_mlp_up[l])) @ w_mlp_down[l]
        x = x + gt1[:, None, :] * rmsnorm(m, norm_post[l, 1])
    return x, jnp.stack(new_gla), jnp.stack(new_ret), jnp.stack(new_re), jnp.stack(new_im)


def setup_inputs(seed: int = 0) -> dict:
    key = jax.random.key(seed)
    ks = jax.random.split(key, 32)
    f32 = jnp.float32

    def nrm(k, shape, scale):
        return jax.random.normal(k, shape, f32) * scale

    p_arange = jnp.arange(S5_STATE, dtype=f32)
    return {
        "x_prompt": nrm(ks[0], (BATCH, SEQ, D_MODEL), 1.0),
        "x_sample": nrm(ks[1], (DEC_BATCH, DEC_SEQ, D_MODEL), 1.0),
        "state_gla": nrm(ks[2], (N_MIX_LAYERS, DEC_BATCH, GLA_HEADS, GLA_DK, GLA_DV), 1.0),
        "state_ret": nrm(ks[3], (N_MIX_LAYERS, DEC_BATCH, RET_HEADS, RET_DK, RET_DV), 1.0),
        "state_s5_re": nrm(ks[4], (N_SSM_LAYERS, DEC_BATCH, S5_GROUPS, S5_STATE), 0.1),
        "state_s5_im": nrm(ks[5], (N_SSM_LAYERS, DEC_BATCH, S5_GROUPS, S5_STATE), 0.1),
        "c_prompt": nrm(ks[6], (BATCH, D_MODEL), 1.0),
        "c_sample": nrm(ks[7], (DEC_BATCH, D_MODEL), 1.0),
        "w_ada": nrm(ks[8], (DEPTH, 2, D_MODEL, 3 * D_MODEL), 0.2 * D_MODEL ** -0.5),
        "b_ada": nrm(ks[9], (DEPTH, 2, 3 * D_MODEL), 0.02),
        "norm_pre": 1.0 + nrm(ks[10], (DEPTH, 2, D_MODEL), 0.02),
        "norm_post": 1.0 + nrm(ks[11], (DEPTH, 2, D_MODEL), 0.02),
        "w_in_mix": nrm(ks[12], (N_MIX_LAYERS, D_MODEL, IN_COLS), D_MODEL ** -0.5),
        "w_gla_gk": nrm(ks[13], (N_MIX_LAYERS, GLA_RANK, GLA_KEY), GLA_RANK ** -0.5),
        "b_gla_gk": nrm(ks[14], (N_MIX_LAYERS, GLA_KEY), 0.1),
        "gla_head_norm": 1.0 + nrm(ks[15], (N_MIX_LAYERS, GLA_HEADS, GLA_DV), 0.02),
        "ret_head_norm": 1.0 + nrm(ks[16], (N_MIX_LAYERS, RET_HEADS, RET_DV), 0.02),
        "w_out_mix": nrm(ks[17], (N_MIX_LAYERS, D_MIX, D_MODEL), D_MIX ** -0.5),
        "s5_lam_re": -0.5 + nrm(ks[18], (N_SSM_LAYERS, S5_GROUPS, S5_STATE), 0.01),
        "s5_lam_im": jnp.pi * p_arange + nrm(ks[19], (N_SSM_LAYERS, S5_GROUPS, S5_STATE), 0.01),
        "s5_log_dt": jax.random.uniform(ks[20], (N_SSM_LAYERS, S5_GROUPS), f32,
                                        math.log(0.001), math.log(0.1)),
        "s5_b_re": nrm(ks[21], (N_SSM_LAYERS, S5_GROUPS, S5_STATE, S5_GROUP), (2 * S5_GROUP) ** -0.5),
        "s5_b_im": nrm(ks[22], (N_SSM_LAYERS, S5_GROUPS, S5_STATE, S5_GROUP), (2 * S5_GROUP) ** -0.5),
        "s5_c_re": nrm(ks[23], (N_SSM_LAYERS, S5_GROUPS, S5_GROUP, S5_STATE), S5_STATE ** -0.5),
        "s5_c_im": nrm(ks[24], (N_SSM_LAYERS, S5_GROUPS, S5_GROUP, S5_STATE), S5_STATE ** -0.5),
        "s5_d": nrm(ks[25], (N_SSM_LAYERS, D_MODEL), 1.0),
        "w_glu_a": nrm(ks[26], (N_SSM_LAYERS, D_MODEL, D_MODEL), D_MODEL ** -0.5),
        "w_glu_b": nrm(ks[27], (N_SSM_LAYERS, D_MODEL, D_MODEL), D_MODEL ** -0.5),
        "w_mlp_up": nrm(ks[28], (DEPTH, D_MODEL, D_FF), D_MODEL ** -0.5),
        "w_mlp_down": nrm(ks[29], (DEPTH, D_FF, D_MODEL), D_FF ** -0.5),
    }


def reference(x_prompt, x_sample, state_gla, state_ret, state_s5_re, state_s5_im, c_prompt, c_sample,
              w_ada, b_ada, norm_pre, norm_post, w_in_mix, w_gla_gk, b_gla_gk, gla_head_norm,
              ret_head_norm, w_out_mix, s5_lam_re, s5_lam_im, s5_log_dt, s5_b_re, s5_b_im,
              s5_c_re, s5_c_im, s5_d, w_glu_a, w_glu_b, w_mlp_up, w_mlp_down):
    weights = (w_ada, b_ada, norm_pre, norm_post, w_in_mix, w_gla_gk, b_gla_gk, gla_head_norm,
               ret_head_norm, w_out_mix, s5_lam_re, s5_lam_im, s5_log_dt, s5_b_re, s5_b_im,
               s5_c_re, s5_c_im, s5_d, w_glu_a, w_glu_b, w_mlp_up, w_mlp_down)
    bp, tp = x_prompt.shape[0], x_prompt.shape[1]
    z_gla = jnp.zeros((N_MIX_LAYERS, bp, GLA_HEADS, GLA_DK, GLA_DV), jnp.float32)
    z_ret = jnp.zeros((N_MIX_LAYERS, bp, RET_HEADS, RET_DK, RET_DV), jnp.float32)
    z_s5 = jnp.zeros((N_SSM_LAYERS, bp, S5_GROUPS, S5_STATE), jnp.float32)
    pos_p = jnp.arange(tp, dtype=jnp.float32)
    y_prompt, gla_p, ret_p, s5re_p, s5im_p = trunk(x_prompt, c_prompt, pos_p, z_gla, z_ret, z_s5, z_s5, *weights)
    pos_s = PAST_LEN + jnp.arange(x_sample.shape[1], dtype=jnp.float32)
    y_sample, gla_s, ret_s, s5re_s, s5im_s = trunk(x_sample, c_sample, pos_s, state_gla, state_ret,
                                                   state_s5_re, state_s5_im, *weights)
    return (y_prompt, y_sample, gla_p, gla_s, ret_p, ret_s, s5re_p, s5re_s, s5im_p, s5im_s)
```

```python
import math
import numpy as np
from contextlib import ExitStack
import concourse.bass as bass
import concourse.mybir as mybir
from concourse.ap import AP
from concourse.bass_utils import run_bass_kernel_spmd

F32 = mybir.dt.float32
BF16 = mybir.dt.bfloat16
I32 = mybir.dt.int32
ALU = mybir.AluOpType
AF = mybir.ActivationFunctionType

D = 2048
KC = 16
NPS = 1024
NS = 64
NSEQ = 16
NT = NPS + NS
EPS = 1e-6
NCORES = 8
TWO_PI = 2.0 * math.pi


class T:
    def __init__(self, ap, name=""):
        self.ap = ap
        self.name = name
        self.w = None
        self.r = []


class DKey:
    def __init__(self, sem):
        self.sem = sem
        self.cnt = 0


class Eng:
    def __init__(self, q, sem, name):
        self.q = q
        self.sem = sem
        self.cnt = 0
        self.name = name
        self.waited = {}


class Ctx:
    def __init__(self, nc, stack):
        self.nc = nc
        self.stack = stack
        self.E = {}
        for name, q in (("pe", nc.tensor), ("act", nc.scalar), ("dve", nc.vector),
                        ("pool", nc.gpsimd), ("sp", nc.sync)):
            sem = stack.enter_context(nc.semaphore("s_" + name))
            self.E[name] = Eng(q, sem, name)
        self.keys = {}
        self.retired = []
        self.nkeys = 0
        self.out_keys = set()
        self.ninst = 0

    def sb(self, name, shape, dt):
        return self.stack.enter_context(self.nc.sbuf_tensor(name, list(shape), dt))

    def ps(self, name, shape, dt=F32):
        return self.stack.enter_context(self.nc.psum_tensor(name, list(shape), dt))

    def key(self, name, rotate=True):
        k = self.keys.get(name)
        if k is None or (rotate and k.cnt >= 12000):
            if k is not None:
                self.retired.append(k)
            self.nkeys += 1
            sem = self.stack.enter_context(self.nc.semaphore("d_%s_%d" % (name, self.nkeys)))
            k = DKey(sem)
            self.keys[name] = k
        return k

    def _collect(self, reads, writes):
        deps = []
        for t in reads:
            if t.w is not None:
                deps.append(t.w)
        for t in writes:
            if t.w is not None:
                deps.append(t.w)
            deps.extend(t.r)
        return deps

    def _wait(self, eng, deps, skip_self=False):
        need = {}
        for d in deps:
            if d[0] == "e":
                s, v = d[1], d[2]
            else:
                s, v = d[1].sem, d[2]
            if v <= 0:
                continue
            if skip_self and s is eng.sem:
                continue
            if need.get(s, 0) < v:
                need[s] = v
        for s, v in need.items():
            if eng.waited.get(s, 0) < v:
                eng.q.wait_ge(s, v)
                eng.waited[s] = v
                self.ninst += 1

    def op(self, en, fn, reads=(), writes=(), signal=True):
        eng = self.E[en]
        self._wait(eng, self._collect(reads, writes), skip_self=(en == "pe"))
        ins = fn(eng.q)
        self.ninst += 1
        if signal:
            eng.cnt += 1
            ins.then_inc(eng.sem, 1)
            me = ("e", eng.sem, eng.cnt)
        else:
            me = ("e", eng.sem, eng.cnt + 1)
        for t in reads:
            t.r.append(me)
        for t in writes:
            t.w = me
            t.r = []
        return ins

    def dma(self, qn, out_ap, in_ap, reads=(), writes=(), key="misc", is_output=False, cont=False, **kw):
        eng = self.E[qn]
        k = self.key(key, rotate=not cont)
        deps = self._collect(reads, writes)
        if cont:
            deps = [d for d in deps if not (d[0] == "d" and d[1] is k)]
        elif k.cnt > 0:
            deps.append(("d", k, k.cnt))
        self._wait(eng, deps)
        ins = eng.q.dma_start(out=out_ap, in_=in_ap, **kw)
        self.ninst += 1
        k.cnt += 16
        ins.then_inc(k.sem, 16)
        me = ("d", k, k.cnt)
        for t in reads:
            t.r.append(me)
        for t in writes:
            t.w = me
            t.r = []
        if is_output:
            self.out_keys.add(key)
        return ins

    def barrier(self):
        deps = [("e", e.sem, e.cnt) for e in self.E.values() if e.cnt > 0]
        deps += [("d", k, k.cnt) for k in list(self.keys.values()) + self.retired if k.cnt > 0]
        for e in self.E.values():
            self._wait(e, deps)

    def finish(self):
        self.barrier()


class _Halt(Exception):
    pass


def pat(ap, pattern):
    base = ap.ap
    return AP(ap.tensor, ap.offset, [list(base[0])] + [list(p) for p in pattern])


def _const_tables(nsb):
    f = np.float32
    j = np.arange(128)[:, None]
    i = np.arange(128)[None, :]
    caus = (j <= i)
    same = ((j // 4) == (i // 4))
    C = {}
    C["ident"] = np.eye(128, dtype=f)
    C["onesf"] = np.ones((128, 128), f)
    C["triUg"] = np.where(caus, -1.0 / 16.0, 0.0).astype(f)
    C["triSg"] = np.where(j > i, -1.0 / 16.0, 0.0).astype(f)
    C["cm"] = caus.astype(f)
    gam = 1.0 - np.power(2.0, -5.0 - np.arange(4))
    sc = 128.0 ** -0.5
    dh = []
    for h in range(4):
        dh.append(np.where(caus, np.power(gam[h], np.maximum(i - j, 0)) * sc, 0.0))
    C["Dh"] = np.concatenate(dh, axis=1).astype(f)
    j6, i6 = j[:64], i[:, :64]
    caus6 = caus[:64, :64] & same[:64, :64]
    C["triUgs"] = np.zeros((128, 64), f); C["triUgs"][:64] = np.where(caus6, -1.0 / 16.0, 0.0)
    C["triSgs"] = np.zeros((128, 64), f); C["triSgs"][:64] = np.where((j6 > i6) & same[:64, :64], -1.0 / 16.0, 0.0)
    C["cms"] = np.zeros((128, 64), f); C["cms"][:64] = caus6
    ds = np.zeros((128, 4 * 64), f)
    for h in range(4):
        ds[:64, h * 64:(h + 1) * 64] = np.where(caus6, np.power(gam[h], np.maximum(i6 - j6, 0)) * sc, 0.0)
    C["Dsh"] = ds
    tq = np.zeros((128, 4 * 128), f); tqs = np.zeros((128, 4 * 64), f); tk = np.zeros((128, 8), f)
    for h in range(4):
        tq[:, h * 128:(h + 1) * 128] = np.power(gam[h], np.arange(128) + 1)[None, :]
        tqs[:, h * 64:(h + 1) * 64] = np.power(gam[h], (np.arange(64) % 4) + 1)[None, :]
        tk[:, h] = np.power(gam[h], 127 - np.arange(128)) * sc
        tk[:64, 4 + h] = np.power(gam[h], 3 - (np.arange(64) % 4)) * sc
    C["tq"] = tq; C["tqs"] = tqs; C["tk"] = tk
    C["seqm"] = np.zeros((128, 16), f)
    C["seqm"][:64] = ((np.arange(64)[:, None] // 4) == np.arange(16)[None, :])
    C["iota1"] = np.broadcast_to((np.arange(128) + 1).astype(f)[None, :], (128, 128)).copy()
    scol = np.zeros((128, 2), f)
    scol[:, 0] = -(np.arange(128) + 1)
    scol[:, 1] = -((np.arange(128) % 4) + 1)
    C["scol"] = scol
    p = np.arange(128)
    mB = np.zeros((128, 4, 8), f)
    for jp in range(4):
        mB[p, jp, 2 * jp + p // 64] = 1.0
    C["maskB"] = mB.reshape(128, 32)
    mC = np.zeros((128, 8), f)
    mC[p, p // 16] = 1.0
    C["maskC"] = mC
    order = ["ident", "onesf", "triUg", "triSg", "cm", "Dh", "triUgs", "triSgs", "cms", "Dsh", "tq", "tqs", "tk",
             "seqm", "iota1", "scol", "maskB", "maskC"]
    off = {}
    o = 0
    for k in order:
        off[k] = (o, C[k].shape[1])
        o += C[k].shape[1]
    cst = np.concatenate([C[k] for k in order], axis=1).astype(f)
    cb = np.zeros((128, 128 * 3 + 64 * 2), f)
    cb[:, 0:128] = 1.0
    cb[:, 128:256] = caus
    cb[:, 256:384] = -caus.astype(f)
    cb[:64, 384:448] = caus6
    cb[:64, 448:512] = -caus6.astype(f)
    half = 64
    inv = (10000.0 ** (-np.arange(half, dtype=np.float32) / half)).astype(np.float32)
    rot_fm = np.zeros((nsb, 2, 128, NT), f)
    rot_tm = np.zeros((nsb, 2, 128, 9, 64), f)
    for sb in range(nsb):
        pos = np.zeros(NT, np.float32)
        pos[:NPS] = sb * NPS + np.arange(NPS)
        pos[NPS:] = 16384 + (np.arange(NS) % 4)
        ang = pos[None, :].astype(np.float32) * inv[:, None]
        co, si = np.cos(ang), np.sin(ang)
        rot_fm[sb, 0, :64] = co; rot_fm[sb, 0, 64:] = co
        rot_fm[sb, 1, :64] = -si; rot_fm[sb, 1, 64:] = si
        for t in range(9):
            n = 128 if t < 8 else 64
            a = ang[:, t * 128:t * 128 + n].T
            rot_tm[sb, 0, :n, t] = np.cos(a)
            rot_tm[sb, 1, :n, t] = np.sin(a)
    return cst, off, cb, rot_fm, rot_tm


_CST_CACHE = {}


def _get_consts(nsb):
    if nsb not in _CST_CACHE:
        _CST_CACHE[nsb] = _const_tables(nsb)
    return _CST_CACHE[nsb]


def build(nsb=2, dbg=None):
    dbg = dbg or {}
    cst_np, coff, cb_np, _, _ = _get_consts(nsb)
    NCST = cst_np.shape[1]
    nc = bass.Bass("TRN2", target_bir_lowering=False)

    def din(name, shape):
        return nc.dram_tensor(name, list(shape), F32, kind="ExternalInput").ap()

    def dout(name, shape):
        return nc.dram_tensor(name, list(shape), F32, kind="ExternalOutput").ap()

    def dscr(name, shape):
        return nc.dram_tensor(name, list(shape), F32).ap()

    NPT = nsb * NPS
    _SH = {
        "xp": [NPT, D], "xs": [NS, D], "cv": [17, D],
        "sgla": [NSEQ * 4 * 128, 256], "sret": [NSEQ * 4 * 128, 256],
        "s5re": [64 * 16, 128], "s5im": [64 * 16, 128],
        "w_ada": [4 * D, 3 * D], "b_ada": [4 * 48, 128], "npre": [4 * 16, 128], "npost": [4 * 16, 128],
        "w_in": [D, 6160], "wgk17": [17, 512], "gnorm": [8, 128], "rnorm": [8, 128], "w_out": [D, D],
        "bT_re": [D, 64], "bT_im": [D, 64], "cT_re": [8192, 16], "cT_im": [8192, 16], "s5d": [16, 128],
        "w_glu_a": [D, D], "w_glu_b": [D, D], "w_up": [2 * D, 4 * D], "w_down": [2 * 4 * D, D],
        "cst": [128, NCST], "cstb": [128, 512], "rot_fm": [nsb * 2 * 128, NT], "rot_tm": [nsb * 2 * 128, 9 * 64],
        "xinj": [nsb * 16 * 128, NT],
    }
    for nm in ("lamre", "lamim", "logdt"):
        _SH[nm + "_c"] = [128, 64]
        _SH[nm + "_gc"] = [D, 64]
        _SH[nm + "_r"] = [1, 8192]

    class _LazyIn(dict):
        def __missing__(self, k):
            v = din(k, _SH[k])
            self[k] = v
            return v
    I = _LazyIn()
    if not dbg.get("lazy_inputs"):
        for k_ in _SH:
            if k_ != "xinj" or "inject" in dbg:
                I[k_]
    O = {}
    O["yp"] = dout("yp", [NPT, D]); O["ys"] = dout("ys", [NS, D])
    O["ogla_p"] = dout("ogla_p", [4 * 128, 256]); O["ogla_s"] = dout("ogla_s", [NSEQ * 4 * 128, 256])
    O["oret_p"] = dout("oret_p", [4 * 128, 256]); O["oret_s"] = dout("oret_s", [NSEQ * 4 * 128, 256])
    O["os5re_p"] = dout("os5re_p", [64, 128]); O["os5re_s"] = dout("os5re_s", [64 * 16, 128])
    O["os5im_p"] = dout("os5im_p", [64, 128]); O["os5im_s"] = dout("os5im_s", [64 * 16, 128])
    if dbg.get("dump"):
        O["dump"] = dout("dump", [nsb * 4 * 16 * 128, NT])
    xscr = dscr("xscr", [16 * 128, NT])
    s5tab = dscr("s5tab", [16 * 128, 3072])
    modscr = dscr("modscr", [4 * 128, 48 * 17])

    with ExitStack() as st:
        c = Ctx(nc, st)

        def halt(n):
            if dbg.get("halt_at") == n:
                raise _Halt()
        RX = c.sb("RX", [128, 16 * NT], F32)
        RH = c.sb("RH", [128, 8 * NT], F32)
        RB = c.sb("RB", [128, 8 * NT], F32)
        RW = c.sb("RW", [128, 9216], F32)
        NRC = NCST + 256 + 816 + 128 + 16 + 16 + 192 + 512 + 2048 + 128 + 128 + 64
        RC = c.sb("RC", [128, NRC], F32)
        RT = c.sb("RT", [128, 2048], F32)
        PS = [T(c.ps("ps%d" % b, [128, 512])[:], "ps%d" % b) for b in range(8)]

        RX3 = RX[:].rearrange("p (k n) -> p k n", k=16)
        X = [T(RX3[:, k, :], "X%d" % k) for k in range(16)]
        RHb = RH[:].bitcast(BF16).rearrange("p (k n) -> p k n", k=16)
        H = [T(RHb[:, k, :], "H%d" % k) for k in range(16)]
        RBb = RB[:].bitcast(BF16).rearrange("p (k n) -> p k n", k=16)
        Bk = [T(RBb[:, k, :], "B%d" % k) for k in range(16)]
        WBv = [RW[:, i * 4096:(i + 1) * 4096].bitcast(BF16).rearrange("p (k n) -> p k n", k=16) for i in range(2)]
        WBk = [[T(WBv[i][:, k, :], "WB%d_%d" % (i, k)) for k in range(16)] for i in range(2)]
        WS = [T(RW[:, 8192 + i * 512: 8192 + (i + 1) * 512], "WS%d" % i) for i in range(2)]

        rc_o = [0]

        def rc(n):
            a = RC[:, rc_o[0]:rc_o[0] + n]
            rc_o[0] += n
            return a

        CSTt = T(rc(NCST), "cst")

        def cs(name, rows=128, lo=None, hi=None):
            o, n = coff[name]
            a = CSTt.ap[0:rows, o:o + n]
            if lo is not None:
                a = CSTt.ap[0:rows, o + lo:o + hi]
            return a

        CBt = T(rc(256).bitcast(BF16), "cstb")
        onesb = CBt.ap[:, 0:128]
        MOD = T(rc(816), "mod")
        mod3 = MOD.ap.rearrange("p (t k s) -> p t k s", t=3, k=16)
        NPRE = T(rc(64), "npre"); NPOST = T(rc(64), "npost")
        S5D = T(rc(16), "s5d"); GN = T(rc(8), "gn"); RN = T(rc(8), "rn")
        rc(16)
        BADA = T(rc(192), "bada")
        WGK = T(rc(512), "wgk")
        SST = [T(rc(256), "S%d" % h) for h in range(8)]
        SBF = T(rc(128).bitcast(BF16), "Sbf")
        XPR = T(rc(64), "xpr"); XPI = T(rc(64), "xpi")
        assert rc_o[0] <= NRC, (rc_o[0], NRC)

        cast_i = [0]
        ws_i = [0]
        ps_i = [0]

        def dve(fn, r, w):
            return c.op("dve", fn, r, w)

        def act(fn, r, w):
            return c.op("act", fn, r, w)

        def pool(fn, r, w):
            return c.op("pool", fn, r, w)

        def mm(out_t, out_ap, lt, lap, rt, rap, start, stop, signal=None):
            sig = stop if signal is None else signal
            return c.op("pe", lambda q: q.matmul(out_ap, lhsT=lap, rhs=rap, start=start, stop=stop),
                        [lt, rt], [out_t], signal=sig)

        def load_cols(dst_t, dst_ap, src, nrows, tmp_t):
            c.dma("sp", tmp_t.ap[0:nrows, 0:128], src, writes=[tmp_t], key="setup")
            c.op("pe", lambda q: q.transpose(PS[7].ap[:, 0:nrows], tmp_t.ap[0:nrows, 0:128], cs("ident", nrows, 0, nrows)),
                 [tmp_t, CSTt], [PS[7]])
            dve(lambda q: q.tensor_copy(out=dst_ap, in_=PS[7].ap[:, 0:nrows]), [PS[7]], [dst_t])

        def wload_gen(pieces, nk, buf):
            kname = "wb%d" % buf
            first = True
            step = 4
            for k0 in range(0, nk, step):
                k1 = min(nk, k0 + step)
                off = 0
                tiles = [WBk[buf][kc] for kc in range(k0, k1)]
                for (wap, c0, n) in pieces:
                    src = wap[k0 * 128:k1 * 128, c0:c0 + n].rearrange("(k p) n -> p k n", p=128)
                    c.dma("pool", WBv[buf][:, k0:k1, off:off + n], src, writes=tiles, key=kname, cont=not first)
                    first = False
                    off += n
                for _ in range(k1 - k0):
                    yield
            k = c.keys[kname]
            for kc in range(nk):
                WBk[buf][kc].w = ("d", k, k.cnt)

        class WStream:
            def __init__(self, groups):
                self.groups = groups
                self.gi = 0
                self.buf = 0
                self.gen = None
                self.nextgen = None
                for _ in wload_gen(groups[0][0], groups[0][1], 0):
                    pass

            def begin(self):
                b = self.buf
                if self.gi + 1 < len(self.groups):
                    p, nk = self.groups[self.gi + 1]
                    self.nextgen = wload_gen(p, nk, 1 - b)
                else:
                    self.nextgen = None
                return b

            def pump(self, n):
                if self.nextgen is None:
                    return
                for _ in range(n):
                    try:
                        next(self.nextgen)
                    except StopIteration:
                        self.nextgen = None
                        return

            def end(self):
                self.pump(64)
                self.gi += 1
                self.buf = 1 - self.buf

        def fm_group(ws, nk, noc, coltiles, rhs_fn, evac_fn, oc_base=0, mrows=128):
            b = ws.begin()
            for oc in range(noc):
                pset = ps_i[0] % 2
                ps_i[0] += 1
                for kc in range(nk):
                    for j, (c0, n) in enumerate(coltiles):
                        pt = PS[3 * pset + j]
                        rt, rap = rhs_fn(kc, c0, n)
                        last = (kc == nk - 1)
                        mm(pt, pt.ap[0:mrows, 0:n], WBk[b][kc], WBk[b][kc].ap[:, oc * 128:oc * 128 + mrows], rt, rap,
                           start=(kc == 0), stop=last, signal=(last and j == len(coltiles) - 1))
                ws.pump((nk + noc - 1) // noc)
                for j, (c0, n) in enumerate(coltiles):
                    pt = PS[3 * pset + j]
                    evac_fn(oc_base + oc, pt, pt.ap[0:mrows, 0:n], c0, n)
            ws.end()

        def tm_group(ws, nk, ncols, toktiles, evac_fn):
            b = ws.begin()
            for ii, (ti, c0, C) in enumerate(toktiles):
                pt = PS[6 + (ps_i[0] % 2)]
                ps_i[0] += 1
                for kc in range(nk):
                    mm(pt, pt.ap[0:C, 0:ncols], H[kc], H[kc].ap[:, c0:c0 + C], WBk[b][kc], WBk[b][kc].ap[:, 0:ncols],
                       start=(kc == 0), stop=(kc == nk - 1))
                ws.pump(2)
                evac_fn(ti, pt, pt.ap[0:C, 0:ncols], C)
            ws.end()

        def rstd_of(src_tiles, coltiles, sq_slots, rstd_t):
            for kc in range(16):
                sq = sq_slots[kc % 2]
                ncol = coltiles[-1][0] + coltiles[-1][1]
                act(lambda q: q.activation(out=sq.ap[:, 0:ncol], in_=src_tiles[kc].ap[:, 0:ncol], func=AF.Square),
                    [src_tiles[kc]], [sq])
                for j, (c0, n) in enumerate(coltiles):
                    mm(PS[j], PS[j].ap[:, 0:n], CBt, onesb, sq, sq.ap[:, c0:c0 + n], start=(kc == 0), stop=(kc == 15),
                       signal=(kc == 15 or j == len(coltiles) - 1))
            for j, (c0, n) in enumerate(coltiles):
                act(lambda q: q.activation(out=rstd_t.ap[:, c0:c0 + n], in_=PS[j].ap[:, 0:n], func=AF.Sqrt,
                                           scale=1.0 / D, bias=EPS), [PS[j]], [rstd_t])
            dve(lambda q: q.reciprocal(out=rstd_t.ap[:, 0:ncol], in_=rstd_t.ap[:, 0:ncol]), [rstd_t], [rstd_t])

        def s17(ap3):
            return pat(ap3, [[1, 16], [0, 4]])

        def modulate_chunk(dst_t, dst_ap_fn, kc, rstd_t, tmp_t, has_s):
            ncol = NT if has_s else NPS
            dve(lambda q: q.tensor_tensor(out=tmp_t.ap[:, 0:ncol], in0=X[kc].ap[:, 0:ncol], in1=rstd_t.ap[:, 0:ncol], op=ALU.mult),
                [X[kc], rstd_t], [tmp_t])
            act(lambda q: q.activation(out=dst_ap_fn(0, NPS), in_=tmp_t.ap[:, 0:NPS], func=AF.Identity,
                                       scale=mod3[:, 1, kc, 0:1], bias=mod3[:, 0, kc, 0:1]), [tmp_t, MOD], [dst_t])
            if has_s:
                tv = tmp_t.ap[:, NPS:NT].rearrange("p (s t) -> p s t", t=4)
                dve(lambda q: q.tensor_tensor(out=tv, in0=tv, in1=s17(mod3[:, 1, kc, 1:17]), op=ALU.mult), [tmp_t, MOD], [tmp_t])
                dve(lambda q: q.tensor_tensor(out=dst_ap_fn(NPS, NT).rearrange("p (s t) -> p s t", t=4), in0=tv,
                                              in1=s17(mod3[:, 0, kc, 1:17]), op=ALU.add), [tmp_t, MOD], [dst_t])

        RWv = RW[:]
        P_SQ = [T(RWv[:, i * 544:(i + 1) * 544].bitcast(BF16), "psq%d" % i) for i in range(2)]
        P_RSTD = T(RWv[:, 1088:2176], "prstd")
        P_TMP = T(RWv[:, 2176:3264], "ptmp")
        P_TMP2 = T(RWv[:, 0:1088], "ptmp2")
        RHf = RH[:]
        E_SQ = [T(RHf[:, i * 544:(i + 1) * 544].bitcast(BF16), "esq%d" % i) for i in range(2)]
        E_RSTD = T(RHf[:, 1088:2176], "erstd")
        E_XO = [T(RHf[:, 2176 + i * NT: 2176 + (i + 1) * NT], "exo%d" % i) for i in range(2)]

        def prologue(ls, coltiles, has_s):
            c.barrier()
            c.dma("sp", MOD.ap, modscr[ls * 128:(ls + 1) * 128, :], writes=[MOD], key="setup")
            gpre = pat(NPRE.ap[:, ls * 16:(ls + 1) * 16], [[1, 16], [0, 17]])
            gpost = pat(NPOST.ap[:, ls * 16:(ls + 1) * 16], [[1, 16], [0, 17]])
            dve(lambda q: q.scalar_tensor_tensor(out=mod3[:, 1], in0=mod3[:, 1], scalar=1.0, in1=gpre, op0=ALU.add, op1=ALU.mult),
                [MOD, NPRE], [MOD])
            dve(lambda q: q.tensor_tensor(out=mod3[:, 2], in0=mod3[:, 2], in1=gpost, op=ALU.mult), [MOD, NPOST], [MOD])
            rstd_of(X, coltiles, P_SQ, P_RSTD)
            for kc in range(16):
                modulate_chunk(H[kc], lambda a, b_, kc=kc: H[kc].ap[:, a:b_], kc, P_RSTD, P_TMP if kc % 2 == 0 else P_TMP2, has_s)

        def epilogue(ls, sb, coltiles, has_s, final, dump_idx=None):
            c.barrier()
            rstd_of(X, coltiles, E_SQ, E_RSTD)
            ncol = NT if has_s else NPS
            for kc in range(16):
                xo = E_XO[kc % 2]
                c.dma("sp", xo.ap[:, 0:ncol], xscr[kc * 128:(kc + 1) * 128, 0:ncol], writes=[xo], key="xo%d" % (kc % 2))
                dve(lambda q: q.tensor_tensor(out=X[kc].ap[:, 0:ncol], in0=X[kc].ap[:, 0:ncol], in1=E_RSTD.ap[:, 0:ncol], op=ALU.mult),
                    [X[kc], E_RSTD], [X[kc]])
                dve(lambda q: q.scalar_tensor_tensor(out=X[kc].ap[:, 0:NPS], in0=X[kc].ap[:, 0:NPS], scalar=mod3[:, 2, kc, 0:1],
                                                     in1=xo.ap[:, 0:NPS], op0=ALU.mult, op1=ALU.add), [X[kc], MOD, xo], [X[kc]])
                if has_s:
                    xv = X[kc].ap[:, NPS:NT].rearrange("p (s t) -> p s t", t=4)
                    dve(lambda q: q.tensor_tensor(out=xv, in0=xv, in1=s17(mod3[:, 2, kc, 1:17]), op=ALU.mult), [X[kc], MOD], [X[kc]])
                    dve(lambda q: q.tensor_tensor(out=X[kc].ap[:, NPS:NT], in0=X[kc].ap[:, NPS:NT], in1=xo.ap[:, NPS:NT], op=ALU.add),
                        [X[kc], xo], [X[kc]])
                if not final:
                    c.dma("pool", xscr[kc * 128:(kc + 1) * 128, 0:ncol], X[kc].ap[:, 0:ncol], reads=[X[kc]], key="xout%d" % (kc % 4))
                if dump_idx is not None and dbg.get("dump"):
                    r0 = ((sb * 4 + dump_idx) * 16 + kc) * 128
                    c.dma("pool", O["dump"][r0:r0 + 128, 0:ncol], X[kc].ap[:, 0:ncol], reads=[X[kc]], key="dump", is_output=True)

        try:
            TMPA = T(RB[:, 0:2048], "tmpa")
            TMPB = T(RB[:, 2048:4096], "tmpb")
            c.dma("sp", CSTt.ap, I["cst"], writes=[CSTt], key="setup")
            c.dma("sp", TMPA.ap[:, 0:512], I["cstb"], writes=[TMPA], key="setup")
            dve(lambda q: q.tensor_copy(out=CBt.ap, in_=TMPA.ap[:, 0:512]), [TMPA], [CBt])
            c.dma("sp", WGK.ap[0:17, :], I["wgk17"], writes=[WGK], key="setup")
            load_cols(NPRE, NPRE.ap, I["npre"], 64, TMPB)
            load_cols(NPOST, NPOST.ap, I["npost"], 64, TMPB)
            load_cols(S5D, S5D.ap, I["s5d"], 16, TMPB)
            load_cols(GN, GN.ap, I["gnorm"], 8, TMPB)
            load_cols(RN, RN.ap, I["rnorm"], 8, TMPB)
            load_cols(BADA, BADA.ap[:, 0:96], I["b_ada"][0:96, :], 96, TMPB)
            load_cols(BADA, BADA.ap[:, 96:192], I["b_ada"][96:192, :], 96, TMPB)
            for h in range(8):
                dve(lambda q: q.memset(SST[h].ap, 0.0), [], [SST[h]])
            dve(lambda q: q.memset(XPR.ap, 0.0), [], [XPR])
            dve(lambda q: q.memset(XPI.ap, 0.0), [], [XPI])

            halt(1)
            if not dbg.get("skip_ada"):
                c.dma("sp", TMPA.ap[0:17, :], I["cv"], writes=[TMPA], key="setup")
                act(lambda q: q.activation(out=TMPA.ap[0:17, :], in_=TMPA.ap[0:17, :], func=AF.Silu), [TMPA], [TMPA])
                SCT = T(RH[:, 0:136].bitcast(BF16).rearrange("p (k s) -> p k s", k=16), "scT")
                for kc in range(16):
                    c.op("pe", lambda q: q.transpose(PS[7].ap[:, 0:17], TMPA.ap[0:17, kc * 128:(kc + 1) * 128], cs("ident", 17, 0, 17)),
                         [TMPA, CSTt], [PS[7]])
                    dve(lambda q: q.tensor_copy(out=SCT.ap[:, kc, :], in_=PS[7].ap[:, 0:17]), [PS[7]], [SCT])
                MODB = T(RH[:, 200:200 + 816], "modb")
                modb3 = MODB.ap.rearrange("p (k s) -> p k s", s=17)
                groups = []
                for ls in range(4):
                    for g in range(12):
                        groups.append(([(I["w_ada"][ls * D:(ls + 1) * D, :], g * 512, 512)], 16))
                ws = WStream(groups)
                for ls in range(4):
                    for g in range(12):
                        def ev(oc, pt, pap, c0, n, ls=ls):
                            dve(lambda q: q.tensor_scalar(out=modb3[:, oc, :], in0=pap, scalar1=BADA.ap[:, ls * 48 + oc: ls * 48 + oc + 1],
                                                          scalar2=None, op0=ALU.add), [pt, BADA], [MODB])
                        fm_group(ws, 16, 4, [(0, 17)], lambda kc, c0, n: (SCT, SCT.ap[:, kc, :]), ev, oc_base=g * 4)
                    c.dma("pool", modscr[ls * 128:(ls + 1) * 128, :], MODB.ap, reads=[MODB], key="modout")
                c.barrier()
            else:
                dve(lambda q: q.memset(TMPA.ap[:, 0:816], 0.0), [], [TMPA])
                for ls in range(4):
                    c.dma("pool", modscr[ls * 128:(ls + 1) * 128, :], TMPA.ap[:, 0:816], reads=[TMPA], key="modout")
                c.barrier()


            RELU = [T(RT[:, i * 512:(i + 1) * 512], "relu%d" % i) for i in range(2)]
            relu_i = [0]

            def mlp(l, coltiles):
                groups = []
                for ffg in range(4):
                    for j in range(4):
                        groups.append(([(I["w_up"][l * D:(l + 1) * D, :], ffg * 2048 + j * 512, 512)], 16))
                    for j in range(4):
                        r0 = l * 4 * D + ffg * 2048
                        groups.append(([(I["w_down"][r0:r0 + 2048, :], j * 512, 512)], 16))
                ws = WStream(groups)
                for ffg in range(4):
                    def ev_up(oc, pt, pap, c0, n):
                        tm_ = RELU[relu_i[0] % 2]
                        relu_i[0] += 1
                        act(lambda q: q.activation(out=tm_.ap[:, 0:n], in_=pap, func=AF.Relu), [pt], [tm_])
                        dve(lambda q: q.tensor_tensor(out=Bk[oc].ap[:, c0:c0 + n], in0=tm_.ap[:, 0:n], in1=tm_.ap[:, 0:n], op=ALU.mult),
                            [tm_], [Bk[oc]])

                    def ev_dn(oc, pt, pap, c0, n, ffg=ffg):
                        if ffg == 0:
                            dve(lambda q: q.tensor_copy(out=X[oc].ap[:, c0:c0 + n], in_=pap), [pt], [X[oc]])
                        else:
                            dve(lambda q: q.tensor_tensor(out=X[oc].ap[:, c0:c0 + n], in0=X[oc].ap[:, c0:c0 + n], in1=pap, op=ALU.add),
                                [pt, X[oc]], [X[oc]])
                    for j in range(4):
                        fm_group(ws, 16, 4, coltiles, lambda kc, c0, n: (H[kc], H[kc].ap[:, c0:c0 + n]), ev_up, oc_base=j * 4)
                    for j in range(4):
                        fm_group(ws, 16, 4, coltiles, lambda kc, c0, n: (Bk[kc], Bk[kc].ap[:, c0:c0 + n]), ev_dn, oc_base=j * 4)

            def mixer(sb, coltiles, toktiles, has_s):
                ncol = NT if has_s else NPS
                last_sb = (sb == nsb - 1)
                RXf = RX[:]
                Fm = [T(RXf[:, i * NT:(i + 1) * NT], "F%d" % i) for i in range(8)]
                o = [8 * NT]

                def rx(n):
                    a = RXf[:, o[0]:o[0] + n]
                    o[0] += n
                    return a
                KTM = T(rx(1152).rearrange("p (t d) -> p t d", t=9), "ktm")
                VTM = T(rx(1152).bitcast(BF16).rearrange("p (t d) -> p t d", t=9), "vtm")
                GTM = T(rx(1152).rearrange("p (t d) -> p t d", t=9), "gtm")
                ROT = [T(rx(NT), "rot%d" % i) for i in range(2)]
                RTM = T(rx(1152).rearrange("p (a t d) -> p a t d", a=2, t=9), "rtm")
                E1 = T(rx(128), "e1"); E2 = T(rx(128), "e2"); E3 = T(rx(128), "e3")
                QIN = T(rx(64).bitcast(BF16), "qin"); KIN = T(rx(64).bitcast(BF16), "kin")
                KOUT = T(rx(64).bitcast(BF16), "kout"); ATTM = T(rx(64).bitcast(BF16), "attm")
                QRAW = T(rx(64).bitcast(BF16), "qraw"); KM = T(rx(64).bitcast(BF16), "km")
                QINF = T(rx(64), "qinf")
                SQB = T(rx(128).bitcast(BF16).rearrange("p (d n) -> p d n", d=2), "sqb")
                RS = T(rx(128), "rs"); M1 = T(rx(128), "m1"); M2 = T(rx(128), "m2")
                assert o[0] <= 16 * NT, o[0]
                RTf = RT[:]
                SS = [T(RTf[:, i * 256:(i + 1) * 256], "ss%d" % i) for i in range(2)]
                SO = [T(RTf[:, 512 + i * 256: 512 + (i + 1) * 256], "so%d" % i) for i in range(2)]
                OSB = T(RTf[:, 1024:1280].rearrange("p (d n) -> p d n", d=2), "osb")
                TMC = T(RTf[:, 1280:1536].rearrange("p (d n) -> p d n", d=2), "tmc")
                GLR = Fm[7]

                c.dma("sp", ROT[0].ap, I["rot_fm"][(sb * 2) * 128:(sb * 2 + 1) * 128, :], writes=[ROT[0]], key="setup")
                c.dma("sp", ROT[1].ap, I["rot_fm"][(sb * 2 + 1) * 128:(sb * 2 + 2) * 128, :], writes=[ROT[1]], key="setup")
                for a in range(2):
                    c.dma("sp", RTM.ap[:, a], I["rot_tm"][(sb * 2 + a) * 128:(sb * 2 + a + 1) * 128, :].rearrange("p (t d) -> p t d", t=9),
                          writes=[RTM], key="setup")

                W = I["w_in"]
                RQ, RK, RV, RG = 3088, 3600, 4112, 5136
                groups = [([(W, 3072, 16)], 16)]
                for h in dbg.get('gla_heads', range(4)):
                    groups.append(([(W, h * 128, 128), (W, 512 + h * 128, 128), (W, 2048 + h * 256, 256)], 16))
                    groups.append(([(W, 512 + h * 128, 128), (W, 1024 + h * 256, 256)], 16))
                for h in dbg.get('ret_heads', range(4)):
                    groups.append(([(W, RQ + h * 128, 128), (W, RQ + h * 128 + 64, 64), (W, RQ + h * 128, 64),
                                    (W, RK + h * 128, 128), (W, RK + h * 128 + 64, 64), (W, RK + h * 128, 64)], 16))
                    groups.append(([(W, RG + h * 256, 256)], 16))
                    groups.append(([(W, RK + h * 128, 128), (W, RV + h * 256, 256)], 16))
                ws = WStream(groups)
                hrhs = lambda kc, c0, n: (H[kc], H[kc].ap[:, c0:c0 + n])

                dve(lambda q: q.memset(GLR.ap[0:32, :], 1.0), [], [GLR])

                def ev_glr(oc, pt, pap, c0, n):
                    dve(lambda q: q.tensor_copy(out=GLR.ap[0:16, c0:c0 + n], in_=pap), [pt], [GLR])
                fm_group(ws, 16, 1, coltiles, hrhs, ev_glr, mrows=16)
                halt(5)

                def ev_tm(ti, pt, pap, C):
                    dve(lambda q: q.tensor_copy(out=KTM.ap[0:C, ti, :], in_=pt.ap[0:C, 0:128]), [pt], [KTM])
                    dve(lambda q: q.tensor_copy(out=VTM.ap[0:C, ti, :], in_=pt.ap[0:C, 128:384]), [pt], [VTM])

                PA, PB, PATT, PO0, PO1, PU, PST, PST2 = PS[6], PS[7], PS[0], PS[1], PS[2], PS[3], PS[4], PS[5]
                gam = [1.0 - 2.0 ** (-5.0 - h) for h in range(4)]

                def o_and_state(hh, ti, c0, C, smp, st_in, st_out_p, st_out_s, dec_ap_fn, dec_imm):
                    PO = [PO0, PO1]
                    for d in range(2):
                        mm(PO[d], PO[d].ap[:, 0:C], VTM, VTM.ap[0:C, ti, d * 128:(d + 1) * 128], ATTM, ATTM.ap[0:C, 0:C], True, False,
                           signal=False)
                    if not smp:
                        for d in range(2):
                            mm(PO[d], PO[d].ap[:, 0:C], SBF, SBF.ap[:, d * 128:(d + 1) * 128], QIN, QIN.ap[:, 0:C], False, True)
                        mm(PU, PU.ap[:, 0:256], KOUT, KOUT.ap[0:C, :], VTM, VTM.ap[0:C, ti, :], True, True)
                        if dec_imm is None:
                            dve(lambda q: q.scalar_tensor_tensor(out=SST[hh].ap, in0=SST[hh].ap, scalar=dec_ap_fn(C - 1), in1=PU.ap[:, 0:256],
                                                                 op0=ALU.mult, op1=ALU.add), [SST[hh], E1, PU], [SST[hh]])
                        else:
                            dve(lambda q: q.scalar_tensor_tensor(out=SST[hh].ap, in0=SST[hh].ap, scalar=float(dec_imm ** 128), in1=PU.ap[:, 0:256],
                                                                 op0=ALU.mult, op1=ALU.add), [SST[hh], PU], [SST[hh]])
                        act(lambda q: q.activation(out=SBF.ap, in_=SST[hh].ap, func=AF.Copy), [SST[hh]], [SBF])
                    else:
                        for s_ in range(NSEQ):
                            ss = SS[s_ % 2]
                            so = SO[s_ % 2]
                            r0 = (s_ * 4 + (hh % 4)) * 128
                            c.dma("sp", ss.ap, st_in[r0:r0 + 128, :], writes=[ss], key="ss%d" % (s_ % 2))
                            for d in range(2):
                                mm(PO[d], PO[d].ap[:, 4 * s_:4 * s_ + 4], ss, ss.ap[:, d * 128:(d + 1) * 128], QINF, QINF.ap[:, 4 * s_:4 * s_ + 4],
                                   False, s_ == NSEQ - 1)
                            dve(lambda q: q.tensor_scalar(out=KM.ap[0:64, :], in0=KOUT.ap[0:64, :], scalar1=cs("seqm", 64, s_, s_ + 1),
                                                          scalar2=None, op0=ALU.mult), [KOUT, CSTt], [KM])
                            mm(PU, PU.ap[:, 0:256], KM, KM.ap[0:64, :], VTM, VTM.ap[0:64, 8, :], True, True)
                            if dec_imm is None:
                                dve(lambda q: q.scalar_tensor_tensor(out=so.ap, in0=ss.ap, scalar=dec_ap_fn(4 * s_ + 3), in1=PU.ap[:, 0:256],
                                                                     op0=ALU.mult, op1=ALU.add), [ss, E1, PU], [so])
                            else:
                                dve(lambda q: q.scalar_tensor_tensor(out=so.ap, in0=ss.ap, scalar=float(dec_imm ** 4), in1=PU.ap[:, 0:256],
                                                                     op0=ALU.mult, op1=ALU.add), [ss, PU], [so])
                            c.dma("pool", st_out_s[r0:r0 + 128, :], so.ap, reads=[so], key="so%d" % (s_ % 2), is_output=True)

                for h in dbg.get('gla_heads', range(4)):
                    def ev_fmA(oc, pt, pap, c0, n):
                        if oc < 2:
                            dve(lambda q: q.tensor_copy(out=Fm[oc].ap[:, c0:c0 + n], in_=pap), [pt], [Fm[oc]])
                        else:
                            act(lambda q: q.activation(out=Fm[oc].ap[:, c0:c0 + n], in_=pap, func=AF.Silu), [pt], [Fm[oc]])
                    fm_group(ws, 16, 4, coltiles, hrhs, ev_fmA)
                    tm_group(ws, 16, 384, toktiles, ev_tm)
                    for (ti, c0, C) in toktiles:
                        mm(PA, PA.ap[0:C, 0:128], GLR, GLR.ap[0:17, c0:c0 + C], WGK, WGK.ap[0:17, h * 128:(h + 1) * 128], True, True)
                        act(lambda q: q.activation(out=GTM.ap[0:C, ti, :], in_=PA.ap[0:C, 0:128], func=AF.Exp, scale=-1.0), [PA], [GTM])
                        act(lambda q: q.activation(out=GTM.ap[0:C, ti, :], in_=GTM.ap[0:C, ti, :], func=AF.Ln, bias=1.0), [GTM], [GTM])
                    act(lambda q: q.activation(out=SBF.ap, in_=SST[h].ap, func=AF.Copy), [SST[h]], [SBF])
                    for (ti, c0, C) in toktiles:
                        smp = (ti == 8)
                        triU = cs("triUgs", 64) if smp else cs("triUg")
                        triS = cs("triSgs", 64) if smp else cs("triSg")
                        cmk = cs("cms", 64) if smp else cs("cm")
                        g_ap = GTM.ap[0:C, ti, :]
                        mm(PA, PA.ap[:, 0:C], GTM, g_ap, CSTt, triU, True, True)
                        mm(PB, PB.ap[0:C, 0:128], CSTt, triS, GTM, g_ap, True, True)
                        act(lambda q: q.activation(out=E1.ap[:, 0:C], in_=PA.ap[:, 0:C], func=AF.Exp), [PA], [E1])
                        act(lambda q: q.activation(out=E2.ap[:, 0:C], in_=PA.ap[:, 0:C], func=AF.Exp, scale=-1.0), [PA], [E2])
                        act(lambda q: q.activation(out=E3.ap[0:C, :], in_=PB.ap[0:C, 0:128], func=AF.Exp), [PB], [E3])
                        dve(lambda q: q.scalar_tensor_tensor(out=QIN.ap[:, 0:C], in0=Fm[0].ap[:, c0:c0 + C], scalar=128.0 ** -0.5,
                                                             in1=E1.ap[:, 0:C], op0=ALU.mult, op1=ALU.mult), [Fm[0], E1], [QIN])
                        if smp:
                            dve(lambda q: q.scalar_tensor_tensor(out=QINF.ap[:, 0:C], in0=Fm[0].ap[:, c0:c0 + C], scalar=128.0 ** -0.5,
                                                                 in1=E1.ap[:, 0:C], op0=ALU.mult, op1=ALU.mult), [Fm[0], E1], [QINF])
                        dve(lambda q: q.tensor_tensor(out=KIN.ap[:, 0:C], in0=Fm[1].ap[:, c0:c0 + C], in1=E2.ap[:, 0:C], op=ALU.mult),
                            [Fm[1], E2], [KIN])
                        dve(lambda q: q.tensor_tensor(out=KOUT.ap[0:C, :], in0=KTM.ap[0:C, ti, :], in1=E3.ap[0:C, :], op=ALU.mult),
                            [KTM, E3], [KOUT])
                        mm(PATT, PATT.ap[0:C, 0:C], KIN, KIN.ap[:, 0:C], QIN, QIN.ap[:, 0:C], True, True)
                        dve(lambda q: q.tensor_tensor(out=ATTM.ap[0:C, 0:C], in0=PATT.ap[0:C, 0:C], in1=cmk[0:C, 0:C], op=ALU.mult),
                            [PATT, CSTt], [ATTM])
                        o_and_state(h, ti, c0, C, smp, I["sgla"], O["ogla_p"], O["ogla_s"], lambda col: E1.ap[:, col:col + 1], None)
                        PO = [PO0, PO1]
                        for d in range(2):
                            act(lambda q: q.activation(out=SQB.ap[:, d, 0:C], in_=PO[d].ap[:, 0:C], func=AF.Square), [PO[d]], [SQB])
                        for d in range(2):
                            mm(PST, PST.ap[:, 0:C], CBt, onesb, SQB, SQB.ap[:, d, 0:C], d == 0, d == 1)
                        act(lambda q: q.activation(out=RS.ap[:, 0:C], in_=PST.ap[:, 0:C], func=AF.Sqrt, scale=1.0 / 256.0, bias=EPS), [PST], [RS])
                        dve(lambda q: q.reciprocal(out=RS.ap[:, 0:C], in_=RS.ap[:, 0:C]), [RS], [RS])
                        for d in range(2):
                            dve(lambda q: q.tensor_tensor(out=TMC.ap[:, d, 0:C], in0=PO[d].ap[:, 0:C], in1=RS.ap[:, 0:C], op=ALU.mult),
                                [PO[d], RS], [TMC])
                            mt = Bk[2 * h + d]
                            dve(lambda q: q.scalar_tensor_tensor(out=mt.ap[:, c0:c0 + C], in0=TMC.ap[:, d, 0:C], scalar=GN.ap[:, 2 * h + d:2 * h + d + 1],
                                                                 in1=Fm[2 + d].ap[:, c0:c0 + C], op0=ALU.mult, op1=ALU.mult), [TMC, GN, Fm[2 + d]], [mt])
                    if last_sb:
                        c.dma("pool", O["ogla_p"][h * 128:(h + 1) * 128, :], SST[h].ap, reads=[SST[h]], key="stp", is_output=True)

                    halt(7)
                for h in dbg.get('ret_heads', range(4)):
                    def ev_fmB(oc, pt, pap, c0, n):
                        dve(lambda q: q.tensor_copy(out=Fm[oc].ap[:, c0:c0 + n], in_=pap), [pt], [Fm[oc]])
                    fm_group(ws, 16, 4, coltiles, hrhs, ev_fmB)
                    halt(20)
                    for (a, b_, dst) in ((0, 1, 4), (2, 3, 5)):
                        dve(lambda q: q.tensor_tensor(out=Fm[a].ap[:, 0:ncol], in0=Fm[a].ap[:, 0:ncol], in1=ROT[0].ap[:, 0:ncol], op=ALU.mult),
                            [Fm[a], ROT[0]], [Fm[a]])
                        dve(lambda q: q.tensor_tensor(out=Fm[b_].ap[:, 0:ncol], in0=Fm[b_].ap[:, 0:ncol], in1=ROT[1].ap[:, 0:ncol], op=ALU.mult),
                            [Fm[b_], ROT[1]], [Fm[b_]])
                        dve(lambda q: q.tensor_tensor(out=Fm[dst].ap[:, 0:ncol], in0=Fm[a].ap[:, 0:ncol], in1=Fm[b_].ap[:, 0:ncol], op=ALU.add),
                            [Fm[a], Fm[b_]], [Fm[dst]])

                    def ev_fmC(oc, pt, pap, c0, n):
                        act(lambda q: q.activation(out=Fm[6 + oc].ap[:, c0:c0 + n], in_=pap, func=AF.Silu), [pt], [Fm[6 + oc]])
                    fm_group(ws, 16, 2, coltiles, hrhs, ev_fmC)
                    halt(21)
                    tm_group(ws, 16, 384, toktiles, ev_tm)
                    halt(26)
                    k1 = KTM.ap[:, :, 0:64]; k2 = KTM.ap[:, :, 64:128]
                    g1 = GTM.ap[:, :, 0:64]; g2 = GTM.ap[:, :, 64:128]
                    tv = Fm[0].ap[:, 0:576].rearrange("p (t d) -> p t d", t=9)
                    cosT = RTM.ap[:, 0]; sinT = RTM.ap[:, 1]
                    dve(lambda q: q.tensor_tensor(out=g1, in0=k1, in1=cosT, op=ALU.mult), [KTM, RTM], [GTM])
                    dve(lambda q: q.tensor_tensor(out=tv, in0=k2, in1=sinT, op=ALU.mult), [KTM, RTM], [Fm[0]])
                    dve(lambda q: q.tensor_tensor(out=g1, in0=g1, in1=tv, op=ALU.subtract), [GTM, Fm[0]], [GTM])
                    dve(lambda q: q.tensor_tensor(out=g2, in0=k1, in1=sinT, op=ALU.mult), [KTM, RTM], [GTM])
                    dve(lambda q: q.tensor_tensor(out=tv, in0=k2, in1=cosT, op=ALU.mult), [KTM, RTM, GTM], [Fm[0]])
                    dve(lambda q: q.tensor_tensor(out=g2, in0=g2, in1=tv, op=ALU.add), [GTM, Fm[0]], [GTM])
                    hh = 4 + h
                    halt(22)
                    act(lambda q: q.activation(out=SBF.ap, in_=SST[hh].ap, func=AF.Copy), [SST[hh]], [SBF])
                    for (ti, c0, C) in toktiles:
                        smp = (ti == 8)
                        tqa = cs("tqs", 128, h * 64, h * 64 + 64) if smp else cs("tq", 128, h * 128, (h + 1) * 128)
                        Dm = cs("Dsh", 64, h * 64, (h + 1) * 64) if smp else cs("Dh", 128, h * 128, (h + 1) * 128)
                        tkc = cs("tk", C, (4 + h) if smp else h, ((4 + h) if smp else h) + 1)
                        act(lambda q: q.activation(out=QRAW.ap[:, 0:C], in_=Fm[4].ap[:, c0:c0 + C], func=AF.Copy), [Fm[4]], [QRAW])
                        dve(lambda q: q.tensor_tensor(out=QIN.ap[:, 0:C], in0=Fm[4].ap[:, c0:c0 + C], in1=tqa, op=ALU.mult), [Fm[4], CSTt], [QIN])
                        if smp:
                            dve(lambda q: q.tensor_tensor(out=QINF.ap[:, 0:C], in0=Fm[4].ap[:, c0:c0 + C], in1=tqa, op=ALU.mult),
                                [Fm[4], CSTt], [QINF])
                        act(lambda q: q.activation(out=KIN.ap[:, 0:C], in_=Fm[5].ap[:, c0:c0 + C], func=AF.Copy), [Fm[5]], [KIN])
                        dve(lambda q: q.tensor_scalar(out=KOUT.ap[0:C, :], in0=GTM.ap[0:C, ti, :], scalar1=tkc, scalar2=None, op0=ALU.mult),
                            [GTM, CSTt], [KOUT])
                        mm(PATT, PATT.ap[0:C, 0:C], KIN, KIN.ap[:, 0:C], QRAW, QRAW.ap[:, 0:C], True, True)
                        dve(lambda q: q.tensor_tensor(out=ATTM.ap[0:C, 0:C], in0=PATT.ap[0:C, 0:C], in1=Dm[0:C, 0:C], op=ALU.mult),
                            [PATT, CSTt], [ATTM])
                        if ti == 0:
                            halt(23)
                        o_and_state(hh, ti, c0, C, smp, I["sret"], O["oret_p"], O["oret_s"], None, gam[h])
                        if ti == 0:
                            halt(24)
                        if ti == 7:
                            halt(25)
                        PO = [PO0, PO1]
                        for d in range(2):
                            act(lambda q: q.activation(out=OSB.ap[:, d, 0:C], in_=PO[d].ap[:, 0:C], func=AF.Copy), [PO[d]], [OSB])
                        for d in range(2):
                            mm(PST, PST.ap[:, 0:C], CSTt, cs("onesf"), OSB, OSB.ap[:, d, 0:C], d == 0, d == 1)
                        for d in range(2):
                            act(lambda q: q.activation(out=TMC.ap[:, d, 0:C], in_=OSB.ap[:, d, 0:C], func=AF.Square), [OSB], [TMC])
                        for d in range(2):
                            mm(PST2, PST2.ap[:, 0:C], CSTt, cs("onesf"), TMC, TMC.ap[:, d, 0:C], d == 0, d == 1)
                        dve(lambda q: q.tensor_scalar(out=M1.ap[:, 0:C], in0=PST.ap[:, 0:C], scalar1=1.0 / 256.0, scalar2=None, op0=ALU.mult),
                            [PST], [M1])
                        dve(lambda q: q.tensor_tensor(out=M2.ap[:, 0:C], in0=M1.ap[:, 0:C], in1=M1.ap[:, 0:C], op=ALU.mult), [M1], [M2])
                        dve(lambda q: q.scalar_tensor_tensor(out=M2.ap[:, 0:C], in0=PST2.ap[:, 0:C], scalar=1.0 / 256.0, in1=M2.ap[:, 0:C],
                                                             op0=ALU.mult, op1=ALU.subtract), [PST2, M2], [M2])
                        act(lambda q: q.activation(out=RS.ap[:, 0:C], in_=M2.ap[:, 0:C], func=AF.Sqrt, bias=EPS), [M2], [RS])
                        dve(lambda q: q.reciprocal(out=RS.ap[:, 0:C], in_=RS.ap[:, 0:C]), [RS], [RS])
                        for d in range(2):
                            dve(lambda q: q.tensor_tensor(out=OSB.ap[:, d, 0:C], in0=OSB.ap[:, d, 0:C], in1=M1.ap[:, 0:C], op=ALU.subtract),
                                [OSB, M1], [OSB])
                            dve(lambda q: q.tensor_tensor(out=OSB.ap[:, d, 0:C], in0=OSB.ap[:, d, 0:C], in1=RS.ap[:, 0:C], op=ALU.mult),
                                [OSB, RS], [OSB])
                            mt = Bk[8 + 2 * h + d]
                            dve(lambda q: q.scalar_tensor_tensor(out=mt.ap[:, c0:c0 + C], in0=OSB.ap[:, d, 0:C], scalar=RN.ap[:, 2 * h + d:2 * h + d + 1],
                                                                 in1=Fm[6 + d].ap[:, c0:c0 + C], op0=ALU.mult, op1=ALU.mult), [OSB, RN, Fm[6 + d]], [mt])
                    if last_sb:
                        c.dma("pool", O["oret_p"][h * 128:(h + 1) * 128, :], SST[hh].ap, reads=[SST[hh]], key="stp", is_output=True)

                    halt(8)
                halt(9)
                c.barrier()
                groups = [([(I["w_out"], j * 512, 512)], 16) for j in range(4)]
                ws = WStream(groups)

                def ev_out(oc, pt, pap, c0, n):
                    dve(lambda q: q.tensor_copy(out=X[oc].ap[:, c0:c0 + n], in_=pap), [pt], [X[oc]])
                for j in range(4):
                    fm_group(ws, 16, 4, coltiles, lambda kc, c0, n: (Bk[kc], Bk[kc].ap[:, c0:c0 + n]), ev_out, oc_base=j * 4)

            def s5_layer(sb, coltiles, toktiles, has_s):
                ncol = NT if has_s else NPS
                last_sb = (sb == nsb - 1)
                o = [3264]

                def rw(n):
                    a = RWv[:, o[0]:o[0] + n]
                    o[0] += n
                    return a
                TMR = T(rw(512), "tmr"); TMI = T(rw(512), "tmi")
                TTR = T(rw(512), "ttr"); TTI = T(rw(512), "tti")
                PQ = [T(rw(256).bitcast(BF16), "pq%d" % i) for i in range(4)]
                ARE = T(rw(512), "are"); AIM = T(rw(512), "aim")
                ccr_f = rw(256)
                CCR = T(ccr_f.bitcast(BF16).rearrange("p (j m) -> p j m", j=4), "ccr")
                QQ = [T(rw(256).bitcast(BF16), "qq%d" % i) for i in range(4)]
                ccin_f = rw(256)
                CCIN = T(ccin_f.bitcast(BF16).rearrange("p (j m) -> p j m", j=4), "ccin")
                assert o[0] <= 9216, o[0]
                ARG = T(RWv[:, 0:512], "arg")
                KI = T(RWv[:, 512:1024].bitcast(I32), "ki")
                GEL = P_TMP
                RTf = RT[:]
                t_o = [0]

                def rt(n):
                    a = RTf[:, t_o[0]:t_o[0] + n]
                    t_o[0] += n
                    return a
                class _V:
                    pass
                SH = P_TMP
                SHap = P_TMP.ap[:, 0:512]
                bbr_f = rt(256); bbi_f = rt(256)
                BBR = T(bbr_f.bitcast(BF16), "bbr"); BBI = T(bbi_f.bitcast(BF16), "bbi")
                sm = [T(rt(64), "sm%d" % i) for i in range(12)]
                DA = T(rt(17), "da"); DS = T(rt(17), "ds")
                c4 = [T(rt(4), "c4_%d" % i) for i in range(8)]
                XSR = T(rt(64), "xsr"); XSI = T(rt(64), "xsi"); XNR = T(rt(64), "xnr"); XNI = T(rt(64), "xni")
                CT = T(rt(128), "ct")
                TR64 = T(rt(128), "tr64")
                YT = T(rt(128), "yt")
                assert t_o[0] <= 2048, t_o[0]

                def sincos(arg_t, arg_ap, sin_ap, cos_ap, sin_t, cos_t, ki_ap):
                    dve(lambda q: q.tensor_scalar(out=ki_ap, in0=arg_ap, scalar1=1.0 / TWO_PI, scalar2=None, op0=ALU.mult), [arg_t], [KI])
                    dve(lambda q: q.scalar_tensor_tensor(out=arg_ap, in0=ki_ap, scalar=-TWO_PI, in1=arg_ap, op0=ALU.mult, op1=ALU.add),
                        [KI, arg_t], [arg_t])
                    dve(lambda q: q.tensor_scalar(out=arg_ap, in0=arg_ap, scalar1=-math.pi, scalar2=math.pi, op0=ALU.max, op1=ALU.min),
                        [arg_t], [arg_t])
                    act(lambda q: q.activation(out=sin_ap, in_=arg_ap, func=AF.Sin), [arg_t], [sin_t])
                    act(lambda q: q.activation(out=cos_ap, in_=arg_ap, func=AF.Sin, scale=0.5), [arg_t], [cos_t])
                    dve(lambda q: q.tensor_tensor(out=cos_ap, in0=cos_ap, in1=cos_ap, op=ALU.mult), [cos_t], [cos_t])
                    dve(lambda q: q.tensor_scalar(out=cos_ap, in0=cos_ap, scalar1=-2.0, scalar2=1.0, op0=ALU.mult, op1=ALU.add), [cos_t], [cos_t])

                def tt(out_t, out_ap, a_t, a_ap, b_t, b_ap, op):
                    dve(lambda q: q.tensor_tensor(out=out_ap, in0=a_ap, in1=b_ap, op=op), [a_t, b_t], [out_t])

                PBR, PBI, PWR, PWI, PY, PX = PS[0], PS[1], PS[2], PS[3], PS[4], PS[5]
                iota4 = pat(cs("iota1"), [[0, 4], [1, 128]])

                for kc in range(16):
                    LR, LI, LD, BTR, BTI, MAG, SN, CS_, NR, FR, FI, DEN = sm
                    cached = (sb > 0)
                    a3 = lambda t_: t_.ap.rearrange("p (j n) -> p j n", j=4)
                    tabs = ((TTR, TTR.ap, 1024, 512), (TTI, TTI.ap, 1536, 512), (BBR, bbr_f, 2048, 256), (BBI, bbi_f, 2304, 256),
                            (CCR, ccr_f, 2560, 256), (CCIN, ccin_f, 2816, 256))
                    tmtabs = ((TMR, TMR.ap, 0, 512), (TMI, TMI.ap, 512, 512))
                    r0_ = kc * 128
                    if cached:
                        for i_, (t_, ap_, o_, n_) in enumerate(tabs):
                            c.dma("sp", ap_, s5tab[r0_:r0_ + 128, o_:o_ + n_], writes=[t_], key="s5tabr", cont=(i_ > 0))
                        k_ = c.keys["s5tabr"]
                        for (t_, ap_, o_, n_) in tabs:
                            t_.w = ("d", k_, k_.cnt)
                    if not cached:
                        for tl, nm in ((LR, "lamre_gc"), (LI, "lamim_gc"), (LD, "logdt_gc"), (BTR, "bT_re"), (BTI, "bT_im")):
                            c.dma("sp", tl.ap, I[nm][kc * 128:(kc + 1) * 128, :], writes=[tl], key="s5small")
                        act(lambda q: q.activation(out=LD.ap, in_=LD.ap, func=AF.Exp), [LD], [LD])
                        tt(MAG, MAG.ap, LR, LR.ap, LD, LD.ap, ALU.mult)
                        tt(NR, NR.ap, LI, LI.ap, LD, LD.ap, ALU.mult)
                        sincos(NR, NR.ap, SN.ap, CS_.ap, SN, CS_, KI.ap[:, 0:64])
                        act(lambda q: q.activation(out=MAG.ap, in_=MAG.ap, func=AF.Exp), [MAG], [MAG])
                        tt(CS_, CS_.ap, CS_, CS_.ap, MAG, MAG.ap, ALU.mult)
                        tt(SN, SN.ap, SN, SN.ap, MAG, MAG.ap, ALU.mult)
                        dve(lambda q: q.tensor_scalar(out=NR.ap, in0=CS_.ap, scalar1=-1.0, scalar2=None, op0=ALU.add), [CS_], [NR])
                        tt(DEN, DEN.ap, LR, LR.ap, LR, LR.ap, ALU.mult)
                        tt(MAG, MAG.ap, LI, LI.ap, LI, LI.ap, ALU.mult)
                        tt(DEN, DEN.ap, DEN, DEN.ap, MAG, MAG.ap, ALU.add)
                        dve(lambda q: q.reciprocal(out=DEN.ap, in_=DEN.ap), [DEN], [DEN])
                        tt(FR, FR.ap, NR, NR.ap, LR, LR.ap, ALU.mult)
                        tt(MAG, MAG.ap, SN, SN.ap, LI, LI.ap, ALU.mult)
                        tt(FR, FR.ap, FR, FR.ap, MAG, MAG.ap, ALU.add)
                        tt(FR, FR.ap, FR, FR.ap, DEN, DEN.ap, ALU.mult)
                        tt(FI, FI.ap, SN, SN.ap, LR, LR.ap, ALU.mult)
                        tt(MAG, MAG.ap, NR, NR.ap, LI, LI.ap, ALU.mult)
                        tt(FI, FI.ap, FI, FI.ap, MAG, MAG.ap, ALU.subtract)
                        tt(FI, FI.ap, FI, FI.ap, DEN, DEN.ap, ALU.mult)
                        tt(MAG, MAG.ap, FR, FR.ap, BTR, BTR.ap, ALU.mult)
                        tt(CS_, CS_.ap, FI, FI.ap, BTI, BTI.ap, ALU.mult)
                        tt(MAG, MAG.ap, MAG, MAG.ap, CS_, CS_.ap, ALU.subtract)
                        tt(CS_, CS_.ap, FR, FR.ap, BTI, BTI.ap, ALU.mult)
                        tt(SN, SN.ap, FI, FI.ap, BTR, BTR.ap, ALU.mult)
                        tt(CS_, CS_.ap, CS_, CS_.ap, SN, SN.ap, ALU.add)
                        mC = pat(cs("maskC"), [[1, 8], [0, 64]])
                        dve(lambda q: q.tensor_tensor(out=BBR.ap.rearrange("p (g n) -> p g n", g=8), in0=pat(MAG.ap, [[0, 8], [1, 64]]), in1=mC, op=ALU.mult),
                            [MAG, CSTt], [BBR])
                        dve(lambda q: q.tensor_tensor(out=BBI.ap.rearrange("p (g n) -> p g n", g=8), in0=pat(CS_.ap, [[0, 8], [1, 64]]), in1=mC, op=ALU.mult),
                            [CS_, CSTt], [BBI])
                        ct4 = CT.ap.rearrange("p (a j c) -> p a j c", a=2, j=4)
                        c.dma("sp", ct4[:, 0], I["cT_re"][kc * 512:(kc + 1) * 512, :].rearrange("(j p) c -> p j c", p=128), writes=[CT], key="s5small")
                        c.dma("sp", ct4[:, 1], I["cT_im"][kc * 512:(kc + 1) * 512, :].rearrange("(j p) c -> p j c", p=128), writes=[CT], key="s5small")
                        mB = pat(cs("maskB"), [[8, 4], [1, 8], [0, 16]])
                        ctre = pat(ct4[:, 0], [[16, 4], [0, 8], [1, 16]])
                        ctim = pat(ct4[:, 1], [[16, 4], [0, 8], [1, 16]])
                        cc4 = lambda t_: t_.ap.rearrange("p j (g c) -> p j g c", g=8)
                        dve(lambda q: q.tensor_tensor(out=cc4(CCR), in0=ctre, in1=mB, op=ALU.mult), [CT, CSTt], [CCR])
                        dve(lambda q: q.tensor_tensor(out=cc4(CCIN), in0=ctim, in1=mB, op=ALU.mult), [CT, CSTt], [CCIN])
                        dve(lambda q: q.tensor_scalar(out=CCIN.ap, in0=CCIN.ap, scalar1=-1.0, scalar2=None, op0=ALU.mult), [CCIN], [CCIN])
                        LRc, LIc, LDc, ACc, THc = c4[0:5]
                        for tl, nm in ((LRc, "lamre_c"), (LIc, "lamim_c"), (LDc, "logdt_c")):
                            c.dma("sp", tl.ap, I[nm][:, kc * 4:(kc + 1) * 4], writes=[tl], key="s5small")
                        act(lambda q: q.activation(out=LDc.ap, in_=LDc.ap, func=AF.Exp), [LDc], [LDc])
                        tt(ACc, ACc.ap, LRc, LRc.ap, LDc, LDc.ap, ALU.mult)
                        tt(THc, THc.ap, LIc, LIc.ap, LDc, LDc.ap, ALU.mult)
                        a3 = lambda t_: t_.ap.rearrange("p (j n) -> p j n", j=4)
                        dve(lambda q: q.tensor_tensor(out=a3(ARG), in0=pat(THc.ap, [[1, 4], [0, 128]]), in1=iota4, op=ALU.mult), [THc, CSTt], [ARG])
                        sincos(ARG, ARG.ap, TTI.ap, TTR.ap, TTI, TTR, KI.ap)
                        dve(lambda q: q.tensor_tensor(out=a3(ARG), in0=pat(ACc.ap, [[1, 4], [0, 128]]), in1=iota4, op=ALU.mult), [ACc, CSTt], [ARG])
                        act(lambda q: q.activation(out=SHap, in_=ARG.ap, func=AF.Exp), [ARG], [SH])
                        tt(TTR, TTR.ap, TTR, TTR.ap, SH, SHap, ALU.mult)
                        tt(TTI, TTI.ap, TTI, TTI.ap, SH, SHap, ALU.mult)
                        if nsb > 1:
                            for (t_, ap_, o_, n_) in tabs:
                                c.dma("sp", s5tab[r0_:r0_ + 128, o_:o_ + n_], ap_, reads=[t_], key="s5tabw")
                    def rowb(nm):
                        a = I[nm][0:1, kc * 512:(kc + 1) * 512]
                        return AP(a.tensor, a.offset, [[0, 128], [1, 512]])
                    def load_rows():
                        c.dma("sp", ARG.ap, rowb("logdt_r"), writes=[ARG], key="s5row")
                        c.dma("sp", TMR.ap, rowb("lamre_r"), writes=[TMR], key="s5row", cont=True)
                        c.dma("sp", TMI.ap, rowb("lamim_r"), writes=[TMI], key="s5row", cont=True)
                        k_ = c.keys["s5row"]
                        for t_ in (ARG, TMR, TMI):
                            t_.w = ("d", k_, k_.cnt)
                        act(lambda q: q.activation(out=ARG.ap, in_=ARG.ap, func=AF.Exp), [ARG], [ARG])
                        tt(TMR, TMR.ap, TMR, TMR.ap, ARG, ARG.ap, ALU.mult)
                        tt(TMI, TMI.ap, TMI, TMI.ap, ARG, ARG.ap, ALU.mult)

                    def build_tm(dr, di, col):
                        sc_ap = cs("scol", 128, col, col + 1)
                        dve(lambda q: q.tensor_scalar(out=ARG.ap, in0=TMI.ap, scalar1=sc_ap, scalar2=None, op0=ALU.mult), [TMI, CSTt], [ARG])
                        act(lambda q: q.activation(out=SHap, in_=TMR.ap, func=AF.Exp, scale=sc_ap), [TMR, CSTt], [SH])
                        sincos(ARG, ARG.ap, di.ap, dr.ap, di, dr, KI.ap)
                        tt(dr, dr.ap, dr, dr.ap, SH, SHap, ALU.mult)
                        tt(di, di.ap, di, di.ap, SH, SHap, ALU.mult)
                    if has_s:
                        for (src, dstt) in ((I["s5re"], XSR), (I["s5im"], XSI)):
                            c.dma("sp", TR64.ap[0:64, :], src[kc * 64:(kc + 1) * 64, :], writes=[TR64], key="s5small")
                            c.op("pe", lambda q: q.transpose(PX.ap[:, 0:64], TR64.ap[0:64, :], cs("ident", 64, 0, 64)), [TR64, CSTt], [PX])
                            dve(lambda q: q.tensor_copy(out=dstt.ap, in_=PX.ap[:, 0:64]), [PX], [dstt])
                    dve(lambda q: q.tensor_scalar(out=DA.ap, in0=mod3[:, 1, kc, :], scalar1=S5D.ap[:, kc:kc + 1], scalar2=None, op0=ALU.mult),
                        [MOD, S5D], [DA])
                    dve(lambda q: q.tensor_scalar(out=DS.ap, in0=mod3[:, 0, kc, :], scalar1=S5D.ap[:, kc:kc + 1], scalar2=None, op0=ALU.mult),
                        [MOD, S5D], [DS])

                    tile_order = [t_ for t_ in toktiles if t_[0] == 8] + [t_ for t_ in toktiles if t_[0] != 8]
                    for t_idx, (ti, c0, C) in enumerate(tile_order):
                        smp = (ti == 8)
                        PBR, PBI = (PS[0], PS[1]) if t_idx % 2 == 0 else (PS[6], PS[7])
                        if cached:
                            if t_idx == 0:
                                for i_, (t_, ap_, o_, n_) in enumerate(tmtabs):
                                    c.dma("sp", ap_, s5tab[r0_:r0_ + 128, o_:o_ + n_], writes=[t_], key="s5tabr", cont=(i_ > 0))
                                k_ = c.keys["s5tabr"]
                                for (t_, ap_, o_, n_) in tmtabs:
                                    t_.w = ("d", k_, k_.cnt)
                        elif t_idx == 0 or (has_s and t_idx == 1):
                            load_rows()
                            build_tm(TMR, TMI, 1 if smp else 0)
                            if (not smp) and nsb > 1:
                                for (t_, ap_, o_, n_) in tmtabs:
                                    c.dma("sp", s5tab[r0_:r0_ + 128, o_:o_ + n_], ap_, reads=[t_], key="s5tabw")
                        mm(PBR, PBR.ap[0:C, :], H[kc], H[kc].ap[:, c0:c0 + C], BBR, BBR.ap, True, True)
                        mm(PBI, PBI.ap[0:C, :], H[kc], H[kc].ap[:, c0:c0 + C], BBI, BBI.ap, True, True)
                        Tr, Ti = (TMR, TMI)
                        for (pq, pb, tb) in ((0, PBR, Tr), (1, PBI, Ti), (2, PBI, Tr), (3, PBR, Ti)):
                            dve(lambda q: q.tensor_tensor(out=PQ[pq].ap[0:C, :], in0=pb.ap[0:C, :], in1=tb.ap[0:C, :], op=ALU.mult), [pb, tb], [PQ[pq]])
                        if smp:
                            triP = CBt.ap[0:64, 384:448]; triN = CBt.ap[0:64, 448:512]
                        else:
                            triP = CBt.ap[:, 128:256]; triN = CBt.ap[:, 256:384]
                        for j in range(4):
                            mm(PWR, PWR.ap[:, j * 128:j * 128 + C], PQ[0], PQ[0].ap[0:C, j * 128:(j + 1) * 128], CBt, triP, True, False, signal=False)
                            mm(PWR, PWR.ap[:, j * 128:j * 128 + C], PQ[1], PQ[1].ap[0:C, j * 128:(j + 1) * 128], CBt, triN, False, True, signal=(j == 3))
                        for j in range(4):
                            mm(PWI, PWI.ap[:, j * 128:j * 128 + C], PQ[2], PQ[2].ap[0:C, j * 128:(j + 1) * 128], CBt, triP, True, False, signal=False)
                            mm(PWI, PWI.ap[:, j * 128:j * 128 + C], PQ[3], PQ[3].ap[0:C, j * 128:(j + 1) * 128], CBt, triP, False, True, signal=(j == 3))
                        are3 = a3(ARE); aim3 = a3(AIM)
                        if not smp:
                            for j in range(4):
                                act(lambda q: q.activation(out=are3[:, j, 0:C], in_=PWR.ap[:, j * 128:j * 128 + C], func=AF.Identity,
                                                           bias=XPR.ap[:, kc * 4 + j:kc * 4 + j + 1]), [PWR, XPR], [ARE])
                                act(lambda q: q.activation(out=aim3[:, j, 0:C], in_=PWI.ap[:, j * 128:j * 128 + C], func=AF.Identity,
                                                           bias=XPI.ap[:, kc * 4 + j:kc * 4 + j + 1]), [PWI, XPI], [AIM])
                            trv, tiv = a3(TTR)[:, :, 0:C], a3(TTI)[:, :, 0:C]
                            arv, aiv = are3[:, :, 0:C], aim3[:, :, 0:C]
                            qv = lambda i_: a3(QQ[i_])[:, :, 0:C]
                        else:
                            w4 = lambda p_: pat(p_.ap, [[128, 4], [4, 16], [1, 4]])
                            a4 = lambda t_: pat(t_.ap, [[128, 4], [4, 16], [1, 4]])
                            x4 = lambda t_: pat(t_.ap, [[16, 4], [1, 16], [0, 4]])
                            dve(lambda q: q.tensor_tensor(out=a4(ARE), in0=w4(PWR), in1=x4(XSR), op=ALU.add), [PWR, XSR], [ARE])
                            dve(lambda q: q.tensor_tensor(out=a4(AIM), in0=w4(PWI), in1=x4(XSI), op=ALU.add), [PWI, XSI], [AIM])
                            trv = pat(TTR.ap, [[128, 4], [0, 16], [1, 4]]); tiv = pat(TTI.ap, [[128, 4], [0, 16], [1, 4]])
                            arv, aiv = a4(ARE), a4(AIM)
                            qv = lambda i_: pat(QQ[i_].ap, [[128, 4], [4, 16], [1, 4]])
                        dve(lambda q: q.tensor_tensor(out=qv(0), in0=trv, in1=arv, op=ALU.mult), [TTR, ARE], [QQ[0]])
                        if not smp:
                            dve(lambda q: q.scalar_tensor_tensor(out=qv(1), in0=tiv, scalar=-1.0, in1=aiv, op0=ALU.mult, op1=ALU.mult), [TTI, AIM], [QQ[1]])
                        else:
                            dve(lambda q: q.tensor_tensor(out=qv(1), in0=tiv, in1=aiv, op=ALU.mult), [TTI, AIM], [QQ[1]])
                            dve(lambda q: q.tensor_scalar(out=QQ[1].ap, in0=QQ[1].ap, scalar1=-1.0, scalar2=None, op0=ALU.mult), [QQ[1]], [QQ[1]])
                        pool(lambda q: q.tensor_tensor(out=qv(2), in0=trv, in1=aiv, op=ALU.mult), [TTR, AIM], [QQ[2]])
                        pool(lambda q: q.tensor_tensor(out=qv(3), in0=tiv, in1=arv, op=ALU.mult), [TTI, ARE], [QQ[3]])
                        if not smp:
                            lr_ = lambda t_: a3(t_)[:, :, C - 1]
                            m1, m2 = c4[5], c4[6]
                            tt(m1, m1.ap, TTR, lr_(TTR), ARE, lr_(ARE), ALU.mult)
                            tt(m2, m2.ap, TTI, lr_(TTI), AIM, lr_(AIM), ALU.mult)
                            xr_dst = XPR.ap[:, kc * 4:(kc + 1) * 4]; xi_dst = XPI.ap[:, kc * 4:(kc + 1) * 4]
                            dve(lambda q: q.tensor_tensor(out=xr_dst, in0=m1.ap, in1=m2.ap, op=ALU.subtract), [m1, m2, ARE, AIM], [XPR])
                            tt(m1, m1.ap, TTR, lr_(TTR), AIM, lr_(AIM), ALU.mult)
                            tt(m2, m2.ap, TTI, lr_(TTI), ARE, lr_(ARE), ALU.mult)
                            dve(lambda q: q.tensor_tensor(out=xi_dst, in0=m1.ap, in1=m2.ap, op=ALU.add), [m1, m2, ARE, AIM], [XPI])
                        else:
                            l4 = lambda t_: pat(t_.ap[:, 3:4], [[128, 4], [4, 16]])
                            tl = lambda t_: pat(t_.ap[:, 3:4], [[128, 4], [0, 16]])
                            n3 = lambda t_: t_.ap.rearrange("p (j s) -> p j s", j=4)
                            m1, m2 = sm[10], sm[11]
                            tt(m1, n3(m1), TTR, tl(TTR), ARE, l4(ARE), ALU.mult)
                            tt(m2, n3(m2), TTI, tl(TTI), AIM, l4(AIM), ALU.mult)
                            tt(XNR, XNR.ap, m1, m1.ap, m2, m2.ap, ALU.subtract)
                            tt(m1, n3(m1), TTR, tl(TTR), AIM, l4(AIM), ALU.mult)
                            tt(m2, n3(m2), TTI, tl(TTI), ARE, l4(ARE), ALU.mult)
                            tt(XNI, XNI.ap, m1, m1.ap, m2, m2.ap, ALU.add)
                            for (xn, dsto) in ((XNR, O["os5re_s"]), (XNI, O["os5im_s"])):
                                c.op("pe", lambda q: q.transpose(PX.ap[0:64, 0:128], xn.ap, cs("ident")), [xn, CSTt], [PX])
                                dve(lambda q: q.tensor_copy(out=TR64.ap[0:64, :], in_=PX.ap[0:64, 0:128]), [PX], [TR64])
                                c.dma("pool", dsto[kc * 64:(kc + 1) * 64, :], TR64.ap[0:64, :], reads=[TR64], key="s5out", is_output=True)
                        for j in range(4):
                            for (qi, cc) in ((0, CCR), (1, CCR), (2, CCIN), (3, CCIN)):
                                if not smp:
                                    rap = a3(QQ[qi])[:, j, 0:C]
                                else:
                                    rap = QQ[qi].ap[:, j * 128:j * 128 + 64]
                                mm(PY, PY.ap[:, 0:C], cc, cc.ap[:, j, :], QQ[qi], rap, (j == 0 and qi == 0), (j == 3 and qi == 3),
                                   signal=(j == 3 and qi == 3))
                        tt(YT, YT.ap[:, 0:C], X[kc], X[kc].ap[:, c0:c0 + C], P_RSTD, P_RSTD.ap[:, c0:c0 + C], ALU.mult)
                        if not smp:
                            dve(lambda q: q.scalar_tensor_tensor(out=YT.ap[:, 0:C], in0=YT.ap[:, 0:C], scalar=DA.ap[:, 0:1], in1=PY.ap[:, 0:C],
                                                                 op0=ALU.mult, op1=ALU.add), [YT, DA, PY], [YT])
                            dve(lambda q: q.tensor_scalar(out=X[kc].ap[:, c0:c0 + C], in0=YT.ap[:, 0:C], scalar1=DS.ap[:, 0:1], scalar2=None, op0=ALU.add),
                                [YT, DS], [X[kc]])
                        else:
                            y3 = YT.ap[:, 0:64].rearrange("p (s t) -> p s t", t=4)
                            dve(lambda q: q.tensor_tensor(out=y3, in0=y3, in1=s17(DA.ap[:, 1:17]), op=ALU.mult), [YT, DA], [YT])
                            dve(lambda q: q.tensor_tensor(out=YT.ap[:, 0:64], in0=YT.ap[:, 0:64], in1=PY.ap[:, 0:64], op=ALU.add), [YT, PY], [YT])
                            dve(lambda q: q.tensor_tensor(out=X[kc].ap[:, c0:c0 + C].rearrange("p (s t) -> p s t", t=4), in0=y3, in1=s17(DS.ap[:, 1:17]), op=ALU.add),
                                [YT, DS], [X[kc]])
                    xa = X[kc].ap[:, 0:ncol]; ga = GEL.ap[:, 0:ncol]
                    tt(GEL, ga, X[kc], xa, X[kc], xa, ALU.mult)
                    dve(lambda q: q.tensor_scalar(out=ga, in0=ga, scalar1=0.044715, scalar2=1.0, op0=ALU.mult, op1=ALU.add), [GEL], [GEL])
                    tt(GEL, ga, GEL, ga, X[kc], xa, ALU.mult)
                    act(lambda q: q.activation(out=ga, in_=ga, func=AF.Sigmoid, scale=1.5957691216057308), [GEL], [GEL])
                    tt(Bk[kc], Bk[kc].ap[:, 0:ncol], GEL, ga, X[kc], xa, ALU.mult)
                if last_sb:
                    for (xn, dsto) in ((XPR, O["os5re_p"]), (XPI, O["os5im_p"])):
                        c.op("pe", lambda q: q.transpose(PX.ap[0:64, 0:128], xn.ap, cs("ident")), [xn, CSTt], [PX])
                        dve(lambda q: q.tensor_copy(out=TR64.ap[0:64, :], in_=PX.ap[0:64, 0:128]), [PX], [TR64])
                        c.dma("pool", dsto, TR64.ap[0:64, :], reads=[TR64], key="s5out", is_output=True)
                c.barrier()
                groups = []
                for j in range(4):
                    groups.append(([(I["w_glu_a"], j * 512, 512)], 16))
                    groups.append(([(I["w_glu_b"], j * 512, 512)], 16))
                ws = WStream(groups)
                zr = lambda kc, c0, n: (Bk[kc], Bk[kc].ap[:, c0:c0 + n])

                def ev_a(oc, pt, pap, c0, n):
                    dve(lambda q: q.tensor_copy(out=X[oc].ap[:, c0:c0 + n], in_=pap), [pt], [X[oc]])

                def ev_b(oc, pt, pap, c0, n):
                    tm_ = RELU[relu_i[0] % 2]
                    relu_i[0] += 1
                    act(lambda q: q.activation(out=tm_.ap[:, 0:n], in_=pap, func=AF.Sigmoid), [pt], [tm_])
                    dve(lambda q: q.tensor_tensor(out=X[oc].ap[:, c0:c0 + n], in0=X[oc].ap[:, c0:c0 + n], in1=tm_.ap[:, 0:n], op=ALU.mult),
                        [tm_, X[oc]], [X[oc]])
                for j in range(4):
                    fm_group(ws, 16, 4, coltiles, zr, ev_a, oc_base=j * 4)
                    fm_group(ws, 16, 4, coltiles, zr, ev_b, oc_base=j * 4)

            halt(2)
            TOKT_ALL = [(t, t * 128, 128) for t in range(8)]

            for sb in range(nsb):
                has_s = (sb == 0)
                coltiles = [(0, 512), (512, 512)] + ([(1024, 64)] if has_s else [])
                toktiles = TOKT_ALL + ([(8, NPS, NS)] if has_s else [])
                ncol = NT if has_s else NPS
                start_at = dbg.get("inject", 0)

                c.barrier()
                if start_at == 0:
                    for (ti, c0, C) in toktiles:
                        src = I["xp"][sb * NPS + ti * 128: sb * NPS + ti * 128 + 128, :] if ti < 8 else I["xs"]
                        tt = TMPA if ti % 2 == 0 else TMPB
                        c.dma("sp", tt.ap[0:C, :], src, writes=[tt], key="xin%d" % (ti % 2))
                        for k4 in range(4):
                            pt = PS[k4 % 2]
                            for jj in range(4):
                                kc = k4 * 4 + jj
                                c.op("pe", lambda q: q.transpose(pt.ap[:, jj * 128: jj * 128 + C], tt.ap[0:C, kc * 128:(kc + 1) * 128],
                                                                 cs("ident", C, 0, C)), [tt, CSTt], [pt], signal=(jj == 3))
                            dve(lambda q: q.tensor_copy(out=RX3[:, k4 * 4:(k4 + 1) * 4, c0:c0 + C],
                                                        in_=pt.ap.rearrange("p (j n) -> p j n", j=4)[:, :, 0:C]),
                                [pt], [X[k4 * 4 + jj] for jj in range(4)])
                else:
                    for kc in range(16):
                        r0 = (sb * 16 + kc) * 128
                        c.dma("sp", X[kc].ap[:, 0:ncol], I["xinj"][r0:r0 + 128, 0:ncol], writes=[X[kc]], key="setup")
                for kc in range(16):
                    c.dma("pool", xscr[kc * 128:(kc + 1) * 128, 0:ncol], X[kc].ap[:, 0:ncol], reads=[X[kc]], key="xout%d" % (kc % 4))

                halt(3)
                if start_at <= 0:
                    prologue(0, coltiles, has_s)
                    c.barrier()
                    halt(4)
                    mixer(sb, coltiles, toktiles, has_s)
                    halt(10)
                    epilogue(0, sb, coltiles, has_s, False, dump_idx=0)
                    halt(11)
                if start_at <= 1 and dbg.get("stop_after", 9) >= 1:
                    prologue(1, coltiles, has_s)
                    c.barrier()
                    mlp(0, coltiles)
                    epilogue(1, sb, coltiles, has_s, False, dump_idx=1)
                if start_at <= 2 and dbg.get("stop_after", 9) >= 2:
                    prologue(2, coltiles, has_s)
                    c.barrier()
                    s5_layer(sb, coltiles, toktiles, has_s)
                    epilogue(2, sb, coltiles, has_s, False, dump_idx=2)
                if start_at <= 3 and dbg.get("stop_after", 9) >= 3:
                    prologue(3, coltiles, has_s)
                    c.barrier()
                    mlp(1, coltiles)
                    epilogue(3, sb, coltiles, has_s, True, dump_idx=3)
                c.barrier()
                for (ti, c0, C) in toktiles:
                    tt = TMPA if ti % 2 == 0 else TMPB
                    for k4 in range(4):
                        pt = PS[k4 % 2]
                        for jj in range(4):
                            kc = k4 * 4 + jj
                            c.op("pe", lambda q: q.transpose(pt.ap[0:C, jj * 128:(jj + 1) * 128], X[kc].ap[:, c0:c0 + C], cs("ident")),
                                 [X[kc], CSTt], [pt], signal=(jj == 3))
                        dve(lambda q: q.tensor_copy(out=tt.ap[0:C, k4 * 512:(k4 + 1) * 512], in_=pt.ap[0:C, :]), [pt], [tt])
                    dst = O["yp"][sb * NPS + ti * 128: sb * NPS + ti * 128 + 128, :] if ti < 8 else O["ys"]
                    c.dma("pool", dst, tt.ap[0:C, :], reads=[tt], key="yout%d" % (ti % 2), is_output=True)
        except _Halt:
            pass
        c.finish()
    return nc


_NC_CACHE = {}


def _prep_shared(inp, nsb):
    f = np.float32
    cst, off, cb, rot_fm, rot_tm = _get_consts(nsb)
    sh = {}
    sh["w_ada"] = np.ascontiguousarray(inp["w_ada"], f).reshape(4 * D, 3 * D)
    sh["b_ada"] = np.ascontiguousarray(inp["b_ada"], f).reshape(4 * 48, 128)
    sh["npre"] = np.ascontiguousarray(inp["norm_pre"], f).reshape(64, 128)
    sh["npost"] = np.ascontiguousarray(inp["norm_post"], f).reshape(64, 128)
    sh["w_in"] = np.ascontiguousarray(inp["w_in_mix"][0], f)
    sh["wgk17"] = np.concatenate([inp["w_gla_gk"][0], inp["b_gla_gk"][0][None, :]], axis=0).astype(f)
    sh["gnorm"] = np.ascontiguousarray(inp["gla_head_norm"][0], f).reshape(8, 128)
    sh["rnorm"] = np.ascontiguousarray(inp["ret_head_norm"][0], f).reshape(8, 128)
    sh["w_out"] = np.ascontiguousarray(inp["w_out_mix"][0], f)
    lamre = inp["s5_lam_re"][0].astype(f); lamim = inp["s5_lam_im"][0].astype(f)
    logdt = np.broadcast_to(inp["s5_log_dt"][0].astype(f)[:, None], (128, 64))
    for nm, a in (("lamre", lamre), ("lamim", lamim), ("logdt", logdt)):
        a = np.ascontiguousarray(a)
        sh[nm + "_c"] = np.ascontiguousarray(a.reshape(64, 128).T)
        sh[nm + "_gc"] = np.ascontiguousarray(np.repeat(a, 16, axis=0))
        sh[nm + "_r"] = np.ascontiguousarray(a.reshape(1, 8192))
    sh["bT_re"] = np.ascontiguousarray(inp["s5_b_re"][0].astype(f).transpose(0, 2, 1)).reshape(D, 64)
    sh["bT_im"] = np.ascontiguousarray(inp["s5_b_im"][0].astype(f).transpose(0, 2, 1)).reshape(D, 64)
    sh["cT_re"] = np.ascontiguousarray(inp["s5_c_re"][0].astype(f).transpose(0, 2, 1)).reshape(8192, 16)
    sh["cT_im"] = np.ascontiguousarray(inp["s5_c_im"][0].astype(f).transpose(0, 2, 1)).reshape(8192, 16)
    sh["s5d"] = np.ascontiguousarray(inp["s5_d"][0], f).reshape(16, 128)
    sh["w_glu_a"] = np.ascontiguousarray(inp["w_glu_a"][0], f)
    sh["w_glu_b"] = np.ascontiguousarray(inp["w_glu_b"][0], f)
    sh["w_up"] = np.ascontiguousarray(inp["w_mlp_up"], f).reshape(2 * D, 4 * D)
    sh["w_down"] = np.ascontiguousarray(inp["w_mlp_down"], f).reshape(2 * 4 * D, D)
    sh["cst"] = cst
    sh["cstb"] = cb
    sh["rot_fm"] = rot_fm.reshape(nsb * 2 * 128, NT)
    sh["rot_tm"] = rot_tm.reshape(nsb * 2 * 128, 9 * 64)
    return sh


def _prep_core(inp, core, nsb):
    f = np.float32
    b = core // 2
    m = {}
    if nsb == 2:
        m["xp"] = np.ascontiguousarray(inp["x_prompt"][b], f)
    else:
        m["xp"] = np.ascontiguousarray(inp["x_prompt"][b, :NPS], f)
    s0 = core * NSEQ
    m["xs"] = np.ascontiguousarray(inp["x_sample"][s0:s0 + NSEQ], f).reshape(NS, D)
    m["cv"] = np.concatenate([inp["c_prompt"][b][None, :], inp["c_sample"][s0:s0 + NSEQ]], axis=0).astype(f)
    m["sgla"] = np.ascontiguousarray(inp["state_gla"][0, s0:s0 + NSEQ], f).reshape(NSEQ * 4 * 128, 256)
    m["sret"] = np.ascontiguousarray(inp["state_ret"][0, s0:s0 + NSEQ], f).reshape(NSEQ * 4 * 128, 256)
    for nm, key in (("s5re", "state_s5_re"), ("s5im", "state_s5_im")):
        a = inp[key][0, s0:s0 + NSEQ].astype(f).reshape(NSEQ, 64, 128)
        m[nm] = np.ascontiguousarray(a.transpose(1, 0, 2)).reshape(64 * 16, 128)
    return m


def kernel(**inp):
    nsb = 2
    if nsb not in _NC_CACHE:
        _NC_CACHE[nsb] = build(nsb)
    nc = _NC_CACHE[nsb]
    sh = _prep_shared(inp, nsb)
    in_maps = []
    for core in range(NCORES):
        m = dict(sh)
        m.update(_prep_core(inp, core, nsb))
        in_maps.append(m)
    res = run_bass_kernel_spmd(nc, in_maps, core_ids=list(range(NCORES)))
    R = res.results
    f = np.float32
    y_prompt = np.stack([R[2 * b]["yp"] for b in range(4)], axis=0).astype(f)
    y_sample = np.concatenate([R[c_]["ys"].reshape(NSEQ, 4, D) for c_ in range(NCORES)], axis=0).astype(f)
    gla_p = np.stack([R[2 * b]["ogla_p"].reshape(4, 128, 256) for b in range(4)], axis=0)[None].astype(f)
    ret_p = np.stack([R[2 * b]["oret_p"].reshape(4, 128, 256) for b in range(4)], axis=0)[None].astype(f)
    gla_s = np.concatenate([R[c_]["ogla_s"].reshape(NSEQ, 4, 128, 256) for c_ in range(NCORES)], axis=0)[None].astype(f)
    ret_s = np.concatenate([R[c_]["oret_s"].reshape(NSEQ, 4, 128, 256) for c_ in range(NCORES)], axis=0)[None].astype(f)

    def s5p(name):
        return np.stack([R[2 * b][name].reshape(128, 64) for b in range(4)], axis=0)[None].astype(f)

    def s5s(name):
        outs = []
        for c_ in range(NCORES):
            a = R[c_][name].reshape(64, NSEQ, 128).transpose(1, 0, 2).reshape(NSEQ, 128, 64)
            outs.append(a)
        return np.concatenate(outs, axis=0)[None].astype(f)
    return (y_prompt, y_sample, gla_p, gla_s, ret_p, ret_s, s5p("os5re_p"), s5s("os5re_s"), s5p("os5im_p"), s5s("os5im_s"))
```

```python
import math
import numpy as np
from contextlib import ExitStack
import concourse.bass as bass
import concourse.mybir as mybir
from concourse.ap import AP
from concourse.bass_utils import run_bass_kernel_spmd

F32 = mybir.dt.float32
BF16 = mybir.dt.bfloat16
I32 = mybir.dt.int32
ALU = mybir.AluOpType
AF = mybir.ActivationFunctionType

D = 2048
KC = 16
NPS = 1024
NS = 64
NSEQ = 16
NT = NPS + NS
EPS = 1e-6
NCORES = 8
TWO_PI = 2.0 * math.pi


class T:
    def __init__(self, ap, name=""):
        self.ap = ap
        self.name = name
        self.w = None
        self.r = []


class DKey:
    def __init__(self, sem):
        self.sem = sem
        self.cnt = 0


class Eng:
    def __init__(self, q, sem, name):
        self.q = q
        self.sem = sem
        self.cnt = 0
        self.name = name
        self.waited = {}


class Ctx:
    def __init__(self, nc, stack):
        self.nc = nc
        self.stack = stack
        self.E = {}
        for name, q in (("pe", nc.tensor), ("act", nc.scalar), ("dve", nc.vector),
                        ("pool", nc.gpsimd), ("sp", nc.sync)):
            sem = stack.enter_context(nc.semaphore("s_" + name))
            self.E[name] = Eng(q, sem, name)
        self.keys = {}
        self.retired = []
        self.nkeys = 0
        self.out_keys = set()
        self.ninst = 0

    def sb(self, name, shape, dt):
        return self.stack.enter_context(self.nc.sbuf_tensor(name, list(shape), dt))

    def ps(self, name, shape, dt=F32):
        return self.stack.enter_context(self.nc.psum_tensor(name, list(shape), dt))

    def key(self, name, rotate=True):
        k = self.keys.get(name)
        if k is None or (rotate and k.cnt >= 12000):
            if k is not None:
                self.retired.append(k)
            self.nkeys += 1
            sem = self.stack.enter_context(self.nc.semaphore("d_%s_%d" % (name, self.nkeys)))
            k = DKey(sem)
            self.keys[name] = k
        return k

    def _collect(self, reads, writes):
        deps = []
        for t in reads:
            if t.w is not None:
                deps.append(t.w)
        for t in writes:
            if t.w is not None:
                deps.append(t.w)
            deps.extend(t.r)
        return deps

    def _wait(self, eng, deps, skip_self=False):
        need = {}
        for d in deps:
            if d[0] == "e":
                s, v = d[1], d[2]
            else:
                s, v = d[1].sem, d[2]
            if v <= 0:
                continue
            if skip_self and s is eng.sem:
                continue
            if need.get(s, 0) < v:
                need[s] = v
        for s, v in need.items():
            if eng.waited.get(s, 0) < v:
                eng.q.wait_ge(s, v)
                eng.waited[s] = v
                self.ninst += 1

    def op(self, en, fn, reads=(), writes=(), signal=True):
        eng = self.E[en]
        self._wait(eng, self._collect(reads, writes), skip_self=(en == "pe"))
        ins = fn(eng.q)
        self.ninst += 1
        if signal:
            eng.cnt += 1
            ins.then_inc(eng.sem, 1)
            me = ("e", eng.sem, eng.cnt)
        else:
            me = ("e", eng.sem, eng.cnt + 1)
        for t in reads:
            t.r.append(me)
        for t in writes:
            t.w = me
            t.r = []
        return ins

    def dma(self, qn, out_ap, in_ap, reads=(), writes=(), key="misc", is_output=False, cont=False, **kw):
        eng = self.E[qn]
        k = self.key(key, rotate=not cont)
        deps = self._collect(reads, writes)
        if cont:
            deps = [d for d in deps if not (d[0] == "d" and d[1] is k)]
        elif k.cnt > 0:
            deps.append(("d", k, k.cnt))
        self._wait(eng, deps)
        ins = eng.q.dma_start(out=out_ap, in_=in_ap, **kw)
        self.ninst += 1
        k.cnt += 16
        ins.then_inc(k.sem, 16)
        me = ("d", k, k.cnt)
        for t in reads:
            t.r.append(me)
        for t in writes:
            t.w = me
            t.r = []
        if is_output:
            self.out_keys.add(key)
        return ins

    def barrier(self):
        deps = [("e", e.sem, e.cnt) for e in self.E.values() if e.cnt > 0]
        deps += [("d", k, k.cnt) for k in list(self.keys.values()) + self.retired if k.cnt > 0]
        for e in self.E.values():
            self._wait(e, deps)

    def finish(self):
        self.barrier()


class _Halt(Exception):
    pass


def pat(ap, pattern):
    base = ap.ap
    return AP(ap.tensor, ap.offset, [list(base[0])] + [list(p) for p in pattern])


def _const_tables(nsb):
    f = np.float32
    j = np.arange(128)[:, None]
    i = np.arange(128)[None, :]
    caus = (j <= i)
    same = ((j // 4) == (i // 4))
    C = {}
    C["ident"] = np.eye(128, dtype=f)
    C["onesf"] = np.ones((128, 128), f)
    C["triUg"] = np.where(caus, -1.0 / 16.0, 0.0).astype(f)
    C["triSg"] = np.where(j > i, -1.0 / 16.0, 0.0).astype(f)
    C["cm"] = caus.astype(f)
    gam = 1.0 - np.power(2.0, -5.0 - np.arange(4))
    sc = 128.0 ** -0.5
    dh = []
    for h in range(4):
        dh.append(np.where(caus, np.power(gam[h], np.maximum(i - j, 0)) * sc, 0.0))
    C["Dh"] = np.concatenate(dh, axis=1).astype(f)
    j6, i6 = j[:64], i[:, :64]
    caus6 = caus[:64, :64] & same[:64, :64]
    C["triUgs"] = np.zeros((128, 64), f); C["triUgs"][:64] = np.where(caus6, -1.0 / 16.0, 0.0)
    C["triSgs"] = np.zeros((128, 64), f); C["triSgs"][:64] = np.where((j6 > i6) & same[:64, :64], -1.0 / 16.0, 0.0)
    C["cms"] = np.zeros((128, 64), f); C["cms"][:64] = caus6
    ds = np.zeros((128, 4 * 64), f)
    for h in range(4):
        ds[:64, h * 64:(h + 1) * 64] = np.where(caus6, np.power(gam[h], np.maximum(i6 - j6, 0)) * sc, 0.0)
    C["Dsh"] = ds
    tq = np.zeros((128, 4 * 128), f); tqs = np.zeros((128, 4 * 64), f); tk = np.zeros((128, 8), f)
    for h in range(4):
        tq[:, h * 128:(h + 1) * 128] = np.power(gam[h], np.arange(128) + 1)[None, :]
        tqs[:, h * 64:(h + 1) * 64] = np.power(gam[h], (np.arange(64) % 4) + 1)[None, :]
        tk[:, h] = np.power(gam[h], 127 - np.arange(128)) * sc
        tk[:64, 4 + h] = np.power(gam[h], 3 - (np.arange(64) % 4)) * sc
    C["tq"] = tq; C["tqs"] = tqs; C["tk"] = tk
    C["seqm"] = np.zeros((128, 16), f)
    C["seqm"][:64] = ((np.arange(64)[:, None] // 4) == np.arange(16)[None, :])
    C["iota1"] = np.broadcast_to((np.arange(128) + 1).astype(f)[None, :], (128, 128)).copy()
    scol = np.zeros((128, 2), f)
    scol[:, 0] = -(np.arange(128) + 1)
    scol[:, 1] = -((np.arange(128) % 4) + 1)
    C["scol"] = scol
    p = np.arange(128)
    mB = np.zeros((128, 4, 8), f)
    for jp in range(4):
        mB[p, jp, 2 * jp + p // 64] = 1.0
    C["maskB"] = mB.reshape(128, 32)
    mC = np.zeros((128, 8), f)
    mC[p, p // 16] = 1.0
    C["maskC"] = mC
    order = ["ident", "onesf", "triUg", "triSg", "cm", "Dh", "triUgs", "triSgs", "cms", "Dsh", "tq", "tqs", "tk",
             "seqm", "iota1", "scol", "maskB", "maskC"]
    off = {}
    o = 0
    for k in order:
        off[k] = (o, C[k].shape[1])
        o += C[k].shape[1]
    cst = np.concatenate([C[k] for k in order], axis=1).astype(f)
    cb = np.zeros((128, 128 * 3 + 64 * 2), f)
    cb[:, 0:128] = 1.0
    cb[:, 128:256] = caus
    cb[:, 256:384] = -caus.astype(f)
    cb[:64, 384:448] = caus6
    cb[:64, 448:512] = -caus6.astype(f)
    half = 64
    inv = (10000.0 ** (-np.arange(half, dtype=np.float32) / half)).astype(np.float32)
    rot_fm = np.zeros((nsb, 2, 128, NT), f)
    rot_tm = np.zeros((nsb, 2, 128, 9, 64), f)
    for sb in range(nsb):
        pos = np.zeros(NT, np.float32)
        pos[:NPS] = sb * NPS + np.arange(NPS)
        pos[NPS:] = 16384 + (np.arange(NS) % 4)
        ang = pos[None, :].astype(np.float32) * inv[:, None]
        co, si = np.cos(ang), np.sin(ang)
        rot_fm[sb, 0, :64] = co; rot_fm[sb, 0, 64:] = co
        rot_fm[sb, 1, :64] = -si; rot_fm[sb, 1, 64:] = si
        for t in range(9):
            n = 128 if t < 8 else 64
            a = ang[:, t * 128:t * 128 + n].T
            rot_tm[sb, 0, :n, t] = np.cos(a)
            rot_tm[sb, 1, :n, t] = np.sin(a)
    return cst, off, cb, rot_fm, rot_tm


_CST_CACHE = {}


def _get_consts(nsb):
    if nsb not in _CST_CACHE:
        _CST_CACHE[nsb] = _const_tables(nsb)
    return _CST_CACHE[nsb]


def build(nsb=2, dbg=None):
    dbg = dbg or {}
    cst_np, coff, cb_np, _, _ = _get_consts(nsb)
    NCST = cst_np.shape[1]
    nc = bass.Bass("TRN2", target_bir_lowering=False)

    def din(name, shape):
        return nc.dram_tensor(name, list(shape), F32, kind="ExternalInput").ap()

    def dout(name, shape):
        return nc.dram_tensor(name, list(shape), F32, kind="ExternalOutput").ap()

    def dscr(name, shape):
        return nc.dram_tensor(name, list(shape), F32).ap()

    NPT = nsb * NPS
    _SH = {
        "xp": [NPT, D], "xs": [NS, D], "cv": [17, D],
        "sgla": [NSEQ * 4 * 128, 256], "sret": [NSEQ * 4 * 128, 256],
        "s5re": [64 * 16, 128], "s5im": [64 * 16, 128],
        "w_ada": [4 * D, 3 * D], "b_ada": [4 * 48, 128], "npre": [4 * 16, 128], "npost": [4 * 16, 128],
        "w_in": [D, 6160], "wgk17": [17, 512], "gnorm": [8, 128], "rnorm": [8, 128], "w_out": [D, D],
        "bT_re": [D, 64], "bT_im": [D, 64], "cT_re": [8192, 16], "cT_im": [8192, 16], "s5d": [16, 128],
        "w_glu_a": [D, D], "w_glu_b": [D, D], "w_up": [2 * D, 4 * D], "w_down": [2 * 4 * D, D],
        "cst": [128, NCST], "cstb": [128, 512], "rot_fm": [nsb * 2 * 128, NT], "rot_tm": [nsb * 2 * 128, 9 * 64],
        "xinj": [nsb * 16 * 128, NT],
    }
    for nm in ("lamre", "lamim", "logdt"):
        _SH[nm + "_c"] = [128, 64]
        _SH[nm + "_gc"] = [D, 64]
        _SH[nm + "_r"] = [1, 8192]

    class _LazyIn(dict):
        def __missing__(self, k):
            v = din(k, _SH[k])
            self[k] = v
            return v
    I = _LazyIn()
    if not dbg.get("lazy_inputs"):
        for k_ in _SH:
            if k_ != "xinj" or "inject" in dbg:
                I[k_]
    O = {}
    O["yp"] = dout("yp", [NPT, D]); O["ys"] = dout("ys", [NS, D])
    O["ogla_p"] = dout("ogla_p", [4 * 128, 256]); O["ogla_s"] = dout("ogla_s", [NSEQ * 4 * 128, 256])
    O["oret_p"] = dout("oret_p", [4 * 128, 256]); O["oret_s"] = dout("oret_s", [NSEQ * 4 * 128, 256])
    O["os5re_p"] = dout("os5re_p", [64, 128]); O["os5re_s"] = dout("os5re_s", [64 * 16, 128])
    O["os5im_p"] = dout("os5im_p", [64, 128]); O["os5im_s"] = dout("os5im_s", [64 * 16, 128])
    if dbg.get("dump"):
        O["dump"] = dout("dump", [nsb * 4 * 16 * 128, NT])
    xscr = dscr("xscr", [16 * 128, NT])
    s5tab = dscr("s5tab", [16 * 128, 3072])
    modscr = dscr("modscr", [4 * 128, 48 * 17])

    with ExitStack() as st:
        c = Ctx(nc, st)

        def halt(n):
            if dbg.get("halt_at") == n:
                raise _Halt()
        RX = c.sb("RX", [128, 16 * NT], F32)
        RH = c.sb("RH", [128, 8 * NT], F32)
        RB = c.sb("RB", [128, 8 * NT], F32)
        RW = c.sb("RW", [128, 9216], F32)
        NRC = NCST + 256 + 816 + 128 + 16 + 16 + 192 + 512 + 2048 + 128 + 128 + 64
        RC = c.sb("RC", [128, NRC], F32)
        RT = c.sb("RT", [128, 2048], F32)
        PS = [T(c.ps("ps%d" % b, [128, 512])[:], "ps%d" % b) for b in range(8)]

        RX3 = RX[:].rearrange("p (k n) -> p k n", k=16)
        X = [T(RX3[:, k, :], "X%d" % k) for k in range(16)]
        RHb = RH[:].bitcast(BF16).rearrange("p (k n) -> p k n", k=16)
        H = [T(RHb[:, k, :], "H%d" % k) for k in range(16)]
        RBb = RB[:].bitcast(BF16).rearrange("p (k n) -> p k n", k=16)
        Bk = [T(RBb[:, k, :], "B%d" % k) for k in range(16)]
        WBv = [RW[:, i * 4096:(i + 1) * 4096].bitcast(BF16).rearrange("p (k n) -> p k n", k=16) for i in range(2)]
        WBk = [[T(WBv[i][:, k, :], "WB%d_%d" % (i, k)) for k in range(16)] for i in range(2)]
        WS = [T(RW[:, 8192 + i * 512: 8192 + (i + 1) * 512], "WS%d" % i) for i in range(2)]

        rc_o = [0]

        def rc(n):
            a = RC[:, rc_o[0]:rc_o[0] + n]
            rc_o[0] += n
            return a

        CSTt = T(rc(NCST), "cst")

        def cs(name, rows=128, lo=None, hi=None):
            o, n = coff[name]
            a = CSTt.ap[0:rows, o:o + n]
            if lo is not None:
                a = CSTt.ap[0:rows, o + lo:o + hi]
            return a

        CBt = T(rc(256).bitcast(BF16), "cstb")
        onesb = CBt.ap[:, 0:128]
        MOD = T(rc(816), "mod")
        mod3 = MOD.ap.rearrange("p (t k s) -> p t k s", t=3, k=16)
        NPRE = T(rc(64), "npre"); NPOST = T(rc(64), "npost")
        S5D = T(rc(16), "s5d"); GN = T(rc(8), "gn"); RN = T(rc(8), "rn")
        rc(16)
        BADA = T(rc(192), "bada")
        WGK = T(rc(512), "wgk")
        SST = [T(rc(256), "S%d" % h) for h in range(8)]
        SBF = T(rc(128).bitcast(BF16), "Sbf")
        XPR = T(rc(64), "xpr"); XPI = T(rc(64), "xpi")
        assert rc_o[0] <= NRC, (rc_o[0], NRC)

        cast_i = [0]
        ws_i = [0]
        ps_i = [0]

        def dve(fn, r, w):
            return c.op("dve", fn, r, w)

        def act(fn, r, w):
            return c.op("act", fn, r, w)

        def pool(fn, r, w):
            return c.op("pool", fn, r, w)

        def mm(out_t, out_ap, lt, lap, rt, rap, start, stop, signal=None):
            sig = stop if signal is None else signal
            return c.op("pe", lambda q: q.matmul(out_ap, lhsT=lap, rhs=rap, start=start, stop=stop),
                        [lt, rt], [out_t], signal=sig)

        def load_cols(dst_t, dst_ap, src, nrows, tmp_t):
            c.dma("sp", tmp_t.ap[0:nrows, 0:128], src, writes=[tmp_t], key="setup")
            c.op("pe", lambda q: q.transpose(PS[7].ap[:, 0:nrows], tmp_t.ap[0:nrows, 0:128], cs("ident", nrows, 0, nrows)),
                 [tmp_t, CSTt], [PS[7]])
            dve(lambda q: q.tensor_copy(out=dst_ap, in_=PS[7].ap[:, 0:nrows]), [PS[7]], [dst_t])

        def wload_gen(pieces, nk, buf):
            kname = "wb%d" % buf
            first = True
            step = 8
            for k0 in range(0, nk, step):
                k1 = min(nk, k0 + step)
                off = 0
                tiles = [WBk[buf][kc] for kc in range(k0, k1)]
                for (wap, c0, n) in pieces:
                    src = wap[k0 * 128:k1 * 128, c0:c0 + n].rearrange("(k p) n -> p k n", p=128)
                    c.dma("pool", WBv[buf][:, k0:k1, off:off + n], src, writes=tiles, key=kname, cont=not first)
                    first = False
                    off += n
                for _ in range(k1 - k0):
                    yield
            k = c.keys[kname]
            for kc in range(nk):
                WBk[buf][kc].w = ("d", k, k.cnt)

        class WStream:
            def __init__(self, groups):
                self.groups = groups
                self.gi = 0
                self.buf = 0
                self.gen = None
                self.nextgen = None
                for _ in wload_gen(groups[0][0], groups[0][1], 0):
                    pass

            def begin(self):
                b = self.buf
                if self.gi + 1 < len(self.groups):
                    p, nk = self.groups[self.gi + 1]
                    self.nextgen = wload_gen(p, nk, 1 - b)
                else:
                    self.nextgen = None
                return b

            def pump(self, n):
                if self.nextgen is None:
                    return
                for _ in range(n):
                    try:
                        next(self.nextgen)
                    except StopIteration:
                        self.nextgen = None
                        return

            def end(self):
                self.pump(64)
                self.gi += 1
                self.buf = 1 - self.buf

        def fm_group(ws, nk, noc, coltiles, rhs_fn, evac_fn, oc_base=0, mrows=128):
            b = ws.begin()
            for oc in range(noc):
                pset = ps_i[0] % 2
                ps_i[0] += 1
                for kc in range(nk):
                    for j, (c0, n) in enumerate(coltiles):
                        pt = PS[3 * pset + j]
                        rt, rap = rhs_fn(kc, c0, n)
                        last = (kc == nk - 1)
                        mm(pt, pt.ap[0:mrows, 0:n], WBk[b][kc], WBk[b][kc].ap[:, oc * 128:oc * 128 + mrows], rt, rap,
                           start=(kc == 0), stop=last, signal=(last and j == len(coltiles) - 1))
                ws.pump((nk + noc - 1) // noc)
                for j, (c0, n) in enumerate(coltiles):
                    pt = PS[3 * pset + j]
                    evac_fn(oc_base + oc, pt, pt.ap[0:mrows, 0:n], c0, n)
            ws.end()

        def tm_group(ws, nk, ncols, toktiles, evac_fn):
            b = ws.begin()
            for ii, (ti, c0, C) in enumerate(toktiles):
                pt = PS[6 + (ps_i[0] % 2)]
                ps_i[0] += 1
                for kc in range(nk):
                    mm(pt, pt.ap[0:C, 0:ncols], H[kc], H[kc].ap[:, c0:c0 + C], WBk[b][kc], WBk[b][kc].ap[:, 0:ncols],
                       start=(kc == 0), stop=(kc == nk - 1))
                ws.pump(2)
                evac_fn(ti, pt, pt.ap[0:C, 0:ncols], C)
            ws.end()

        def rstd_of(src_tiles, coltiles, sq_slots, rstd_t):
            for kc in range(16):
                sq = sq_slots[kc % 2]
                ncol = coltiles[-1][0] + coltiles[-1][1]
                act(lambda q: q.activation(out=sq.ap[:, 0:ncol], in_=src_tiles[kc].ap[:, 0:ncol], func=AF.Square),
                    [src_tiles[kc]], [sq])
                for j, (c0, n) in enumerate(coltiles):
                    mm(PS[j], PS[j].ap[:, 0:n], CBt, onesb, sq, sq.ap[:, c0:c0 + n], start=(kc == 0), stop=(kc == 15),
                       signal=(kc == 15 or j == len(coltiles) - 1))
            for j, (c0, n) in enumerate(coltiles):
                act(lambda q: q.activation(out=rstd_t.ap[:, c0:c0 + n], in_=PS[j].ap[:, 0:n], func=AF.Sqrt,
                                           scale=1.0 / D, bias=EPS), [PS[j]], [rstd_t])
            dve(lambda q: q.reciprocal(out=rstd_t.ap[:, 0:ncol], in_=rstd_t.ap[:, 0:ncol]), [rstd_t], [rstd_t])

        def s17(ap3):
            return pat(ap3, [[1, 16], [0, 4]])

        def modulate_chunk(dst_t, dst_ap_fn, kc, rstd_t, tmp_t, has_s):
            ncol = NT if has_s else NPS
            dve(lambda q: q.tensor_tensor(out=tmp_t.ap[:, 0:ncol], in0=X[kc].ap[:, 0:ncol], in1=rstd_t.ap[:, 0:ncol], op=ALU.mult),
                [X[kc], rstd_t], [tmp_t])
            dve(lambda q: q.tensor_scalar(out=dst_ap_fn(0, NPS), in0=tmp_t.ap[:, 0:NPS], scalar1=mod3[:, 1, kc, 0:1],
                                          scalar2=mod3[:, 0, kc, 0:1], op0=ALU.mult, op1=ALU.add), [tmp_t, MOD], [dst_t])
            if has_s:
                tv = tmp_t.ap[:, NPS:NT].rearrange("p (s t) -> p s t", t=4)
                dve(lambda q: q.tensor_tensor(out=tv, in0=tv, in1=s17(mod3[:, 1, kc, 1:17]), op=ALU.mult), [tmp_t, MOD], [tmp_t])
                dve(lambda q: q.tensor_tensor(out=dst_ap_fn(NPS, NT).rearrange("p (s t) -> p s t", t=4), in0=tv,
                                              in1=s17(mod3[:, 0, kc, 1:17]), op=ALU.add), [tmp_t, MOD], [dst_t])

        RWv = RW[:]
        P_SQ = [T(RWv[:, i * 544:(i + 1) * 544].bitcast(BF16), "psq%d" % i) for i in range(2)]
        P_RSTD = T(RWv[:, 1088:2176], "prstd")
        P_TMP = T(RWv[:, 2176:3264], "ptmp")
        RHf = RH[:]
        E_SQ = [T(RHf[:, i * 544:(i + 1) * 544].bitcast(BF16), "esq%d" % i) for i in range(2)]
        E_RSTD = T(RHf[:, 1088:2176], "erstd")
        E_XO = [T(RHf[:, 2176 + i * NT: 2176 + (i + 1) * NT], "exo%d" % i) for i in range(2)]

        def prologue(ls, coltiles, has_s):
            c.barrier()
            c.dma("sp", MOD.ap, modscr[ls * 128:(ls + 1) * 128, :], writes=[MOD], key="setup")
            gpre = pat(NPRE.ap[:, ls * 16:(ls + 1) * 16], [[1, 16], [0, 17]])
            gpost = pat(NPOST.ap[:, ls * 16:(ls + 1) * 16], [[1, 16], [0, 17]])
            dve(lambda q: q.scalar_tensor_tensor(out=mod3[:, 1], in0=mod3[:, 1], scalar=1.0, in1=gpre, op0=ALU.add, op1=ALU.mult),
                [MOD, NPRE], [MOD])
            dve(lambda q: q.tensor_tensor(out=mod3[:, 2], in0=mod3[:, 2], in1=gpost, op=ALU.mult), [MOD, NPOST], [MOD])
            rstd_of(X, coltiles, P_SQ, P_RSTD)
            for kc in range(16):
                modulate_chunk(H[kc], lambda a, b_, kc=kc: H[kc].ap[:, a:b_], kc, P_RSTD, P_TMP, has_s)

        def epilogue(ls, sb, coltiles, has_s, final, dump_idx=None):
            c.barrier()
            rstd_of(X, coltiles, E_SQ, E_RSTD)
            ncol = NT if has_s else NPS
            for kc in range(16):
                xo = E_XO[kc % 2]
                c.dma("sp", xo.ap[:, 0:ncol], xscr[kc * 128:(kc + 1) * 128, 0:ncol], writes=[xo], key="xo%d" % (kc % 2))
                dve(lambda q: q.tensor_tensor(out=X[kc].ap[:, 0:ncol], in0=X[kc].ap[:, 0:ncol], in1=E_RSTD.ap[:, 0:ncol], op=ALU.mult),
                    [X[kc], E_RSTD], [X[kc]])
                dve(lambda q: q.scalar_tensor_tensor(out=X[kc].ap[:, 0:NPS], in0=X[kc].ap[:, 0:NPS], scalar=mod3[:, 2, kc, 0:1],
                                                     in1=xo.ap[:, 0:NPS], op0=ALU.mult, op1=ALU.add), [X[kc], MOD, xo], [X[kc]])
                if has_s:
                    xv = X[kc].ap[:, NPS:NT].rearrange("p (s t) -> p s t", t=4)
                    dve(lambda q: q.tensor_tensor(out=xv, in0=xv, in1=s17(mod3[:, 2, kc, 1:17]), op=ALU.mult), [X[kc], MOD], [X[kc]])
                    dve(lambda q: q.tensor_tensor(out=X[kc].ap[:, NPS:NT], in0=X[kc].ap[:, NPS:NT], in1=xo.ap[:, NPS:NT], op=ALU.add),
                        [X[kc], xo], [X[kc]])
                if not final:
                    c.dma("pool", xscr[kc * 128:(kc + 1) * 128, 0:ncol], X[kc].ap[:, 0:ncol], reads=[X[kc]], key="xout%d" % (kc % 4))
                if dump_idx is not None and dbg.get("dump"):
                    r0 = ((sb * 4 + dump_idx) * 16 + kc) * 128
                    c.dma("pool", O["dump"][r0:r0 + 128, 0:ncol], X[kc].ap[:, 0:ncol], reads=[X[kc]], key="dump", is_output=True)

        try:
            TMPA = T(RB[:, 0:2048], "tmpa")
            TMPB = T(RB[:, 2048:4096], "tmpb")
            c.dma("sp", CSTt.ap, I["cst"], writes=[CSTt], key="setup")
            c.dma("sp", TMPA.ap[:, 0:512], I["cstb"], writes=[TMPA], key="setup")
            dve(lambda q: q.tensor_copy(out=CBt.ap, in_=TMPA.ap[:, 0:512]), [TMPA], [CBt])
            c.dma("sp", WGK.ap[0:17, :], I["wgk17"], writes=[WGK], key="setup")
            load_cols(NPRE, NPRE.ap, I["npre"], 64, TMPB)
            load_cols(NPOST, NPOST.ap, I["npost"], 64, TMPB)
            load_cols(S5D, S5D.ap, I["s5d"], 16, TMPB)
            load_cols(GN, GN.ap, I["gnorm"], 8, TMPB)
            load_cols(RN, RN.ap, I["rnorm"], 8, TMPB)
            load_cols(BADA, BADA.ap[:, 0:96], I["b_ada"][0:96, :], 96, TMPB)
            load_cols(BADA, BADA.ap[:, 96:192], I["b_ada"][96:192, :], 96, TMPB)
            for h in range(8):
                dve(lambda q: q.memset(SST[h].ap, 0.0), [], [SST[h]])
            dve(lambda q: q.memset(XPR.ap, 0.0), [], [XPR])
            dve(lambda q: q.memset(XPI.ap, 0.0), [], [XPI])

            halt(1)
            if not dbg.get("skip_ada"):
                c.dma("sp", TMPA.ap[0:17, :], I["cv"], writes=[TMPA], key="setup")
                act(lambda q: q.activation(out=TMPA.ap[0:17, :], in_=TMPA.ap[0:17, :], func=AF.Silu), [TMPA], [TMPA])
                SCT = T(RH[:, 0:136].bitcast(BF16).rearrange("p (k s) -> p k s", k=16), "scT")
                for kc in range(16):
                    c.op("pe", lambda q: q.transpose(PS[7].ap[:, 0:17], TMPA.ap[0:17, kc * 128:(kc + 1) * 128], cs("ident", 17, 0, 17)),
                         [TMPA, CSTt], [PS[7]])
                    dve(lambda q: q.tensor_copy(out=SCT.ap[:, kc, :], in_=PS[7].ap[:, 0:17]), [PS[7]], [SCT])
                MODB = T(RH[:, 200:200 + 816], "modb")
                modb3 = MODB.ap.rearrange("p (k s) -> p k s", s=17)
                groups = []
                for ls in range(4):
                    for g in range(12):
                        groups.append(([(I["w_ada"][ls * D:(ls + 1) * D, :], g * 512, 512)], 16))
                ws = WStream(groups)
                for ls in range(4):
                    for g in range(12):
                        def ev(oc, pt, pap, c0, n, ls=ls):
                            dve(lambda q: q.tensor_scalar(out=modb3[:, oc, :], in0=pap, scalar1=BADA.ap[:, ls * 48 + oc: ls * 48 + oc + 1],
                                                          scalar2=None, op0=ALU.add), [pt, BADA], [MODB])
                        fm_group(ws, 16, 4, [(0, 17)], lambda kc, c0, n: (SCT, SCT.ap[:, kc, :]), ev, oc_base=g * 4)
                    c.dma("pool", modscr[ls * 128:(ls + 1) * 128, :], MODB.ap, reads=[MODB], key="modout")
                c.barrier()
            else:
                dve(lambda q: q.memset(TMPA.ap[:, 0:816], 0.0), [], [TMPA])
                for ls in range(4):
                    c.dma("pool", modscr[ls * 128:(ls + 1) * 128, :], TMPA.ap[:, 0:816], reads=[TMPA], key="modout")
                c.barrier()


            RELU = [T(RT[:, i * 512:(i + 1) * 512], "relu%d" % i) for i in range(2)]
            relu_i = [0]

            def mlp(l, coltiles):
                groups = []
                for ffg in range(4):
                    for j in range(4):
                        groups.append(([(I["w_up"][l * D:(l + 1) * D, :], ffg * 2048 + j * 512, 512)], 16))
                    for j in range(4):
                        r0 = l * 4 * D + ffg * 2048
                        groups.append(([(I["w_down"][r0:r0 + 2048, :], j * 512, 512)], 16))
                ws = WStream(groups)
                for ffg in range(4):
                    def ev_up(oc, pt, pap, c0, n):
                        tm_ = RELU[relu_i[0] % 2]
                        relu_i[0] += 1
                        act(lambda q: q.activation(out=tm_.ap[:, 0:n], in_=pap, func=AF.Relu), [pt], [tm_])
                        dve(lambda q: q.tensor_tensor(out=Bk[oc].ap[:, c0:c0 + n], in0=tm_.ap[:, 0:n], in1=tm_.ap[:, 0:n], op=ALU.mult),
                            [tm_], [Bk[oc]])

                    def ev_dn(oc, pt, pap, c0, n, ffg=ffg):
                        if ffg == 0:
                            dve(lambda q: q.tensor_copy(out=X[oc].ap[:, c0:c0 + n], in_=pap), [pt], [X[oc]])
                        else:
                            dve(lambda q: q.tensor_tensor(out=X[oc].ap[:, c0:c0 + n], in0=X[oc].ap[:, c0:c0 + n], in1=pap, op=ALU.add),
                                [pt, X[oc]], [X[oc]])
                    for j in range(4):
                        fm_group(ws, 16, 4, coltiles, lambda kc, c0, n: (H[kc], H[kc].ap[:, c0:c0 + n]), ev_up, oc_base=j * 4)
                    for j in range(4):
                        fm_group(ws, 16, 4, coltiles, lambda kc, c0, n: (Bk[kc], Bk[kc].ap[:, c0:c0 + n]), ev_dn, oc_base=j * 4)

            def mixer(sb, coltiles, toktiles, has_s):
                ncol = NT if has_s else NPS
                last_sb = (sb == nsb - 1)
                RXf = RX[:]
                Fm = [T(RXf[:, i * NT:(i + 1) * NT], "F%d" % i) for i in range(8)]
                o = [8 * NT]

                def rx(n):
                    a = RXf[:, o[0]:o[0] + n]
                    o[0] += n
                    return a
                KTM = T(rx(1152).rearrange("p (t d) -> p t d", t=9), "ktm")
                VTM = T(rx(1152).bitcast(BF16).rearrange("p (t d) -> p t d", t=9), "vtm")
                GTM = T(rx(1152).rearrange("p (t d) -> p t d", t=9), "gtm")
                ROT = [T(rx(NT), "rot%d" % i) for i in range(2)]
                RTM = T(rx(1152).rearrange("p (a t d) -> p a t d", a=2, t=9), "rtm")
                E1 = T(rx(128), "e1"); E2 = T(rx(128), "e2"); E3 = T(rx(128), "e3")
                QIN = T(rx(64).bitcast(BF16), "qin"); KIN = T(rx(64).bitcast(BF16), "kin")
                KOUT = T(rx(64).bitcast(BF16), "kout"); ATTM = T(rx(64).bitcast(BF16), "attm")
                QRAW = T(rx(64).bitcast(BF16), "qraw"); KM = T(rx(64).bitcast(BF16), "km")
                QINF = T(rx(64), "qinf")
                SQB = T(rx(128).bitcast(BF16).rearrange("p (d n) -> p d n", d=2), "sqb")
                RS = T(rx(128), "rs"); M1 = T(rx(128), "m1"); M2 = T(rx(128), "m2")
                assert o[0] <= 16 * NT, o[0]
                RTf = RT[:]
                SS = [T(RTf[:, i * 256:(i + 1) * 256], "ss%d" % i) for i in range(2)]
                SO = [T(RTf[:, 512 + i * 256: 512 + (i + 1) * 256], "so%d" % i) for i in range(2)]
                OSB = T(RTf[:, 1024:1280].rearrange("p (d n) -> p d n", d=2), "osb")
                TMC = T(RTf[:, 1280:1536].rearrange("p (d n) -> p d n", d=2), "tmc")
                GLR = Fm[7]

                c.dma("sp", ROT[0].ap, I["rot_fm"][(sb * 2) * 128:(sb * 2 + 1) * 128, :], writes=[ROT[0]], key="setup")
                c.dma("sp", ROT[1].ap, I["rot_fm"][(sb * 2 + 1) * 128:(sb * 2 + 2) * 128, :], writes=[ROT[1]], key="setup")
                for a in range(2):
                    c.dma("sp", RTM.ap[:, a], I["rot_tm"][(sb * 2 + a) * 128:(sb * 2 + a + 1) * 128, :].rearrange("p (t d) -> p t d", t=9),
                          writes=[RTM], key="setup")

                W = I["w_in"]
                RQ, RK, RV, RG = 3088, 3600, 4112, 5136
                groups = [([(W, 3072, 16)], 16)]
                for h in dbg.get('gla_heads', range(4)):
                    groups.append(([(W, h * 128, 128), (W, 512 + h * 128, 128), (W, 2048 + h * 256, 256)], 16))
                    groups.append(([(W, 512 + h * 128, 128), (W, 1024 + h * 256, 256)], 16))
                for h in dbg.get('ret_heads', range(4)):
                    groups.append(([(W, RQ + h * 128, 128), (W, RQ + h * 128 + 64, 64), (W, RQ + h * 128, 64),
                                    (W, RK + h * 128, 128), (W, RK + h * 128 + 64, 64), (W, RK + h * 128, 64)], 16))
                    groups.append(([(W, RG + h * 256, 256)], 16))
                    groups.append(([(W, RK + h * 128, 128), (W, RV + h * 256, 256)], 16))
                ws = WStream(groups)
                hrhs = lambda kc, c0, n: (H[kc], H[kc].ap[:, c0:c0 + n])

                dve(lambda q: q.memset(GLR.ap[0:32, :], 1.0), [], [GLR])

                def ev_glr(oc, pt, pap, c0, n):
                    dve(lambda q: q.tensor_copy(out=GLR.ap[0:16, c0:c0 + n], in_=pap), [pt], [GLR])
                fm_group(ws, 16, 1, coltiles, hrhs, ev_glr, mrows=16)
                halt(5)

                def ev_tm(ti, pt, pap, C):
                    dve(lambda q: q.tensor_copy(out=KTM.ap[0:C, ti, :], in_=pt.ap[0:C, 0:128]), [pt], [KTM])
                    dve(lambda q: q.tensor_copy(out=VTM.ap[0:C, ti, :], in_=pt.ap[0:C, 128:384]), [pt], [VTM])

                PA, PB, PATT, PO0, PO1, PU, PST, PST2 = PS[6], PS[7], PS[0], PS[1], PS[2], PS[3], PS[4], PS[5]
                gam = [1.0 - 2.0 ** (-5.0 - h) for h in range(4)]

                def o_and_state(hh, ti, c0, C, smp, st_in, st_out_p, st_out_s, dec_ap_fn, dec_imm):
                    PO = [PO0, PO1]
                    for d in range(2):
                        mm(PO[d], PO[d].ap[:, 0:C], VTM, VTM.ap[0:C, ti, d * 128:(d + 1) * 128], ATTM, ATTM.ap[0:C, 0:C], True, False,
                           signal=False)
                    if not smp:
                        for d in range(2):
                            mm(PO[d], PO[d].ap[:, 0:C], SBF, SBF.ap[:, d * 128:(d + 1) * 128], QIN, QIN.ap[:, 0:C], False, True)
                        mm(PU, PU.ap[:, 0:256], KOUT, KOUT.ap[0:C, :], VTM, VTM.ap[0:C, ti, :], True, True)
                        if dec_imm is None:
                            dve(lambda q: q.scalar_tensor_tensor(out=SST[hh].ap, in0=SST[hh].ap, scalar=dec_ap_fn(C - 1), in1=PU.ap[:, 0:256],
                                                                 op0=ALU.mult, op1=ALU.add), [SST[hh], E1, PU], [SST[hh]])
                        else:
                            dve(lambda q: q.scalar_tensor_tensor(out=SST[hh].ap, in0=SST[hh].ap, scalar=float(dec_imm ** 128), in1=PU.ap[:, 0:256],
                                                                 op0=ALU.mult, op1=ALU.add), [SST[hh], PU], [SST[hh]])
                        act(lambda q: q.activation(out=SBF.ap, in_=SST[hh].ap, func=AF.Copy), [SST[hh]], [SBF])
                    else:
                        for s_ in range(NSEQ):
                            ss = SS[s_ % 2]
                            so = SO[s_ % 2]
                            r0 = (s_ * 4 + (hh % 4)) * 128
                            c.dma("sp", ss.ap, st_in[r0:r0 + 128, :], writes=[ss], key="ss%d" % (s_ % 2))
                            for d in range(2):
                                mm(PO[d], PO[d].ap[:, 4 * s_:4 * s_ + 4], ss, ss.ap[:, d * 128:(d + 1) * 128], QINF, QINF.ap[:, 4 * s_:4 * s_ + 4],
                                   False, s_ == NSEQ - 1)
                            dve(lambda q: q.tensor_scalar(out=KM.ap[0:64, :], in0=KOUT.ap[0:64, :], scalar1=cs("seqm", 64, s_, s_ + 1),
                                                          scalar2=None, op0=ALU.mult), [KOUT, CSTt], [KM])
                            mm(PU, PU.ap[:, 0:256], KM, KM.ap[0:64, :], VTM, VTM.ap[0:64, 8, :], True, True)
                            if dec_imm is None:
                                dve(lambda q: q.scalar_tensor_tensor(out=so.ap, in0=ss.ap, scalar=dec_ap_fn(4 * s_ + 3), in1=PU.ap[:, 0:256],
                                                                     op0=ALU.mult, op1=ALU.add), [ss, E1, PU], [so])
                            else:
                                dve(lambda q: q.scalar_tensor_tensor(out=so.ap, in0=ss.ap, scalar=float(dec_imm ** 4), in1=PU.ap[:, 0:256],
                                                                     op0=ALU.mult, op1=ALU.add), [ss, PU], [so])
                            c.dma("pool", st_out_s[r0:r0 + 128, :], so.ap, reads=[so], key="so%d" % (s_ % 2), is_output=True)

                for h in dbg.get('gla_heads', range(4)):
                    def ev_fmA(oc, pt, pap, c0, n):
                        if oc < 2:
                            dve(lambda q: q.tensor_copy(out=Fm[oc].ap[:, c0:c0 + n], in_=pap), [pt], [Fm[oc]])
                        else:
                            act(lambda q: q.activation(out=Fm[oc].ap[:, c0:c0 + n], in_=pap, func=AF.Silu), [pt], [Fm[oc]])
                    fm_group(ws, 16, 4, coltiles, hrhs, ev_fmA)
                    tm_group(ws, 16, 384, toktiles, ev_tm)
                    for (ti, c0, C) in toktiles:
                        mm(PA, PA.ap[0:C, 0:128], GLR, GLR.ap[0:17, c0:c0 + C], WGK, WGK.ap[0:17, h * 128:(h + 1) * 128], True, True)
                        act(lambda q: q.activation(out=GTM.ap[0:C, ti, :], in_=PA.ap[0:C, 0:128], func=AF.Exp, scale=-1.0), [PA], [GTM])
                        act(lambda q: q.activation(out=GTM.ap[0:C, ti, :], in_=GTM.ap[0:C, ti, :], func=AF.Ln, bias=1.0), [GTM], [GTM])
                    act(lambda q: q.activation(out=SBF.ap, in_=SST[h].ap, func=AF.Copy), [SST[h]], [SBF])
                    for (ti, c0, C) in toktiles:
                        smp = (ti == 8)
                        triU = cs("triUgs", 64) if smp else cs("triUg")
                        triS = cs("triSgs", 64) if smp else cs("triSg")
                        cmk = cs("cms", 64) if smp else cs("cm")
                        g_ap = GTM.ap[0:C, ti, :]
                        mm(PA, PA.ap[:, 0:C], GTM, g_ap, CSTt, triU, True, True)
                        mm(PB, PB.ap[0:C, 0:128], CSTt, triS, GTM, g_ap, True, True)
                        act(lambda q: q.activation(out=E1.ap[:, 0:C], in_=PA.ap[:, 0:C], func=AF.Exp), [PA], [E1])
                        act(lambda q: q.activation(out=E2.ap[:, 0:C], in_=PA.ap[:, 0:C], func=AF.Exp, scale=-1.0), [PA], [E2])
                        act(lambda q: q.activation(out=E3.ap[0:C, :], in_=PB.ap[0:C, 0:128], func=AF.Exp), [PB], [E3])
                        dve(lambda q: q.scalar_tensor_tensor(out=QIN.ap[:, 0:C], in0=Fm[0].ap[:, c0:c0 + C], scalar=128.0 ** -0.5,
                                                             in1=E1.ap[:, 0:C], op0=ALU.mult, op1=ALU.mult), [Fm[0], E1], [QIN])
                        if smp:
                            dve(lambda q: q.scalar_tensor_tensor(out=QINF.ap[:, 0:C], in0=Fm[0].ap[:, c0:c0 + C], scalar=128.0 ** -0.5,
                                                                 in1=E1.ap[:, 0:C], op0=ALU.mult, op1=ALU.mult), [Fm[0], E1], [QINF])
                        dve(lambda q: q.tensor_tensor(out=KIN.ap[:, 0:C], in0=Fm[1].ap[:, c0:c0 + C], in1=E2.ap[:, 0:C], op=ALU.mult),
                            [Fm[1], E2], [KIN])
                        dve(lambda q: q.tensor_tensor(out=KOUT.ap[0:C, :], in0=KTM.ap[0:C, ti, :], in1=E3.ap[0:C, :], op=ALU.mult),
                            [KTM, E3], [KOUT])
                        mm(PATT, PATT.ap[0:C, 0:C], KIN, KIN.ap[:, 0:C], QIN, QIN.ap[:, 0:C], True, True)
                        dve(lambda q: q.tensor_tensor(out=ATTM.ap[0:C, 0:C], in0=PATT.ap[0:C, 0:C], in1=cmk[0:C, 0:C], op=ALU.mult),
                            [PATT, CSTt], [ATTM])
                        o_and_state(h, ti, c0, C, smp, I["sgla"], O["ogla_p"], O["ogla_s"], lambda col: E1.ap[:, col:col + 1], None)
                        PO = [PO0, PO1]
                        for d in range(2):
                            act(lambda q: q.activation(out=SQB.ap[:, d, 0:C], in_=PO[d].ap[:, 0:C], func=AF.Square), [PO[d]], [SQB])
                        for d in range(2):
                            mm(PST, PST.ap[:, 0:C], CBt, onesb, SQB, SQB.ap[:, d, 0:C], d == 0, d == 1)
                        act(lambda q: q.activation(out=RS.ap[:, 0:C], in_=PST.ap[:, 0:C], func=AF.Sqrt, scale=1.0 / 256.0, bias=EPS), [PST], [RS])
                        dve(lambda q: q.reciprocal(out=RS.ap[:, 0:C], in_=RS.ap[:, 0:C]), [RS], [RS])
                        for d in range(2):
                            dve(lambda q: q.tensor_tensor(out=TMC.ap[:, d, 0:C], in0=PO[d].ap[:, 0:C], in1=RS.ap[:, 0:C], op=ALU.mult),
                                [PO[d], RS], [TMC])
                            mt = Bk[2 * h + d]
                            dve(lambda q: q.scalar_tensor_tensor(out=mt.ap[:, c0:c0 + C], in0=TMC.ap[:, d, 0:C], scalar=GN.ap[:, 2 * h + d:2 * h + d + 1],
                                                                 in1=Fm[2 + d].ap[:, c0:c0 + C], op0=ALU.mult, op1=ALU.mult), [TMC, GN, Fm[2 + d]], [mt])
                    if last_sb:
                        c.dma("pool", O["ogla_p"][h * 128:(h + 1) * 128, :], SST[h].ap, reads=[SST[h]], key="stp", is_output=True)

                    halt(7)
                for h in dbg.get('ret_heads', range(4)):
                    def ev_fmB(oc, pt, pap, c0, n):
                        dve(lambda q: q.tensor_copy(out=Fm[oc].ap[:, c0:c0 + n], in_=pap), [pt], [Fm[oc]])
                    fm_group(ws, 16, 4, coltiles, hrhs, ev_fmB)
                    halt(20)
                    for (a, b_, dst) in ((0, 1, 4), (2, 3, 5)):
                        dve(lambda q: q.tensor_tensor(out=Fm[a].ap[:, 0:ncol], in0=Fm[a].ap[:, 0:ncol], in1=ROT[0].ap[:, 0:ncol], op=ALU.mult),
                            [Fm[a], ROT[0]], [Fm[a]])
                        dve(lambda q: q.tensor_tensor(out=Fm[b_].ap[:, 0:ncol], in0=Fm[b_].ap[:, 0:ncol], in1=ROT[1].ap[:, 0:ncol], op=ALU.mult),
                            [Fm[b_], ROT[1]], [Fm[b_]])
                        dve(lambda q: q.tensor_tensor(out=Fm[dst].ap[:, 0:ncol], in0=Fm[a].ap[:, 0:ncol], in1=Fm[b_].ap[:, 0:ncol], op=ALU.add),
                            [Fm[a], Fm[b_]], [Fm[dst]])

                    def ev_fmC(oc, pt, pap, c0, n):
                        act(lambda q: q.activation(out=Fm[6 + oc].ap[:, c0:c0 + n], in_=pap, func=AF.Silu), [pt], [Fm[6 + oc]])
                    fm_group(ws, 16, 2, coltiles, hrhs, ev_fmC)
                    halt(21)
                    tm_group(ws, 16, 384, toktiles, ev_tm)
                    halt(26)
                    k1 = KTM.ap[:, :, 0:64]; k2 = KTM.ap[:, :, 64:128]
                    g1 = GTM.ap[:, :, 0:64]; g2 = GTM.ap[:, :, 64:128]
                    tv = Fm[0].ap[:, 0:576].rearrange("p (t d) -> p t d", t=9)
                    cosT = RTM.ap[:, 0]; sinT = RTM.ap[:, 1]
                    dve(lambda q: q.tensor_tensor(out=g1, in0=k1, in1=cosT, op=ALU.mult), [KTM, RTM], [GTM])
                    dve(lambda q: q.tensor_tensor(out=tv, in0=k2, in1=sinT, op=ALU.mult), [KTM, RTM], [Fm[0]])
                    dve(lambda q: q.tensor_tensor(out=g1, in0=g1, in1=tv, op=ALU.subtract), [GTM, Fm[0]], [GTM])
                    dve(lambda q: q.tensor_tensor(out=g2, in0=k1, in1=sinT, op=ALU.mult), [KTM, RTM], [GTM])
                    dve(lambda q: q.tensor_tensor(out=tv, in0=k2, in1=cosT, op=ALU.mult), [KTM, RTM, GTM], [Fm[0]])
                    dve(lambda q: q.tensor_tensor(out=g2, in0=g2, in1=tv, op=ALU.add), [GTM, Fm[0]], [GTM])
                    hh = 4 + h
                    halt(22)
                    act(lambda q: q.activation(out=SBF.ap, in_=SST[hh].ap, func=AF.Copy), [SST[hh]], [SBF])
                    for (ti, c0, C) in toktiles:
                        smp = (ti == 8)
                        tqa = cs("tqs", 128, h * 64, h * 64 + 64) if smp else cs("tq", 128, h * 128, (h + 1) * 128)
                        Dm = cs("Dsh", 64, h * 64, (h + 1) * 64) if smp else cs("Dh", 128, h * 128, (h + 1) * 128)
                        tkc = cs("tk", C, (4 + h) if smp else h, ((4 + h) if smp else h) + 1)
                        act(lambda q: q.activation(out=QRAW.ap[:, 0:C], in_=Fm[4].ap[:, c0:c0 + C], func=AF.Copy), [Fm[4]], [QRAW])
                        dve(lambda q: q.tensor_tensor(out=QIN.ap[:, 0:C], in0=Fm[4].ap[:, c0:c0 + C], in1=tqa, op=ALU.mult), [Fm[4], CSTt], [QIN])
                        if smp:
                            dve(lambda q: q.tensor_tensor(out=QINF.ap[:, 0:C], in0=Fm[4].ap[:, c0:c0 + C], in1=tqa, op=ALU.mult),
                                [Fm[4], CSTt], [QINF])
                        act(lambda q: q.activation(out=KIN.ap[:, 0:C], in_=Fm[5].ap[:, c0:c0 + C], func=AF.Copy), [Fm[5]], [KIN])
                        dve(lambda q: q.tensor_scalar(out=KOUT.ap[0:C, :], in0=GTM.ap[0:C, ti, :], scalar1=tkc, scalar2=None, op0=ALU.mult),
                            [GTM, CSTt], [KOUT])
                        mm(PATT, PATT.ap[0:C, 0:C], KIN, KIN.ap[:, 0:C], QRAW, QRAW.ap[:, 0:C], True, True)
                        dve(lambda q: q.tensor_tensor(out=ATTM.ap[0:C, 0:C], in0=PATT.ap[0:C, 0:C], in1=Dm[0:C, 0:C], op=ALU.mult),
                            [PATT, CSTt], [ATTM])
                        if ti == 0:
                            halt(23)
                        o_and_state(hh, ti, c0, C, smp, I["sret"], O["oret_p"], O["oret_s"], None, gam[h])
                        if ti == 0:
                            halt(24)
                        if ti == 7:
                            halt(25)
                        PO = [PO0, PO1]
                        for d in range(2):
                            act(lambda q: q.activation(out=OSB.ap[:, d, 0:C], in_=PO[d].ap[:, 0:C], func=AF.Copy), [PO[d]], [OSB])
                        for d in range(2):
                            mm(PST, PST.ap[:, 0:C], CSTt, cs("onesf"), OSB, OSB.ap[:, d, 0:C], d == 0, d == 1)
                        for d in range(2):
                            act(lambda q: q.activation(out=TMC.ap[:, d, 0:C], in_=OSB.ap[:, d, 0:C], func=AF.Square), [OSB], [TMC])
                        for d in range(2):
                            mm(PST2, PST2.ap[:, 0:C], CSTt, cs("onesf"), TMC, TMC.ap[:, d, 0:C], d == 0, d == 1)
                        dve(lambda q: q.tensor_scalar(out=M1.ap[:, 0:C], in0=PST.ap[:, 0:C], scalar1=1.0 / 256.0, scalar2=None, op0=ALU.mult),
                            [PST], [M1])
                        dve(lambda q: q.tensor_tensor(out=M2.ap[:, 0:C], in0=M1.ap[:, 0:C], in1=M1.ap[:, 0:C], op=ALU.mult), [M1], [M2])
                        dve(lambda q: q.scalar_tensor_tensor(out=M2.ap[:, 0:C], in0=PST2.ap[:, 0:C], scalar=1.0 / 256.0, in1=M2.ap[:, 0:C],
                                                             op0=ALU.mult, op1=ALU.subtract), [PST2, M2], [M2])
                        act(lambda q: q.activation(out=RS.ap[:, 0:C], in_=M2.ap[:, 0:C], func=AF.Sqrt, bias=EPS), [M2], [RS])
                        dve(lambda q: q.reciprocal(out=RS.ap[:, 0:C], in_=RS.ap[:, 0:C]), [RS], [RS])
                        for d in range(2):
                            dve(lambda q: q.tensor_tensor(out=OSB.ap[:, d, 0:C], in0=OSB.ap[:, d, 0:C], in1=M1.ap[:, 0:C], op=ALU.subtract),
                                [OSB, M1], [OSB])
                            dve(lambda q: q.tensor_tensor(out=OSB.ap[:, d, 0:C], in0=OSB.ap[:, d, 0:C], in1=RS.ap[:, 0:C], op=ALU.mult),
                                [OSB, RS], [OSB])
                            mt = Bk[8 + 2 * h + d]
                            dve(lambda q: q.scalar_tensor_tensor(out=mt.ap[:, c0:c0 + C], in0=OSB.ap[:, d, 0:C], scalar=RN.ap[:, 2 * h + d:2 * h + d + 1],
                                                                 in1=Fm[6 + d].ap[:, c0:c0 + C], op0=ALU.mult, op1=ALU.mult), [OSB, RN, Fm[6 + d]], [mt])
                    if last_sb:
                        c.dma("pool", O["oret_p"][h * 128:(h + 1) * 128, :], SST[hh].ap, reads=[SST[hh]], key="stp", is_output=True)

                    halt(8)
                halt(9)
                c.barrier()
                groups = [([(I["w_out"], j * 512, 512)], 16) for j in range(4)]
                ws = WStream(groups)

                def ev_out(oc, pt, pap, c0, n):
                    dve(lambda q: q.tensor_copy(out=X[oc].ap[:, c0:c0 + n], in_=pap), [pt], [X[oc]])
                for j in range(4):
                    fm_group(ws, 16, 4, coltiles, lambda kc, c0, n: (Bk[kc], Bk[kc].ap[:, c0:c0 + n]), ev_out, oc_base=j * 4)

            def s5_layer(sb, coltiles, toktiles, has_s):
                ncol = NT if has_s else NPS
                last_sb = (sb == nsb - 1)
                o = [3264]

                def rw(n):
                    a = RWv[:, o[0]:o[0] + n]
                    o[0] += n
                    return a
                TMR = T(rw(512), "tmr"); TMI = T(rw(512), "tmi")
                TTR = T(rw(512), "ttr"); TTI = T(rw(512), "tti")
                PQ = [T(rw(256).bitcast(BF16), "pq%d" % i) for i in range(4)]
                ARE = T(rw(512), "are"); AIM = T(rw(512), "aim")
                ccr_f = rw(256)
                CCR = T(ccr_f.bitcast(BF16).rearrange("p (j m) -> p j m", j=4), "ccr")
                QQ = [T(rw(256).bitcast(BF16), "qq%d" % i) for i in range(4)]
                ccin_f = rw(256)
                CCIN = T(ccin_f.bitcast(BF16).rearrange("p (j m) -> p j m", j=4), "ccin")
                assert o[0] <= 9216, o[0]
                ARG = T(RWv[:, 0:512], "arg")
                KI = T(RWv[:, 512:1024].bitcast(I32), "ki")
                GEL = P_TMP
                RTf = RT[:]
                t_o = [0]

                def rt(n):
                    a = RTf[:, t_o[0]:t_o[0] + n]
                    t_o[0] += n
                    return a
                class _V:
                    pass
                SH = P_TMP
                SHap = P_TMP.ap[:, 0:512]
                bbr_f = rt(256); bbi_f = rt(256)
                BBR = T(bbr_f.bitcast(BF16), "bbr"); BBI = T(bbi_f.bitcast(BF16), "bbi")
                sm = [T(rt(64), "sm%d" % i) for i in range(12)]
                DA = T(rt(17), "da"); DS = T(rt(17), "ds")
                c4 = [T(rt(4), "c4_%d" % i) for i in range(8)]
                XSR = T(rt(64), "xsr"); XSI = T(rt(64), "xsi"); XNR = T(rt(64), "xnr"); XNI = T(rt(64), "xni")
                CT = T(rt(128), "ct")
                TR64 = T(rt(128), "tr64")
                YT = T(rt(128), "yt")
                assert t_o[0] <= 2048, t_o[0]

                def sincos(arg_t, arg_ap, sin_ap, cos_ap, sin_t, cos_t, ki_ap):
                    dve(lambda q: q.tensor_scalar(out=ki_ap, in0=arg_ap, scalar1=1.0 / TWO_PI, scalar2=None, op0=ALU.mult), [arg_t], [KI])
                    dve(lambda q: q.scalar_tensor_tensor(out=arg_ap, in0=ki_ap, scalar=-TWO_PI, in1=arg_ap, op0=ALU.mult, op1=ALU.add),
                        [KI, arg_t], [arg_t])
                    dve(lambda q: q.tensor_scalar(out=arg_ap, in0=arg_ap, scalar1=-math.pi, scalar2=math.pi, op0=ALU.max, op1=ALU.min),
                        [arg_t], [arg_t])
                    act(lambda q: q.activation(out=sin_ap, in_=arg_ap, func=AF.Sin), [arg_t], [sin_t])
                    act(lambda q: q.activation(out=cos_ap, in_=arg_ap, func=AF.Sin, scale=0.5), [arg_t], [cos_t])
                    dve(lambda q: q.tensor_tensor(out=cos_ap, in0=cos_ap, in1=cos_ap, op=ALU.mult), [cos_t], [cos_t])
                    dve(lambda q: q.tensor_scalar(out=cos_ap, in0=cos_ap, scalar1=-2.0, scalar2=1.0, op0=ALU.mult, op1=ALU.add), [cos_t], [cos_t])

                def tt(out_t, out_ap, a_t, a_ap, b_t, b_ap, op):
                    dve(lambda q: q.tensor_tensor(out=out_ap, in0=a_ap, in1=b_ap, op=op), [a_t, b_t], [out_t])

                PBR, PBI, PWR, PWI, PY, PX = PS[0], PS[1], PS[2], PS[3], PS[4], PS[5]
                iota4 = pat(cs("iota1"), [[0, 4], [1, 128]])

                for kc in range(16):
                    LR, LI, LD, BTR, BTI, MAG, SN, CS_, NR, FR, FI, DEN = sm
                    cached = (sb > 0)
                    a3 = lambda t_: t_.ap.rearrange("p (j n) -> p j n", j=4)
                    tabs = ((TTR, TTR.ap, 1024, 512), (TTI, TTI.ap, 1536, 512), (BBR, bbr_f, 2048, 256), (BBI, bbi_f, 2304, 256),
                            (CCR, ccr_f, 2560, 256), (CCIN, ccin_f, 2816, 256))
                    tmtabs = ((TMR, TMR.ap, 0, 512), (TMI, TMI.ap, 512, 512))
                    r0_ = kc * 128
                    if cached:
                        for i_, (t_, ap_, o_, n_) in enumerate(tabs):
                            c.dma("sp", ap_, s5tab[r0_:r0_ + 128, o_:o_ + n_], writes=[t_], key="s5tabr", cont=(i_ > 0))
                        k_ = c.keys["s5tabr"]
                        for (t_, ap_, o_, n_) in tabs:
                            t_.w = ("d", k_, k_.cnt)
                    if not cached:
                        for tl, nm in ((LR, "lamre_gc"), (LI, "lamim_gc"), (LD, "logdt_gc"), (BTR, "bT_re"), (BTI, "bT_im")):
                            c.dma("sp", tl.ap, I[nm][kc * 128:(kc + 1) * 128, :], writes=[tl], key="s5small")
                        act(lambda q: q.activation(out=LD.ap, in_=LD.ap, func=AF.Exp), [LD], [LD])
                        tt(MAG, MAG.ap, LR, LR.ap, LD, LD.ap, ALU.mult)
                        tt(NR, NR.ap, LI, LI.ap, LD, LD.ap, ALU.mult)
                        sincos(NR, NR.ap, SN.ap, CS_.ap, SN, CS_, KI.ap[:, 0:64])
                        act(lambda q: q.activation(out=MAG.ap, in_=MAG.ap, func=AF.Exp), [MAG], [MAG])
                        tt(CS_, CS_.ap, CS_, CS_.ap, MAG, MAG.ap, ALU.mult)
                        tt(SN, SN.ap, SN, SN.ap, MAG, MAG.ap, ALU.mult)
                        dve(lambda q: q.tensor_scalar(out=NR.ap, in0=CS_.ap, scalar1=-1.0, scalar2=None, op0=ALU.add), [CS_], [NR])
                        tt(DEN, DEN.ap, LR, LR.ap, LR, LR.ap, ALU.mult)
                        tt(MAG, MAG.ap, LI, LI.ap, LI, LI.ap, ALU.mult)
                        tt(DEN, DEN.ap, DEN, DEN.ap, MAG, MAG.ap, ALU.add)
                        dve(lambda q: q.reciprocal(out=DEN.ap, in_=DEN.ap), [DEN], [DEN])
                        tt(FR, FR.ap, NR, NR.ap, LR, LR.ap, ALU.mult)
                        tt(MAG, MAG.ap, SN, SN.ap, LI, LI.ap, ALU.mult)
                        tt(FR, FR.ap, FR, FR.ap, MAG, MAG.ap, ALU.add)
                        tt(FR, FR.ap, FR, FR.ap, DEN, DEN.ap, ALU.mult)
                        tt(FI, FI.ap, SN, SN.ap, LR, LR.ap, ALU.mult)
                        tt(MAG, MAG.ap, NR, NR.ap, LI, LI.ap, ALU.mult)
                        tt(FI, FI.ap, FI, FI.ap, MAG, MAG.ap, ALU.subtract)
                        tt(FI, FI.ap, FI, FI.ap, DEN, DEN.ap, ALU.mult)
                        tt(MAG, MAG.ap, FR, FR.ap, BTR, BTR.ap, ALU.mult)
                        tt(CS_, CS_.ap, FI, FI.ap, BTI, BTI.ap, ALU.mult)
                        tt(MAG, MAG.ap, MAG, MAG.ap, CS_, CS_.ap, ALU.subtract)
                        tt(CS_, CS_.ap, FR, FR.ap, BTI, BTI.ap, ALU.mult)
                        tt(SN, SN.ap, FI, FI.ap, BTR, BTR.ap, ALU.mult)
                        tt(CS_, CS_.ap, CS_, CS_.ap, SN, SN.ap, ALU.add)
                        mC = pat(cs("maskC"), [[1, 8], [0, 64]])
                        dve(lambda q: q.tensor_tensor(out=BBR.ap.rearrange("p (g n) -> p g n", g=8), in0=pat(MAG.ap, [[0, 8], [1, 64]]), in1=mC, op=ALU.mult),
                            [MAG, CSTt], [BBR])
                        dve(lambda q: q.tensor_tensor(out=BBI.ap.rearrange("p (g n) -> p g n", g=8), in0=pat(CS_.ap, [[0, 8], [1, 64]]), in1=mC, op=ALU.mult),
                            [CS_, CSTt], [BBI])
                        ct4 = CT.ap.rearrange("p (a j c) -> p a j c", a=2, j=4)
                        c.dma("sp", ct4[:, 0], I["cT_re"][kc * 512:(kc + 1) * 512, :].rearrange("(j p) c -> p j c", p=128), writes=[CT], key="s5small")
                        c.dma("sp", ct4[:, 1], I["cT_im"][kc * 512:(kc + 1) * 512, :].rearrange("(j p) c -> p j c", p=128), writes=[CT], key="s5small")
                        mB = pat(cs("maskB"), [[8, 4], [1, 8], [0, 16]])
                        ctre = pat(ct4[:, 0], [[16, 4], [0, 8], [1, 16]])
                        ctim = pat(ct4[:, 1], [[16, 4], [0, 8], [1, 16]])
                        cc4 = lambda t_: t_.ap.rearrange("p j (g c) -> p j g c", g=8)
                        dve(lambda q: q.tensor_tensor(out=cc4(CCR), in0=ctre, in1=mB, op=ALU.mult), [CT, CSTt], [CCR])
                        dve(lambda q: q.tensor_tensor(out=cc4(CCIN), in0=ctim, in1=mB, op=ALU.mult), [CT, CSTt], [CCIN])
                        dve(lambda q: q.tensor_scalar(out=CCIN.ap, in0=CCIN.ap, scalar1=-1.0, scalar2=None, op0=ALU.mult), [CCIN], [CCIN])
                        LRc, LIc, LDc, ACc, THc = c4[0:5]
                        for tl, nm in ((LRc, "lamre_c"), (LIc, "lamim_c"), (LDc, "logdt_c")):
                            c.dma("sp", tl.ap, I[nm][:, kc * 4:(kc + 1) * 4], writes=[tl], key="s5small")
                        act(lambda q: q.activation(out=LDc.ap, in_=LDc.ap, func=AF.Exp), [LDc], [LDc])
                        tt(ACc, ACc.ap, LRc, LRc.ap, LDc, LDc.ap, ALU.mult)
                        tt(THc, THc.ap, LIc, LIc.ap, LDc, LDc.ap, ALU.mult)
                        a3 = lambda t_: t_.ap.rearrange("p (j n) -> p j n", j=4)
                        dve(lambda q: q.tensor_tensor(out=a3(ARG), in0=pat(THc.ap, [[1, 4], [0, 128]]), in1=iota4, op=ALU.mult), [THc, CSTt], [ARG])
                        sincos(ARG, ARG.ap, TTI.ap, TTR.ap, TTI, TTR, KI.ap)
                        dve(lambda q: q.tensor_tensor(out=a3(ARG), in0=pat(ACc.ap, [[1, 4], [0, 128]]), in1=iota4, op=ALU.mult), [ACc, CSTt], [ARG])
                        act(lambda q: q.activation(out=SHap, in_=ARG.ap, func=AF.Exp), [ARG], [SH])
                        tt(TTR, TTR.ap, TTR, TTR.ap, SH, SHap, ALU.mult)
                        tt(TTI, TTI.ap, TTI, TTI.ap, SH, SHap, ALU.mult)
                        if nsb > 1:
                            for (t_, ap_, o_, n_) in tabs:
                                c.dma("sp", s5tab[r0_:r0_ + 128, o_:o_ + n_], ap_, reads=[t_], key="s5tabw")
                    def rowb(nm):
                        a = I[nm][0:1, kc * 512:(kc + 1) * 512]
                        return AP(a.tensor, a.offset, [[0, 128], [1, 512]])
                    def load_rows():
                        c.dma("sp", ARG.ap, rowb("logdt_r"), writes=[ARG], key="s5row")
                        c.dma("sp", TMR.ap, rowb("lamre_r"), writes=[TMR], key="s5row", cont=True)
                        c.dma("sp", TMI.ap, rowb("lamim_r"), writes=[TMI], key="s5row", cont=True)
                        k_ = c.keys["s5row"]
                        for t_ in (ARG, TMR, TMI):
                            t_.w = ("d", k_, k_.cnt)
                        act(lambda q: q.activation(out=ARG.ap, in_=ARG.ap, func=AF.Exp), [ARG], [ARG])
                        tt(TMR, TMR.ap, TMR, TMR.ap, ARG, ARG.ap, ALU.mult)
                        tt(TMI, TMI.ap, TMI, TMI.ap, ARG, ARG.ap, ALU.mult)

                    def build_tm(dr, di, col):
                        sc_ap = cs("scol", 128, col, col + 1)
                        dve(lambda q: q.tensor_scalar(out=ARG.ap, in0=TMI.ap, scalar1=sc_ap, scalar2=None, op0=ALU.mult), [TMI, CSTt], [ARG])
                        act(lambda q: q.activation(out=SHap, in_=TMR.ap, func=AF.Exp, scale=sc_ap), [TMR, CSTt], [SH])
                        sincos(ARG, ARG.ap, di.ap, dr.ap, di, dr, KI.ap)
                        tt(dr, dr.ap, dr, dr.ap, SH, SHap, ALU.mult)
                        tt(di, di.ap, di, di.ap, SH, SHap, ALU.mult)
                    if has_s:
                        for (src, dstt) in ((I["s5re"], XSR), (I["s5im"], XSI)):
                            c.dma("sp", TR64.ap[0:64, :], src[kc * 64:(kc + 1) * 64, :], writes=[TR64], key="s5small")
                            c.op("pe", lambda q: q.transpose(PX.ap[:, 0:64], TR64.ap[0:64, :], cs("ident", 64, 0, 64)), [TR64, CSTt], [PX])
                            dve(lambda q: q.tensor_copy(out=dstt.ap, in_=PX.ap[:, 0:64]), [PX], [dstt])
                    dve(lambda q: q.tensor_scalar(out=DA.ap, in0=mod3[:, 1, kc, :], scalar1=S5D.ap[:, kc:kc + 1], scalar2=None, op0=ALU.mult),
                        [MOD, S5D], [DA])
                    dve(lambda q: q.tensor_scalar(out=DS.ap, in0=mod3[:, 0, kc, :], scalar1=S5D.ap[:, kc:kc + 1], scalar2=None, op0=ALU.mult),
                        [MOD, S5D], [DS])

                    tile_order = [t_ for t_ in toktiles if t_[0] == 8] + [t_ for t_ in toktiles if t_[0] != 8]
                    for t_idx, (ti, c0, C) in enumerate(tile_order):
                        smp = (ti == 8)
                        if cached:
                            if t_idx == 0:
                                for i_, (t_, ap_, o_, n_) in enumerate(tmtabs):
                                    c.dma("sp", ap_, s5tab[r0_:r0_ + 128, o_:o_ + n_], writes=[t_], key="s5tabr", cont=(i_ > 0))
                                k_ = c.keys["s5tabr"]
                                for (t_, ap_, o_, n_) in tmtabs:
                                    t_.w = ("d", k_, k_.cnt)
                        elif t_idx == 0 or (has_s and t_idx == 1):
                            load_rows()
                            build_tm(TMR, TMI, 1 if smp else 0)
                            if (not smp) and nsb > 1:
                                for (t_, ap_, o_, n_) in tmtabs:
                                    c.dma("sp", s5tab[r0_:r0_ + 128, o_:o_ + n_], ap_, reads=[t_], key="s5tabw")
                        mm(PBR, PBR.ap[0:C, :], H[kc], H[kc].ap[:, c0:c0 + C], BBR, BBR.ap, True, True)
                        mm(PBI, PBI.ap[0:C, :], H[kc], H[kc].ap[:, c0:c0 + C], BBI, BBI.ap, True, True)
                        Tr, Ti = (TMR, TMI)
                        for (pq, pb, tb) in ((0, PBR, Tr), (1, PBI, Ti), (2, PBI, Tr), (3, PBR, Ti)):
                            dve(lambda q: q.tensor_tensor(out=PQ[pq].ap[0:C, :], in0=pb.ap[0:C, :], in1=tb.ap[0:C, :], op=ALU.mult), [pb, tb], [PQ[pq]])
                        if smp:
                            triP = CBt.ap[0:64, 384:448]; triN = CBt.ap[0:64, 448:512]
                        else:
                            triP = CBt.ap[:, 128:256]; triN = CBt.ap[:, 256:384]
                        for j in range(4):
                            mm(PWR, PWR.ap[:, j * 128:j * 128 + C], PQ[0], PQ[0].ap[0:C, j * 128:(j + 1) * 128], CBt, triP, True, False, signal=False)
                            mm(PWR, PWR.ap[:, j * 128:j * 128 + C], PQ[1], PQ[1].ap[0:C, j * 128:(j + 1) * 128], CBt, triN, False, True, signal=(j == 3))
                        for j in range(4):
                            mm(PWI, PWI.ap[:, j * 128:j * 128 + C], PQ[2], PQ[2].ap[0:C, j * 128:(j + 1) * 128], CBt, triP, True, False, signal=False)
                            mm(PWI, PWI.ap[:, j * 128:j * 128 + C], PQ[3], PQ[3].ap[0:C, j * 128:(j + 1) * 128], CBt, triP, False, True, signal=(j == 3))
                        are3 = a3(ARE); aim3 = a3(AIM)
                        if not smp:
                            for j in range(4):
                                act(lambda q: q.activation(out=are3[:, j, 0:C], in_=PWR.ap[:, j * 128:j * 128 + C], func=AF.Identity,
                                                           bias=XPR.ap[:, kc * 4 + j:kc * 4 + j + 1]), [PWR, XPR], [ARE])
                                act(lambda q: q.activation(out=aim3[:, j, 0:C], in_=PWI.ap[:, j * 128:j * 128 + C], func=AF.Identity,
                                                           bias=XPI.ap[:, kc * 4 + j:kc * 4 + j + 1]), [PWI, XPI], [AIM])
                            trv, tiv = a3(TTR)[:, :, 0:C], a3(TTI)[:, :, 0:C]
                            arv, aiv = are3[:, :, 0:C], aim3[:, :, 0:C]
                            qv = lambda i_: a3(QQ[i_])[:, :, 0:C]
                        else:
                            w4 = lambda p_: pat(p_.ap, [[128, 4], [4, 16], [1, 4]])
                            a4 = lambda t_: pat(t_.ap, [[128, 4], [4, 16], [1, 4]])
                            x4 = lambda t_: pat(t_.ap, [[16, 4], [1, 16], [0, 4]])
                            dve(lambda q: q.tensor_tensor(out=a4(ARE), in0=w4(PWR), in1=x4(XSR), op=ALU.add), [PWR, XSR], [ARE])
                            dve(lambda q: q.tensor_tensor(out=a4(AIM), in0=w4(PWI), in1=x4(XSI), op=ALU.add), [PWI, XSI], [AIM])
                            trv = pat(TTR.ap, [[128, 4], [0, 16], [1, 4]]); tiv = pat(TTI.ap, [[128, 4], [0, 16], [1, 4]])
                            arv, aiv = a4(ARE), a4(AIM)
                            qv = lambda i_: pat(QQ[i_].ap, [[128, 4], [4, 16], [1, 4]])
                        dve(lambda q: q.tensor_tensor(out=qv(0), in0=trv, in1=arv, op=ALU.mult), [TTR, ARE], [QQ[0]])
                        if not smp:
                            dve(lambda q: q.scalar_tensor_tensor(out=qv(1), in0=tiv, scalar=-1.0, in1=aiv, op0=ALU.mult, op1=ALU.mult), [TTI, AIM], [QQ[1]])
                        else:
                            dve(lambda q: q.tensor_tensor(out=qv(1), in0=tiv, in1=aiv, op=ALU.mult), [TTI, AIM], [QQ[1]])
                            dve(lambda q: q.tensor_scalar(out=QQ[1].ap, in0=QQ[1].ap, scalar1=-1.0, scalar2=None, op0=ALU.mult), [QQ[1]], [QQ[1]])
                        pool(lambda q: q.tensor_tensor(out=qv(2), in0=trv, in1=aiv, op=ALU.mult), [TTR, AIM], [QQ[2]])
                        pool(lambda q: q.tensor_tensor(out=qv(3), in0=tiv, in1=arv, op=ALU.mult), [TTI, ARE], [QQ[3]])
                        if not smp:
                            lr_ = lambda t_: a3(t_)[:, :, C - 1]
                            m1, m2 = c4[5], c4[6]
                            tt(m1, m1.ap, TTR, lr_(TTR), ARE, lr_(ARE), ALU.mult)
                            tt(m2, m2.ap, TTI, lr_(TTI), AIM, lr_(AIM), ALU.mult)
                            xr_dst = XPR.ap[:, kc * 4:(kc + 1) * 4]; xi_dst = XPI.ap[:, kc * 4:(kc + 1) * 4]
                            dve(lambda q: q.tensor_tensor(out=xr_dst, in0=m1.ap, in1=m2.ap, op=ALU.subtract), [m1, m2, ARE, AIM], [XPR])
                            tt(m1, m1.ap, TTR, lr_(TTR), AIM, lr_(AIM), ALU.mult)
                            tt(m2, m2.ap, TTI, lr_(TTI), ARE, lr_(ARE), ALU.mult)
                            dve(lambda q: q.tensor_tensor(out=xi_dst, in0=m1.ap, in1=m2.ap, op=ALU.add), [m1, m2, ARE, AIM], [XPI])
                        else:
                            l4 = lambda t_: pat(t_.ap[:, 3:4], [[128, 4], [4, 16]])
                            tl = lambda t_: pat(t_.ap[:, 3:4], [[128, 4], [0, 16]])
                            n3 = lambda t_: t_.ap.rearrange("p (j s) -> p j s", j=4)
                            m1, m2 = sm[10], sm[11]
                            tt(m1, n3(m1), TTR, tl(TTR), ARE, l4(ARE), ALU.mult)
                            tt(m2, n3(m2), TTI, tl(TTI), AIM, l4(AIM), ALU.mult)
                            tt(XNR, XNR.ap, m1, m1.ap, m2, m2.ap, ALU.subtract)
                            tt(m1, n3(m1), TTR, tl(TTR), AIM, l4(AIM), ALU.mult)
                            tt(m2, n3(m2), TTI, tl(TTI), ARE, l4(ARE), ALU.mult)
                            tt(XNI, XNI.ap, m1, m1.ap, m2, m2.ap, ALU.add)
                            for (xn, dsto) in ((XNR, O["os5re_s"]), (XNI, O["os5im_s"])):
                                c.op("pe", lambda q: q.transpose(PX.ap[0:64, 0:128], xn.ap, cs("ident")), [xn, CSTt], [PX])
                                dve(lambda q: q.tensor_copy(out=TR64.ap[0:64, :], in_=PX.ap[0:64, 0:128]), [PX], [TR64])
                                c.dma("pool", dsto[kc * 64:(kc + 1) * 64, :], TR64.ap[0:64, :], reads=[TR64], key="s5out", is_output=True)
                        for j in range(4):
                            for (qi, cc) in ((0, CCR), (1, CCR), (2, CCIN), (3, CCIN)):
                                if not smp:
                                    rap = a3(QQ[qi])[:, j, 0:C]
                                else:
                                    rap = QQ[qi].ap[:, j * 128:j * 128 + 64]
                                mm(PY, PY.ap[:, 0:C], cc, cc.ap[:, j, :], QQ[qi], rap, (j == 0 and qi == 0), (j == 3 and qi == 3),
                                   signal=(j == 3 and qi == 3))
                        tt(YT, YT.ap[:, 0:C], X[kc], X[kc].ap[:, c0:c0 + C], P_RSTD, P_RSTD.ap[:, c0:c0 + C], ALU.mult)
                        if not smp:
                            dve(lambda q: q.scalar_tensor_tensor(out=YT.ap[:, 0:C], in0=YT.ap[:, 0:C], scalar=DA.ap[:, 0:1], in1=PY.ap[:, 0:C],
                                                                 op0=ALU.mult, op1=ALU.add), [YT, DA, PY], [YT])
                            dve(lambda q: q.tensor_scalar(out=X[kc].ap[:, c0:c0 + C], in0=YT.ap[:, 0:C], scalar1=DS.ap[:, 0:1], scalar2=None, op0=ALU.add),
                                [YT, DS], [X[kc]])
                        else:
                            y3 = YT.ap[:, 0:64].rearrange("p (s t) -> p s t", t=4)
                            dve(lambda q: q.tensor_tensor(out=y3, in0=y3, in1=s17(DA.ap[:, 1:17]), op=ALU.mult), [YT, DA], [YT])
                            dve(lambda q: q.tensor_tensor(out=YT.ap[:, 0:64], in0=YT.ap[:, 0:64], in1=PY.ap[:, 0:64], op=ALU.add), [YT, PY], [YT])
                            dve(lambda q: q.tensor_tensor(out=X[kc].ap[:, c0:c0 + C].rearrange("p (s t) -> p s t", t=4), in0=y3, in1=s17(DS.ap[:, 1:17]), op=ALU.add),
                                [YT, DS], [X[kc]])
                    xa = X[kc].ap[:, 0:ncol]; ga = GEL.ap[:, 0:ncol]
                    tt(GEL, ga, X[kc], xa, X[kc], xa, ALU.mult)
                    dve(lambda q: q.tensor_scalar(out=ga, in0=ga, scalar1=0.044715, scalar2=1.0, op0=ALU.mult, op1=ALU.add), [GEL], [GEL])
                    tt(GEL, ga, GEL, ga, X[kc], xa, ALU.mult)
                    act(lambda q: q.activation(out=ga, in_=ga, func=AF.Sigmoid, scale=1.5957691216057308), [GEL], [GEL])
                    tt(Bk[kc], Bk[kc].ap[:, 0:ncol], GEL, ga, X[kc], xa, ALU.mult)
                if last_sb:
                    for (xn, dsto) in ((XPR, O["os5re_p"]), (XPI, O["os5im_p"])):
                        c.op("pe", lambda q: q.transpose(PX.ap[0:64, 0:128], xn.ap, cs("ident")), [xn, CSTt], [PX])
                        dve(lambda q: q.tensor_copy(out=TR64.ap[0:64, :], in_=PX.ap[0:64, 0:128]), [PX], [TR64])
                        c.dma("pool", dsto, TR64.ap[0:64, :], reads=[TR64], key="s5out", is_output=True)
                c.barrier()
                groups = []
                for j in range(4):
                    groups.append(([(I["w_glu_a"], j * 512, 512)], 16))
                    groups.append(([(I["w_glu_b"], j * 512, 512)], 16))
                ws = WStream(groups)
                zr = lambda kc, c0, n: (Bk[kc], Bk[kc].ap[:, c0:c0 + n])

                def ev_a(oc, pt, pap, c0, n):
                    dve(lambda q: q.tensor_copy(out=X[oc].ap[:, c0:c0 + n], in_=pap), [pt], [X[oc]])

                def ev_b(oc, pt, pap, c0, n):
                    tm_ = RELU[relu_i[0] % 2]
                    relu_i[0] += 1
                    act(lambda q: q.activation(out=tm_.ap[:, 0:n], in_=pap, func=AF.Sigmoid), [pt], [tm_])
                    dve(lambda q: q.tensor_tensor(out=X[oc].ap[:, c0:c0 + n], in0=X[oc].ap[:, c0:c0 + n], in1=tm_.ap[:, 0:n], op=ALU.mult),
                        [tm_, X[oc]], [X[oc]])
                for j in range(4):
                    fm_group(ws, 16, 4, coltiles, zr, ev_a, oc_base=j * 4)
                    fm_group(ws, 16, 4, coltiles, zr, ev_b, oc_base=j * 4)

            halt(2)
            TOKT_ALL = [(t, t * 128, 128) for t in range(8)]

            for sb in range(nsb):
                has_s = (sb == 0)
                coltiles = [(0, 512), (512, 512)] + ([(1024, 64)] if has_s else [])
                toktiles = TOKT_ALL + ([(8, NPS, NS)] if has_s else [])
                ncol = NT if has_s else NPS
                start_at = dbg.get("inject", 0)

                c.barrier()
                if start_at == 0:
                    for (ti, c0, C) in toktiles:
                        src = I["xp"][sb * NPS + ti * 128: sb * NPS + ti * 128 + 128, :] if ti < 8 else I["xs"]
                        tt = TMPA if ti % 2 == 0 else TMPB
                        c.dma("sp", tt.ap[0:C, :], src, writes=[tt], key="xin%d" % (ti % 2))
                        for k4 in range(4):
                            pt = PS[k4 % 2]
                            for jj in range(4):
                                kc = k4 * 4 + jj
                                c.op("pe", lambda q: q.transpose(pt.ap[:, jj * 128: jj * 128 + C], tt.ap[0:C, kc * 128:(kc + 1) * 128],
                                                                 cs("ident", C, 0, C)), [tt, CSTt], [pt], signal=(jj == 3))
                            dve(lambda q: q.tensor_copy(out=RX3[:, k4 * 4:(k4 + 1) * 4, c0:c0 + C],
                                                        in_=pt.ap.rearrange("p (j n) -> p j n", j=4)[:, :, 0:C]),
                                [pt], [X[k4 * 4 + jj] for jj in range(4)])
                else:
                    for kc in range(16):
                        r0 = (sb * 16 + kc) * 128
                        c.dma("sp", X[kc].ap[:, 0:ncol], I["xinj"][r0:r0 + 128, 0:ncol], writes=[X[kc]], key="setup")
                for kc in range(16):
                    c.dma("pool", xscr[kc * 128:(kc + 1) * 128, 0:ncol], X[kc].ap[:, 0:ncol], reads=[X[kc]], key="xout%d" % (kc % 4))

                halt(3)
                if start_at <= 0:
                    prologue(0, coltiles, has_s)
                    c.barrier()
                    halt(4)
                    mixer(sb, coltiles, toktiles, has_s)
                    halt(10)
                    epilogue(0, sb, coltiles, has_s, False, dump_idx=0)
                    halt(11)
                if start_at <= 1 and dbg.get("stop_after", 9) >= 1:
                    prologue(1, coltiles, has_s)
                    c.barrier()
                    mlp(0, coltiles)
                    epilogue(1, sb, coltiles, has_s, False, dump_idx=1)
                if start_at <= 2 and dbg.get("stop_after", 9) >= 2:
                    prologue(2, coltiles, has_s)
                    c.barrier()
                    s5_layer(sb, coltiles, toktiles, has_s)
                    epilogue(2, sb, coltiles, has_s, False, dump_idx=2)
                if start_at <= 3 and dbg.get("stop_after", 9) >= 3:
                    prologue(3, coltiles, has_s)
                    c.barrier()
                    mlp(1, coltiles)
                    epilogue(3, sb, coltiles, has_s, True, dump_idx=3)
                c.barrier()
                for (ti, c0, C) in toktiles:
                    tt = TMPA if ti % 2 == 0 else TMPB
                    for k4 in range(4):
                        pt = PS[k4 % 2]
                        for jj in range(4):
                            kc = k4 * 4 + jj
                            c.op("pe", lambda q: q.transpose(pt.ap[0:C, jj * 128:(jj + 1) * 128], X[kc].ap[:, c0:c0 + C], cs("ident")),
                                 [X[kc], CSTt], [pt], signal=(jj == 3))
                        dve(lambda q: q.tensor_copy(out=tt.ap[0:C, k4 * 512:(k4 + 1) * 512], in_=pt.ap[0:C, :]), [pt], [tt])
                    dst = O["yp"][sb * NPS + ti * 128: sb * NPS + ti * 128 + 128, :] if ti < 8 else O["ys"]
                    c.dma("pool", dst, tt.ap[0:C, :], reads=[tt], key="yout%d" % (ti % 2), is_output=True)
        except _Halt:
            pass
        c.finish()
    return nc


_NC_CACHE = {}


def _prep_shared(inp, nsb):
    f = np.float32
    cst, off, cb, rot_fm, rot_tm = _get_consts(nsb)
    sh = {}
    sh["w_ada"] = np.ascontiguousarray(inp["w_ada"], f).reshape(4 * D, 3 * D)
    sh["b_ada"] = np.ascontiguousarray(inp["b_ada"], f).reshape(4 * 48, 128)
    sh["npre"] = np.ascontiguousarray(inp["norm_pre"], f).reshape(64, 128)
    sh["npost"] = np.ascontiguousarray(inp["norm_post"], f).reshape(64, 128)
    sh["w_in"] = np.ascontiguousarray(inp["w_in_mix"][0], f)
    sh["wgk17"] = np.concatenate([inp["w_gla_gk"][0], inp["b_gla_gk"][0][None, :]], axis=0).astype(f)
    sh["gnorm"] = np.ascontiguousarray(inp["gla_head_norm"][0], f).reshape(8, 128)
    sh["rnorm"] = np.ascontiguousarray(inp["ret_head_norm"][0], f).reshape(8, 128)
    sh["w_out"] = np.ascontiguousarray(inp["w_out_mix"][0], f)
    lamre = inp["s5_lam_re"][0].astype(f); lamim = inp["s5_lam_im"][0].astype(f)
    logdt = np.broadcast_to(inp["s5_log_dt"][0].astype(f)[:, None], (128, 64))
    for nm, a in (("lamre", lamre), ("lamim", lamim), ("logdt", logdt)):
        a = np.ascontiguousarray(a)
        sh[nm + "_c"] = np.ascontiguousarray(a.reshape(64, 128).T)
        sh[nm + "_gc"] = np.ascontiguousarray(np.repeat(a, 16, axis=0))
        sh[nm + "_r"] = np.ascontiguousarray(a.reshape(1, 8192))
    sh["bT_re"] = np.ascontiguousarray(inp["s5_b_re"][0].astype(f).transpose(0, 2, 1)).reshape(D, 64)
    sh["bT_im"] = np.ascontiguousarray(inp["s5_b_im"][0].astype(f).transpose(0, 2, 1)).reshape(D, 64)
    sh["cT_re"] = np.ascontiguousarray(inp["s5_c_re"][0].astype(f).transpose(0, 2, 1)).reshape(8192, 16)
    sh["cT_im"] = np.ascontiguousarray(inp["s5_c_im"][0].astype(f).transpose(0, 2, 1)).reshape(8192, 16)
    sh["s5d"] = np.ascontiguousarray(inp["s5_d"][0], f).reshape(16, 128)
    sh["w_glu_a"] = np.ascontiguousarray(inp["w_glu_a"][0], f)
    sh["w_glu_b"] = np.ascontiguousarray(inp["w_glu_b"][0], f)
    sh["w_up"] = np.ascontiguousarray(inp["w_mlp_up"], f).reshape(2 * D, 4 * D)
    sh["w_down"] = np.ascontiguousarray(inp["w_mlp_down"], f).reshape(2 * 4 * D, D)
    sh["cst"] = cst
    sh["cstb"] = cb
    sh["rot_fm"] = rot_fm.reshape(nsb * 2 * 128, NT)
    sh["rot_tm"] = rot_tm.reshape(nsb * 2 * 128, 9 * 64)
    return sh


def _prep_core(inp, core, nsb):
    f = np.float32
    b = core // 2
    m = {}
    if nsb == 2:
        m["xp"] = np.ascontiguousarray(inp["x_prompt"][b], f)
    else:
        m["xp"] = np.ascontiguousarray(inp["x_prompt"][b, :NPS], f)
    s0 = core * NSEQ
    m["xs"] = np.ascontiguousarray(inp["x_sample"][s0:s0 + NSEQ], f).reshape(NS, D)
    m["cv"] = np.concatenate([inp["c_prompt"][b][None, :], inp["c_sample"][s0:s0 + NSEQ]], axis=0).astype(f)
    m["sgla"] = np.ascontiguousarray(inp["state_gla"][0, s0:s0 + NSEQ], f).reshape(NSEQ * 4 * 128, 256)
    m["sret"] = np.ascontiguousarray(inp["state_ret"][0, s0:s0 + NSEQ], f).reshape(NSEQ * 4 * 128, 256)
    for nm, key in (("s5re", "state_s5_re"), ("s5im", "state_s5_im")):
        a = inp[key][0, s0:s0 + NSEQ].astype(f).reshape(NSEQ, 64, 128)
        m[nm] = np.ascontiguousarray(a.transpose(1, 0, 2)).reshape(64 * 16, 128)
    return m


def kernel(**inp):
    nsb = 2
    if nsb not in _NC_CACHE:
        _NC_CACHE[nsb] = build(nsb)
    nc = _NC_CACHE[nsb]
    sh = _prep_shared(inp, nsb)
    in_maps = []
    for core in range(NCORES):
        m = dict(sh)
        m.update(_prep_core(inp, core, nsb))
        in_maps.append(m)
    res = run_bass_kernel_spmd(nc, in_maps, core_ids=list(range(NCORES)))
    R = res.results
    f = np.float32
    y_prompt = np.stack([R[2 * b]["yp"] for b in range(4)], axis=0).astype(f)
    y_sample = np.concatenate([R[c_]["ys"].reshape(NSEQ, 4, D) for c_ in range(NCORES)], axis=0).astype(f)
    gla_p = np.stack([R[2 * b]["ogla_p"].reshape(4, 128, 256) for b in range(4)], axis=0)[None].astype(f)
    ret_p = np.stack([R[2 * b]["oret_p"].reshape(4, 128, 256) for b in range(4)], axis=0)[None].astype(f)
    gla_s = np.concatenate([R[c_]["ogla_s"].reshape(NSEQ, 4, 128, 256) for c_ in range(NCORES)], axis=0)[None].astype(f)
    ret_s = np.concatenate([R[c_]["oret_s"].reshape(NSEQ, 4, 128, 256) for c_ in range(NCORES)], axis=0)[None].astype(f)

    def s5p(name):
        return np.stack([R[2 * b][name].reshape(128, 64) for b in range(4)], axis=0)[None].astype(f)

    def s5s(name):
        outs = []
        for c_ in range(NCORES):
            a = R[c_][name].reshape(64, NSEQ, 128).transpose(1, 0, 2).reshape(NSEQ, 128, 64)
            outs.append(a)
        return np.concatenate(outs, axis=0)[None].astype(f)
    return (y_prompt, y_sample, gla_p, gla_s, ret_p, ret_s, s5p("os5re_p"), s5s("os5re_s"), s5p("os5im_p"), s5s("os5im_s"))
```
